# Optimizing a Trainium2 kernel written in Bass

```python
import math
import jax, jax.numpy as jnp
from jax import lax
import numpy as np

D_MODEL = 1024
BATCH = 8
SEQ = 4096
DEPTH = 4

MIX_HALF = D_MODEL // 2
HEAD_DIM = 64
ROT_DIMS = HEAD_DIM // 4
ROPE_THETA = 500000.0
GLA_DK = 64
GLA_DV = 128
N_GLA_HEADS = MIX_HALF // GLA_DV
GLA_GATE_RANK = 16
GLA_GATE_NORMALIZER = 16.0
GLA_CHUNK = 64
N_SWA_HEADS = MIX_HALF // HEAD_DIM
N_SWA_KV_HEADS = 2
WINDOW = 128
N_DIFF_HEADS = MIX_HALF // (2 * HEAD_DIM)
Q_BLOCK = 128
HGRN_EXPAND = 128
N_HGRN_HEADS = MIX_HALF // HGRN_EXPAND
HGRN_DV = MIX_HALF // N_HGRN_HEADS
HGRN_CHUNK = 64
D_FF = 2816
CONV_WIDTH = 3
N_EVEN = (DEPTH + 1) // 2
N_ODD = DEPTH // 2
EVEN_SPLITS = (N_GLA_HEADS * GLA_DK, N_GLA_HEADS * GLA_DK, N_GLA_HEADS * GLA_DV, N_GLA_HEADS * GLA_DV,
               GLA_GATE_RANK, N_SWA_HEADS * HEAD_DIM, N_SWA_KV_HEADS * HEAD_DIM, N_SWA_KV_HEADS * HEAD_DIM)
ODD_SPLITS = (N_DIFF_HEADS * 2 * HEAD_DIM, N_DIFF_HEADS * 2 * HEAD_DIM, N_DIFF_HEADS * 2 * HEAD_DIM,
              N_HGRN_HEADS * HGRN_EXPAND, N_HGRN_HEADS * HGRN_EXPAND, N_HGRN_HEADS * HGRN_DV, N_HGRN_HEADS * HGRN_DV)

kernel_name = 'hybrid_gla_swa_diff_hgrn2_convffn_adaln'

F32 = jnp.float32


def rms_norm(x, w, eps=1e-6):
    x32 = x.astype(F32)
    y = x32 * lax.rsqrt(jnp.mean(x32 * x32, axis=-1, keepdims=True) + eps)
    return (y * w.astype(F32)).astype(x.dtype)


def split_cols(t, sizes):
    idx = np.cumsum(np.array(sizes))[:-1].tolist()
    return jnp.split(t, idx, axis=-1)


def rope_tables(positions):
    inv_freq = ROPE_THETA ** (-jnp.arange(0, ROT_DIMS, 2, dtype=F32) / ROT_DIMS)
    ang = positions.astype(F32)[..., None] * inv_freq
    return jnp.cos(ang), jnp.sin(ang)


def apply_partial_rope(x, cos, sin):
    half = ROT_DIMS // 2
    shape = cos.shape[:2] + (1,) * (x.ndim - 3) + (half,)
    cs = cos.reshape(shape).astype(x.dtype)
    sn = sin.reshape(shape).astype(x.dtype)
    x1 = x[..., :half]
    x2 = x[..., half:ROT_DIMS]
    return jnp.concatenate([x1 * cs - x2 * sn, x2 * cs + x1 * sn, x[..., ROT_DIMS:]], axis=-1)


def gated_linear_chunked(q, k, v, log_g, chunk):
    bsz, seq, nh, dk = q.shape
    dv = v.shape[-1]
    n = seq // chunk

    def to_chunks(t):
        return t.reshape(bsz, n, chunk, nh, t.shape[-1]).transpose(1, 0, 3, 2, 4)

    qc, kc, vc = to_chunks(q), to_chunks(k), to_chunks(v)
    gc = to_chunks(log_g.astype(F32))
    causal = jnp.tril(jnp.ones((chunk, chunk), dtype=bool))[:, :, None]

    def step(state, inp):
        qb, kb, vb, gb = inp
        qb, kb, vb = qb.astype(F32), kb.astype(F32), vb.astype(F32)
        bcum = lax.cumsum(gb, axis=2)
        diff = bcum[:, :, :, None, :] - bcum[:, :, None, :, :]
        decay = jnp.exp(jnp.where(causal, diff, -jnp.inf))
        scores = jnp.einsum('bhik,bhjk,bhijk->bhij', qb, kb, decay)
        intra = jnp.einsum('bhij,bhjv->bhiv', scores, vb)
        inter = jnp.einsum('bhik,bhkv->bhiv', qb * jnp.exp(bcum), state)
        total = bcum[:, :, -1, :]
        state = state * jnp.exp(total)[..., None] + jnp.einsum(
            'bhjk,bhjv->bhkv', kb * jnp.exp(total[:, :, None, :] - bcum), vb)
        return state, intra + inter

    state0 = jnp.zeros((bsz, nh, dk, dv), F32)
    _, out = lax.scan(step, state0, (qc, kc, vc, gc))
    return out.transpose(1, 0, 3, 2, 4).reshape(bsz, seq, nh, dv).astype(v.dtype)


def sliding_window_sink_attention(q, k, v, sinks):
    bsz, seq, hq, d = q.shape
    hkv = k.shape[2]
    grp = hq // hkv
    nb = seq // WINDOW
    qb = q.reshape(bsz, nb, WINDOW, hkv, grp, d)

    def with_prev(t):
        tb = t.reshape(bsz, nb, WINDOW, hkv, d)
        prev = jnp.concatenate([jnp.zeros_like(tb[:, :1]), tb[:, :-1]], axis=1)
        return jnp.concatenate([prev, tb], axis=2)

    kk, vv = with_prev(k), with_prev(v)
    s = jnp.einsum('bnqhgd,bnkhd->bnhgqk', qb, kk).astype(F32) * (d ** -0.5)
    qi = jnp.arange(WINDOW)[:, None] + WINDOW
    kj = jnp.arange(2 * WINDOW)[None, :]
    rel = qi - kj
    band = (rel >= 0) & (rel < WINDOW)
    blk = jnp.arange(nb)[:, None, None]
    mask = band[None] & ((blk > 0) | (kj >= WINDOW)[None])
    s = jnp.where(mask[None, :, None, None], s, -jnp.inf)
    sink = sinks.astype(F32).reshape(1, 1, hkv, grp, 1, 1)
    m = jnp.maximum(jnp.max(s, axis=-1, keepdims=True), sink)
    p = jnp.exp(s - m)
    p = p / (jnp.sum(p, axis=-1, keepdims=True) + jnp.exp(sink - m))
    o = jnp.einsum('bnhgqk,bnkhd->bnqhgd', p.astype(v.dtype), vv)
    return o.reshape(bsz, seq, hq, d)


def differential_attention(q, k, v, lam):
    bsz, seq, nh, _, d = q.shape
    nb = seq // Q_BLOCK
    qblocks = q.reshape(bsz, nb, Q_BLOCK, nh, 2, d).transpose(1, 0, 2, 3, 4, 5)
    kpos = jnp.arange(seq)
    scale = d ** -0.5

    def one_block(args):
        qb, n = args
        s = jnp.einsum('bqhmd,bkhmd->bhmqk', qb, k).astype(F32) * scale
        qpos = n * Q_BLOCK + jnp.arange(Q_BLOCK)
        s = jnp.where(kpos[None, :] <= qpos[:, None], s, -jnp.inf)
        p = jax.nn.softmax(s, axis=-1)
        a = p[:, :, 0] - lam * p[:, :, 1]
        return jnp.einsum('bhqk,bkhe->bqhe', a.astype(v.dtype), v)

    out = lax.map(one_block, (qblocks, jnp.arange(nb)))
    return out.transpose(1, 0, 2, 3, 4).reshape(bsz, seq, nh, v.shape[-1])


def even_mixer(h, w_in, gla_gate_w, gla_gate_b, gla_norm_w, swa_sinks, w_out, cos, sin):
    bsz, seq, _ = h.shape
    gq, gk, gv, gr, glr, sq, sk, sv = split_cols(h @ w_in, EVEN_SPLITS)
    gq = gq.reshape(bsz, seq, N_GLA_HEADS, GLA_DK) * (GLA_DK ** -0.5)
    gk = gk.reshape(bsz, seq, N_GLA_HEADS, GLA_DK)
    gv = gv.reshape(bsz, seq, N_GLA_HEADS, GLA_DV)
    log_a = jax.nn.log_sigmoid((glr @ gla_gate_w + gla_gate_b).astype(F32)) / GLA_GATE_NORMALIZER
    log_a = log_a.reshape(bsz, seq, N_GLA_HEADS, GLA_DK)
    o_gla = gated_linear_chunked(gq, gk, gv, log_a, GLA_CHUNK)
    o_gla = rms_norm(o_gla, gla_norm_w).reshape(bsz, seq, -1) * jax.nn.silu(gr)
    sq = apply_partial_rope(sq.reshape(bsz, seq, N_SWA_HEADS, HEAD_DIM), cos, sin)
    sk = apply_partial_rope(sk.reshape(bsz, seq, N_SWA_KV_HEADS, HEAD_DIM), cos, sin)
    sv = sv.reshape(bsz, seq, N_SWA_KV_HEADS, HEAD_DIM)
    o_swa = sliding_window_sink_attention(sq, sk, sv, swa_sinks).reshape(bsz, seq, -1)
    return jnp.concatenate([o_gla, o_swa], axis=-1) @ w_out


def odd_mixer(h, w_in, diff_lambda, diff_norm_w, lb, hgrn_norm_w, w_out, cos, sin, lam_init):
    bsz, seq, _ = h.shape
    dq, dk, dv, hq, hf, hi, hg = split_cols(h @ w_in, ODD_SPLITS)
    dq = apply_partial_rope(dq.reshape(bsz, seq, N_DIFF_HEADS, 2, HEAD_DIM), cos, sin)
    dk = apply_partial_rope(dk.reshape(bsz, seq, N_DIFF_HEADS, 2, HEAD_DIM), cos, sin)
    dv = dv.reshape(bsz, seq, N_DIFF_HEADS, 2 * HEAD_DIM)
    lv = diff_lambda.astype(F32)
    lam = jnp.exp(jnp.sum(lv[0] * lv[1])) - jnp.exp(jnp.sum(lv[2] * lv[3])) + lam_init
    o_diff = differential_attention(dq, dk, dv, lam)
    o_diff = rms_norm(o_diff, diff_norm_w).reshape(bsz, seq, -1) * (1.0 - lam_init)
    hq = (jax.nn.silu(hq) * (HGRN_EXPAND ** -0.5)).reshape(bsz, seq, N_HGRN_HEADS, HGRN_EXPAND)
    z = hf.astype(F32)
    log_f = jnp.logaddexp(jnp.log(lb), jnp.log1p(-lb) + jax.nn.log_sigmoid(z))
    k_in = (1.0 - lb) * jax.nn.sigmoid(-z)
    log_f = log_f.reshape(bsz, seq, N_HGRN_HEADS, HGRN_EXPAND)
    k_in = k_in.reshape(bsz, seq, N_HGRN_HEADS, HGRN_EXPAND)
    hi = hi.reshape(bsz, seq, N_HGRN_HEADS, HGRN_DV)
    o_h = gated_linear_chunked(hq, k_in, hi, log_f, HGRN_CHUNK)
    o_h = rms_norm(o_h, hgrn_norm_w).reshape(bsz, seq, -1) * jax.nn.silu(hg)
    return jnp.concatenate([o_diff, o_h], axis=-1) @ w_out


def conv_ffn(h, w_in, conv_w, conv_b, w_out):
    a, u = jnp.split(h @ w_in, 2, axis=-1)
    ap = jnp.pad(a, ((0, 0), (CONV_WIDTH - 1, 0), (0, 0)))
    a = ap[:, :-2] * conv_w[0] + ap[:, 1:-1] * conv_w[1] + ap[:, 2:] * conv_w[2] + conv_b
    return (jax.nn.silu(a) * u) @ w_out


def setup_inputs(seed: int = 0) -> dict:
    key = jax.random.key(seed)
    ks = jax.random.split(key, 32)

    def nrm(k, shape, s):
        return jax.random.normal(k, shape, F32) * s

    offsets = jax.random.randint(ks[2], (BATCH, 1), 0, 4096, dtype=jnp.int32)
    return {
        'x': nrm(ks[0], (BATCH, SEQ, D_MODEL), 1.0),
        'c': nrm(ks[1], (BATCH, D_MODEL), 1.0),
        'positions': offsets + jnp.arange(SEQ, dtype=jnp.int32)[None, :],
        'mod_w': nrm(ks[3], (DEPTH, D_MODEL, 6 * D_MODEL), 0.5 * D_MODEL ** -0.5),
        'mod_b': nrm(ks[4], (DEPTH, 6 * D_MODEL), 0.01),
        'norm_mix_w': 1.0 + nrm(ks[5], (DEPTH, D_MODEL), 0.02),
        'norm_ffn_w': 1.0 + nrm(ks[6], (DEPTH, D_MODEL), 0.02),
        'ev_w_in': nrm(ks[7], (N_EVEN, D_MODEL, sum(EVEN_SPLITS)), D_MODEL ** -0.5),
        'gla_gate_w': nrm(ks[8], (N_EVEN, GLA_GATE_RANK, N_GLA_HEADS * GLA_DK), GLA_GATE_RANK ** -0.5),
        'gla_gate_b': nrm(ks[9], (N_EVEN, N_GLA_HEADS * GLA_DK), 0.01),
        'gla_norm_w': 1.0 + nrm(ks[10], (N_EVEN, GLA_DV), 0.02),
        'swa_sinks': nrm(ks[11], (N_EVEN, N_SWA_HEADS), 1.0),
        'ev_w_out': nrm(ks[12], (N_EVEN, D_MODEL, D_MODEL), D_MODEL ** -0.5),
        'od_w_in': nrm(ks[13], (N_ODD, D_MODEL, sum(ODD_SPLITS)), D_MODEL ** -0.5),
        'diff_lambda': nrm(ks[14], (N_ODD, 4, HEAD_DIM), 0.1),
        'diff_norm_w': 1.0 + nrm(ks[15], (N_ODD, 2 * HEAD_DIM), 0.02),
        'hgrn_lb_logits': nrm(ks[16], (N_ODD, N_HGRN_HEADS * HGRN_EXPAND), 1.0),
        'hgrn_norm_w': 1.0 + nrm(ks[17], (N_ODD, HGRN_DV), 0.02),
        'od_w_out': nrm(ks[18], (N_ODD, D_MODEL, D_MODEL), D_MODEL ** -0.5),
        'ffn_w_in': nrm(ks[19], (DEPTH, D_MODEL, 2 * D_FF), D_MODEL ** -0.5),
        'ffn_conv_w': nrm(ks[20], (DEPTH, CONV_WIDTH, D_FF), CONV_WIDTH ** -0.5),
        'ffn_conv_b': nrm(ks[21], (DEPTH, D_FF), 0.01),
        'ffn_w_out': nrm(ks[22], (DEPTH, D_FF, D_MODEL), D_FF ** -0.5),
        'final_norm_w': 1.0 + nrm(ks[23], (D_MODEL,), 0.02),
    }


def reference(x, c, positions, mod_w, mod_b, norm_mix_w, norm_ffn_w, ev_w_in, gla_gate_w, gla_gate_b,
              gla_norm_w, swa_sinks, ev_w_out, od_w_in, diff_lambda, diff_norm_w, hgrn_lb_logits,
              hgrn_norm_w, od_w_out, ffn_w_in, ffn_conv_w, ffn_conv_b, ffn_w_out, final_norm_w):
    cos, sin = rope_tables(positions)
    lbs = lax.cumsum(jax.nn.softmax(hgrn_lb_logits.astype(F32), axis=0), axis=0)
    lbs = lbs - lbs[0:1]
    c_act = jax.nn.silu(c)
    for l in range(DEPTH):
        mod = (c_act @ mod_w[l] + mod_b[l])[:, None, :]
        sh1, sc1, g1, sh2, sc2, g2 = jnp.split(mod, 6, axis=-1)
        h = rms_norm(x, norm_mix_w[l]) * (1.0 + sc1) + sh1
        j = l // 2
        if l % 2 == 0:
            y = even_mixer(h, ev_w_in[j], gla_gate_w[j], gla_gate_b[j], gla_norm_w[j], swa_sinks[j],
                           ev_w_out[j], cos, sin)
        else:
            lam_init = 0.8 - 0.6 * math.exp(-0.3 * l)
            y = odd_mixer(h, od_w_in[j], diff_lambda[j], diff_norm_w[j], lbs[j], hgrn_norm_w[j],
                          od_w_out[j], cos, sin, lam_init)
        x = x + g1 * y
        h = rms_norm(x, norm_ffn_w[l]) * (1.0 + sc2) + sh2
        x = x + g2 * conv_ffn(h, ffn_w_in[l], ffn_conv_w[l], ffn_conv_b[l], ffn_w_out[l])
    return rms_norm(x, final_norm_w)
```

```python
import math
from contextlib import ExitStack
import numpy as np
import concourse.bass as bass
import concourse.mybir as mybir
from concourse.bass_utils import run_bass_kernel_spmd

F32 = mybir.dt.float32
BF16 = mybir.dt.bfloat16
I32 = mybir.dt.int32
ALU = mybir.AluOpType
AF = mybir.ActivationFunctionType
AX = mybir.AxisListType

D = 1024
SEQ = 4096
DEPTH = 4
DFF = 2816
NT = SEQ // 128
NG = SEQ // 512
EPS = 1e-6
EV_COLS = 2320
OD_COLS = 3584
NCHUNK_FF = DFF // 128
CM_N = 453
TWO_PI = 2.0 * math.pi
EPOCH = 30000


class Buf:
    __slots__ = ("w", "r", "strict")

    def __init__(self):
        self.w = None
        self.r = {}
        self.strict = False


class V:
    __slots__ = ("ap", "bufs")

    def __init__(self, ap, bufs):
        self.ap = ap
        self.bufs = bufs


class _Sub:
    def __init__(self, t, bufs):
        self.t = t
        self.bufs = bufs

    def __getitem__(self, idx):
        return V(self.t.h[idx], self.bufs)


class T:
    def __init__(self, h, nsub=1):
        self.h = h
        self.bufs = tuple(Buf() for _ in range(nsub))

    def __getitem__(self, idx):
        return V(self.h[idx], self.bufs)

    def s(self, i):
        return _Sub(self, (self.bufs[i],))

    def ss(self, idxs):
        return _Sub(self, tuple(self.bufs[i] for i in idxs))

    def v(self, ap, subs=None):
        return V(ap, self.bufs if subs is None else tuple(self.bufs[i] for i in subs))


class Eng:
    def __init__(self, S, name, h):
        self.S = S
        self.name = name
        self.h = h
        self.known = {}
        self.sem_id = S.new_sem(name)
        self.count = 0


class Sched:
    def __init__(self, nc, stack):
        self.nc = nc
        self.stack = stack
        self.sems = []
        self.pe = Eng(self, "pe", nc.tensor)
        self.act = Eng(self, "act", nc.scalar)
        self.dve = Eng(self, "dve", nc.vector)
        self.pool = Eng(self, "pool", nc.gpsimd)
        self.sp = Eng(self, "sp", nc.sync)
        self.engs = [self.pe, self.act, self.dve, self.pool, self.sp]
        self.dma_K = 8
        self.rings = {}
        for e in (self.sp, self.pool):
            self.rings[e.name] = {"ring": [{"sem": self.new_sem("dma" + e.name), "n": 0} for _ in range(self.dma_K)], "i": 0}
        self.nt = 0
        self.ninst = 0

    def new_sem(self, name):
        h = self.stack.enter_context(self.nc.semaphore("s%d_%s" % (len(self.sems), name)))
        self.sems.append(h)
        return len(self.sems) - 1

    def sb(self, shape, dtype, nsub=1, stack=None):
        self.nt += 1
        h = (stack or self.stack).enter_context(self.nc.sbuf_tensor("t%d" % self.nt, list(shape), dtype))
        return T(h, nsub)

    def ps(self, shape, dtype, stack=None):
        self.nt += 1
        h = (stack or self.stack).enter_context(self.nc.psum_tensor("p%d" % self.nt, list(shape), dtype))
        return T(h)

    def _need(self, eng, ev, waits):
        sid, val = ev
        if eng.known.get(sid, 0) >= val:
            return
        if waits.get(sid, 0) < val:
            waits[sid] = val

    def _collect(self, eng, reads, writes, my_sid):
        waits = {}
        for v in reads:
            for b in v.bufs:
                if b.w is not None and not (eng is self.pe and b.w[0] == my_sid):
                    self._need(eng, b.w, waits)
        pool_strict = eng is self.pool
        for v in writes:
            for b in v.bufs:
                strict = pool_strict or b.strict
                if b.w is not None and (b.w[0] != my_sid or strict):
                    self._need(eng, b.w, waits)
                for sid, val in b.r.items():
                    if sid != my_sid or strict:
                        self._need(eng, (sid, val), waits)
        return waits

    def _emit_waits(self, eng, waits):
        if eng.sem_id in waits:
            waits[eng.sem_id] = max(waits[eng.sem_id], eng.count - 3)
        for sid, val in waits.items():
            eng.h.wait_ge(self.sems[sid], val)
            eng.known[sid] = val
            self.ninst += 1

    def _record(self, ev, reads, writes):
        sid, val = ev
        for v in reads:
            for b in v.bufs:
                if b.r.get(sid, 0) < val:
                    b.r[sid] = val
        for v in writes:
            for b in v.bufs:
                b.w = ev
                b.r = {}

    def op(self, eng, fn, reads, writes):
        if eng.count >= EPOCH:
            eng.sem_id = self.new_sem(eng.name)
            eng.count = 0
        waits = self._collect(eng, reads, writes, eng.sem_id)
        self._emit_waits(eng, waits)
        ins = fn()
        eng.count += 1
        self.ninst += 1
        ins.then_inc(self.sems[eng.sem_id], 1)
        ev = (eng.sem_id, eng.count)
        self._record(ev, reads, writes)
        return ev

    def dma(self, eng, out, in_):
        rs = self.rings[eng.name]
        slot = rs["ring"][rs["i"] % self.dma_K]
        rs["i"] += 1
        sid = slot["sem"]
        waits = self._collect(eng, [in_], [out], -1)
        if slot["n"] > 0:
            self._need(eng, (sid, 16 * slot["n"]), waits)
        self._emit_waits(eng, waits)
        ins = eng.h.dma_start(out=out.ap, in_=in_.ap)
        slot["n"] += 1
        ins.then_inc(self.sems[sid], 16)
        self.ninst += 1
        ev = (sid, 16 * slot["n"])
        self._record(ev, [in_], [out])
        return ev

    def wait_all(self, eng):
        waits = {}
        for e in self.engs:
            if e.count > 0 and e is not eng:
                self._need(eng, (e.sem_id, e.count), waits)
        for rs in self.rings.values():
            for slot in rs["ring"]:
                if slot["n"] > 0:
                    self._need(eng, (slot["sem"], 16 * slot["n"]), waits)
        self._emit_waits(eng, waits)

    def barrier(self):
        for e in self.engs:
            self.wait_all(e)

    def mm(self, out, lhsT, rhs, start=True, stop=True):
        return self.op(self.pe, lambda: self.nc.tensor.matmul(out.ap, lhsT.ap, rhs.ap, start=start, stop=stop),
                       [lhsT, rhs], [out])

    def transpose(self, out, in_, ident):
        return self.op(self.pe, lambda: self.nc.tensor.transpose(out.ap, in_.ap, ident.ap), [in_, ident], [out])

    def actf(self, out, in_, func, bias=None, scale=None, accum_out=None):
        reads = [in_]
        kw = {}
        if bias is not None:
            if isinstance(bias, V):
                reads.append(bias)
                kw["bias"] = bias.ap
            else:
                kw["bias"] = bias
        if scale is not None:
            if isinstance(scale, V):
                reads.append(scale)
                kw["scale"] = scale.ap
            else:
                kw["scale"] = scale
        writes = [out]
        if accum_out is not None:
            writes.append(accum_out)
            kw["accum_out"] = accum_out.ap
        return self.op(self.act, lambda: self.nc.scalar.activation(out.ap, in_.ap, func, **kw), reads, writes)

    def tt(self, out, in0, in1, op, eng=None):
        e = eng or self.dve
        return self.op(e, lambda: e.h.tensor_tensor(out.ap, in0.ap, in1.ap, op), [in0, in1], [out])

    def ts(self, out, in0, s1, s2, op0, op1=None, eng=None):
        e = eng or self.dve
        reads = [in0]
        a1, a2 = s1, s2
        if isinstance(s1, V):
            reads.append(s1)
            a1 = s1.ap
        if isinstance(s2, V):
            reads.append(s2)
            a2 = s2.ap
        if op1 is None:
            return self.op(e, lambda: e.h.tensor_scalar(out.ap, in0.ap, a1, a2, op0), reads, [out])
        return self.op(e, lambda: e.h.tensor_scalar(out.ap, in0.ap, a1, a2, op0, op1), reads, [out])

    def stt(self, out, in0, scalar, in1, op0, op1, eng=None):
        e = self.dve
        reads = [in0, in1]
        a = scalar
        if isinstance(scalar, V):
            reads.append(scalar)
            a = scalar.ap
        return self.op(e, lambda: e.h.scalar_tensor_tensor(out.ap, in0.ap, a, in1.ap, op0, op1), reads, [out])

    def copy(self, out, in_, eng=None):
        e = eng or self.dve
        if e is self.act:
            return self.op(e, lambda: self.nc.scalar.copy(out.ap, in_.ap), [in_], [out])
        return self.op(e, lambda: e.h.tensor_copy(out.ap, in_.ap), [in_], [out])

    def memset(self, out, val, eng=None):
        e = eng or self.dve
        return self.op(e, lambda: e.h.memset(out.ap, val), [], [out])

    def reduce_sum(self, out, in_, eng=None):
        e = eng or self.dve
        return self.op(e, lambda: e.h.tensor_reduce(out.ap, in_.ap, AX.X, ALU.add), [in_], [out])

    def rstd(self, out, tmp, ss, scale):
        self.actf(tmp, ss, AF.Ln, scale=scale, bias=EPS)
        self.actf(out, tmp, AF.Exp, scale=-0.5)

    def recip(self, out, in_):
        return self.op(self.dve, lambda: self.nc.vector.reciprocal(out.ap, in_.ap), [in_], [out])


class PsumPool:
    def __init__(self, S, stack):
        self.S = S
        self.banks = [S.ps([128, 512], F32, stack=stack) for _ in range(8)]
        self.avail = list(range(8))
        self.i = 0

    def set_avail(self, lst):
        self.avail = list(lst)
        self.i = 0

    def get(self):
        b = self.banks[self.avail[self.i % len(self.avail)]]
        self.i += 1
        return b


def bf16v(t):
    return t.h[:].bitcast(BF16)


def _consts():
    ident = np.eye(128, dtype=np.float32)
    j = np.arange(128)[:, None]
    i = np.arange(128)[None, :]
    mask_cur = (j <= i).astype(np.float32)
    mask_prev = (j > i).astype(np.float32)
    cm = np.zeros((128, CM_N), np.float32)
    offs = [0, 32, 96, 192]
    jp = np.arange(128)
    for I in range(4):
        r = 32 * I - 1
        for jj in range(32 * (I + 1)):
            col = offs[I] + jj
            cm[(jp > jj) & (jp <= r), col] = 1.0
            cm[(jp > r) & (jp <= jj), col] = -1.0
    for ii in range(128):
        cm[(jp >= 32 * (ii // 32)) & (jp <= ii), 320 + ii] = 1.0
    for I in range(4):
        cm[jp <= 32 * I - 1, 448 + I] = 1.0
    cm[:, 452] = 1.0
    su = (j > i).astype(np.float32)
    perm = np.zeros((128, 128), np.float32)
    invf = np.zeros((128, 1), np.float32)
    sgn = np.zeros((128, 1), np.float32)
    inv_freq = (np.float32(500000.0) ** (-np.arange(0, 16, 2, dtype=np.float32) / np.float32(16))).astype(np.float32)
    for f in range(128):
        m = f % 64
        if m < 8:
            perm[f + 8, f] = 1.0
            invf[f] = inv_freq[m]
            sgn[f] = -1.0
        elif m < 16:
            perm[f - 8, f] = 1.0
            invf[f] = inv_freq[m - 8]
            sgn[f] = 1.0
    hm = np.zeros((128, 2), np.float32)
    hm[0:64, 0] = 1.0
    hm[64:128, 1] = 1.0
    parts = [ident, mask_cur, mask_prev, cm, su, perm, invf, sgn, hm]
    offsets = {}
    o = 0
    for name, p in zip(["ident", "mask_cur", "mask_prev", "cm", "su", "perm", "invf", "sgn", "hm"], parts):
        offsets[name] = (o, p.shape[1])
        o += p.shape[1]
    return np.concatenate(parts, axis=1), offsets


CONSTS, COFF = _consts()
NCONST = CONSTS.shape[1]


class Prog:
    def __init__(self, dbg=None):
        self.dbg = dbg or {}
        self.nc = bass.Bass("TRN2", target_bir_lowering=False)
        nc = self.nc
        self.din = {}

        def inp(name, shape, dt=F32):
            self.din[name] = T(nc.dram_tensor(name, list(shape), dt, kind="ExternalInput").ap())
            return self.din[name]

        inp("x", [SEQ, D])
        inp("c_t", [128, 8])
        inp("pos", [1, SEQ], I32)
        inp("consts", [128, NCONST])
        inp("mod_w", [DEPTH, D, 6 * D])
        inp("mod_bc", [DEPTH, 128, 48])
        inp("mod_b", [DEPTH, 6 * D])
        inp("nmw_c", [DEPTH, 128, 8])
        inp("nfw_c", [DEPTH, 128, 8])
        inp("ev_w_in", [2, D, EV_COLS])
        inp("gatew_ext", [2, 32, 256])
        inp("gla_norm_w", [2, 128])
        inp("swa_sinks", [2, 8])
        inp("ev_w_out", [2, D, D])
        inp("od_w_in", [2, D, OD_COLS])
        inp("diff_lambda", [2, 256])
        inp("diff_norm_w", [2, 128])
        inp("lb_c", [128, 2, 4])
        inp("lb_r", [2, 512])
        inp("hgrn_norm_w", [2, 128])
        inp("od_w_out", [2, D, D])
        inp("ffn_w_in", [DEPTH, D, 2 * DFF])
        inp("convw_c", [DEPTH, 128, 3, NCHUNK_FF])
        inp("convb_c", [DEPTH, 128, NCHUNK_FF])
        inp("ffn_w_out", [DEPTH, DFF, D])
        inp("final_norm_w", [1, D])
        self.out = T(nc.dram_tensor("out", [SEQ, D], F32, kind="ExternalOutput").ap(), NT)
        self.xres = T(nc.dram_tensor("xres", [SEQ, D], F32, kind="Internal").ap(), NT)
        self.ropetab = T(nc.dram_tensor("ropetab", [2, 128, SEQ], F32, kind="Internal").ap(), 2)
        self.gates = T(nc.dram_tensor("gates", [DEPTH * 2, D], F32, kind="Internal").ap(), DEPTH * 2)
        self.odiff = T(nc.dram_tensor("odiff", [SEQ, 512], BF16, kind="Internal").ap(), NT)

        with ExitStack() as st:
            self.S = Sched(nc, st)
            self.pp = PsumPool(self.S, st)
            self.build(st)
            self.S.barrier()

    def cv(self, name, rows=128):
        o, n = COFF[name]
        return self.cst[0:rows, o:o + n]

    def build(self, st):
        S = self.S
        nc = self.nc
        self.cst = S.sb([128, NCONST], F32)
        S.dma(S.sp, self.cst[:], self.din["consts"][:])
        self.ident = S.sb([128, 128], BF16)
        S.copy(self.ident[:], self.cv("ident"))
        self.mask_cur = S.sb([128, 128], BF16)
        S.copy(self.mask_cur[:], self.cv("mask_cur"))
        self.mask_prev = S.sb([128, 128], BF16)
        S.copy(self.mask_prev[:], self.cv("mask_prev"))
        self.mb_cur = S.sb([128, 128], BF16)
        S.ts(self.mb_cur[:], self.cv("mask_cur"), 30000.0, -30000.0, ALU.mult, ALU.add)
        self.mb_prev = S.sb([128, 128], BF16)
        S.ts(self.mb_prev[:], self.cv("mask_prev"), 30000.0, -30000.0, ALU.mult, ALU.add)
        self.modc = S.sb([128, DEPTH, 4, 8], F32)
        self.lbc = S.sb([128, 2, 4], F32)
        self.omlc = S.sb([128, 2, 4], F32)
        for b in self.pp.banks:
            S.memset(b[:], 0.0)

        self.prologue()
        x_src = self.din["x"]
        x_src_subs = False
        nlayers = self.dbg.get("nlayers", DEPTH)
        for l in range(nlayers):
            if self.dbg.get("skip_mixer"):
                pass
            elif l % 2 == 0:
                self.even_pass(l, x_src, x_src_subs)
                x_src, x_src_subs = self.xres, True
            else:
                self.odd_pass_a(l, x_src, x_src_subs)
                self.odd_pass_b(l, x_src, x_src_subs)
                x_src, x_src_subs = self.xres, True
            if not self.dbg.get("skip_ffn"):
                self.ffn_pass(l, x_src, x_src_subs)
                x_src, x_src_subs = self.xres, True
        self.final_pass(x_src, x_src_subs)

    def xv(self, src, subs, t):
        ap = src.h[t * 128:(t + 1) * 128, :]
        return src.v(ap, [t] if subs else None)

    def prologue(self):
        S = self.S
        nc = self.nc
        with ExitStack() as st:
            ct = S.sb([128, 8], F32, stack=st)
            S.dma(S.sp, ct[:], self.din["c_t"][:])
            cact = S.sb([128, 8], F32, stack=st)
            S.actf(cact[:], ct[:], AF.Silu)
            cact_b = S.sb([128, 8], BF16, stack=st)
            S.copy(cact_b[:], cact[:])
            cact_bc = S.sb([128, 8, 128], BF16, stack=st)
            S.copy(cact_bc[:], V(cact.h[:].unsqueeze(2).to_broadcast([128, 8, 128]), cact.bufs))
            mw = [S.sb([128, 8, 1024], BF16, stack=st) for _ in range(2)]
            modbc = S.sb([128, DEPTH, 48], F32, stack=st)
            S.dma(S.sp, modbc[:], V(self.din["mod_bc"].h.rearrange("l p c -> p l c"), self.din["mod_bc"].bufs))
            nmw = S.sb([128, DEPTH, 8], F32, stack=st)
            S.dma(S.sp, nmw[:], V(self.din["nmw_c"].h.rearrange("l p c -> p l c"), self.din["nmw_c"].bufs))
            nfw = S.sb([128, DEPTH, 8], F32, stack=st)
            S.dma(S.sp, nfw[:], V(self.din["nfw_c"].h.rearrange("l p c -> p l c"), self.din["nfw_c"].bufs))
            gb = [S.sb([128, 1024], F32, stack=st) for _ in range(2)]
            grow = [S.sb([128, 1024], F32, stack=st) for _ in range(2)]
            it = 0
            nl = self.dbg.get("nlayers", DEPTH)
            for l in range(nl):
                for piece in range(6):
                    w = mw[it % 2]
                    it += 1
                    src = self.din["mod_w"].h[l, :, piece * 1024:(piece + 1) * 1024].rearrange("(k p) n -> p k n", p=128)
                    S.dma(S.pool, w[:], V(src, self.din["mod_w"].bufs))
                    if piece in (2, 5):
                        gi = 0 if piece == 2 else 1
                        S.dma(S.sp, gb[gi][:], V(self.din["mod_b"].h[l:l + 1, piece * 1024:(piece + 1) * 1024].partition_broadcast(128), self.din["mod_b"].bufs))
                        for n in range(2):
                            pb = self.pp.get()
                            for k in range(8):
                                S.mm(pb[:], cact_bc[:, k, :], w[:, k, n * 512:(n + 1) * 512], start=(k == 0), stop=(k == 7))
                            S.tt(grow[gi][:, n * 512:(n + 1) * 512], pb[:], gb[gi][:, n * 512:(n + 1) * 512], ALU.add)
                        S.dma(S.sp, self.gates.s(l * 2 + gi)[l * 2 + gi:l * 2 + gi + 1, :], grow[gi][0:1, :])
                    else:
                        kind = {0: 0, 1: 1, 3: 2, 4: 3}[piece]
                        pb = self.pp.get()
                        for cch in range(8):
                            for k in range(8):
                                S.mm(pb[:, cch:cch + 1], w[:, k, cch * 128:(cch + 1) * 128], cact_b[:, k:k + 1], start=(k == 0), stop=(k == 7))
                        S.tt(self.modc[:, l, kind, :], pb[:, 0:8], modbc[:, l, piece * 8:(piece + 1) * 8], ALU.add)
                        if kind in (1, 3):
                            nw = nmw if kind == 1 else nfw
                            S.stt(self.modc[:, l, kind, :], self.modc[:, l, kind, :], 1.0, nw[:, l, :], ALU.add, ALU.mult)
            lc = S.sb([128, 2, 4], F32, stack=st)
            S.dma(S.sp, lc[:], self.din["lb_c"][:])
            S.memset(self.lbc[:], 0.0)
            dlt = S.sb([128, 4], F32, stack=st)
            S.tt(dlt[:], lc[:, 1, :], lc[:, 0, :], ALU.subtract)
            S.actf(self.lbc[:, 1, :], dlt[:], AF.Sigmoid)
            S.ts(self.omlc[:], self.lbc[:], -1.0, 1.0, ALU.mult, ALU.add)
        with ExitStack() as st:
            posi = S.sb([128, 1024], I32, stack=st)
            ang = S.sb([128, 1024], F32, stack=st)
            kf = S.sb([128, 1024], F32, stack=st)
            ki = S.sb([128, 1024], I32, stack=st)
            r = S.sb([128, 1024], F32, stack=st)
            ra = S.sb([128, 1024], F32, stack=st)
            cs = S.sb([128, 1024], F32, stack=st)
            sn = S.sb([128, 1024], F32, stack=st)
            C1 = 6.28125
            C2 = TWO_PI - C1
            for q in range(4):
                sl = slice(q * 1024, (q + 1) * 1024)
                S.dma(S.sp, posi[:], V(self.din["pos"].h[0:1, sl].partition_broadcast(128), self.din["pos"].bufs))
                S.copy(ang[:], posi[:])
                S.ts(ang[:], ang[:], self.cv("invf"), None, ALU.mult)
                S.ts(kf[:], ang[:], 1.0 / TWO_PI, None, ALU.mult)
                S.copy(ki[:], kf[:])
                S.copy(kf[:], ki[:])
                S.stt(r[:], kf[:], -C1, ang[:], ALU.mult, ALU.add)
                S.stt(r[:], kf[:], -C2, r[:], ALU.mult, ALU.add)
                S.ts(r[:], r[:], -math.pi, math.pi, ALU.max, ALU.min)
                S.actf(sn[:], r[:], AF.Sin)
                S.ts(sn[:], sn[:], self.cv("sgn"), None, ALU.mult)
                S.actf(ra[:], r[:], AF.Abs)
                S.ts(ra[:], ra[:], -1.0, math.pi / 2, ALU.mult, ALU.add)
                S.actf(cs[:], ra[:], AF.Sin)
                S.dma(S.sp, self.ropetab.s(0)[0, :, sl], cs[:])
                S.dma(S.sp, self.ropetab.s(1)[1, :, sl], sn[:])

    def load_w(self, wt, src_t, src_ap_fn, nk):
        for k in range(nk):
            self.S.dma(self.S.pool, wt.s(k)[:, k, :], V(src_ap_fn(k), src_t.bufs))

    def norm_front(self, ctx, x_src, x_subs, g):
        S = self.S
        ss = ctx["ss"][g % 2]
        nb = ctx["batch"]
        S.memset(ss[:, 0, :], 0.0, eng=S.pool)
        for j0 in range(0, 4, nb):
            xts = []
            for j in range(j0, j0 + nb):
                t = g * 4 + j
                xt = ctx["xring"][ctx["xi"] % len(ctx["xring"])]
                ctx["xi"] += 1
                S.dma(S.sp, xt[:], self.xv(x_src, x_subs, t))
                S.actf(ctx["sq"][j % len(ctx["sq"])][:], xt[:], AF.Square, accum_out=ss[:, 0, j:j + 1])
                xts.append(xt)
            S.rstd(ss[:, 2, j0:j0 + nb], ss[:, 1, j0:j0 + nb], ss[:, 0, j0:j0 + nb], 1.0 / D)
            for j in range(j0, j0 + nb):
                S.ts(ctx["xn"][j][:], xts[j - j0][:], ss[:, 2, j:j + 1], None, ALU.mult)

    def norm_back(self, ctx, l, which, hT):
        S = self.S
        kind_sh, kind_w = (0, 1) if which == 0 else (2, 3)
        for j in range(4):
            xn = ctx["xn"][j]
            pb = self.pp.get()
            pv = bf16v(pb)
            for k in range(8):
                S.transpose(V(pv[:, k * 128:(k + 1) * 128], pb.bufs), xn[:, k * 128:(k + 1) * 128], self.ident[:])
            for k in range(8):
                o = hT[:, k, j * 128:(j + 1) * 128]
                i_ = V(pv[:, k * 128:(k + 1) * 128], pb.bufs)
                if k % 2 == 0:
                    S.actf(o, i_, AF.Identity, scale=self.modc[:, l, kind_w, k:k + 1], bias=self.modc[:, l, kind_sh, k:k + 1])
                else:
                    S.ts(o, i_, self.modc[:, l, kind_w, k:k + 1], self.modc[:, l, kind_sh, k:k + 1], ALU.mult, ALU.add)

    def norm_stage(self, ctx, l, which, x_src, x_subs, g, hT):
        self.norm_front(ctx, x_src, x_subs, g)
        self.norm_back(ctx, l, which, hT)

    def norm_ctx(self, st, nx=4):
        S = self.S
        ctx = {"_": None, "xring": [S.sb([128, D], F32, stack=st) for _ in range(nx)], "xi": 0, "batch": 4 if nx >= 4 else 2,
               "sq": [S.sb([128, D], BF16, stack=st) for _ in range(2 if nx >= 4 else 1)],
               "ss": [S.sb([128, 3, 4], F32, stack=st) for _ in range(2)],
               "xn": [S.sb([128, D], BF16, stack=st) for _ in range(4)]}
        for t_ in ctx["sq"]:
            t_.bufs[0].strict = True
        return ctx

    def out_stage_a(self, octx, t, ocat):
        S = self.S
        pb = self.pp.get()
        pv = bf16v(pb)
        for k in range(8):
            S.transpose(V(pv[:, k * 128:(k + 1) * 128], pb.bufs), ocat[:, k * 128:(k + 1) * 128], self.ident[:])
        oT = octx["oT"][t % 2]
        S.copy(oT[:], V(pv, pb.bufs), eng=S.act)

    def out_stage_b(self, ctx, octx, wout, x_src, x_subs, t, dst):
        S = self.S
        oT = octx["oT"][t % 2]
        xt = ctx["xring"][ctx["xi"] % len(ctx["xring"])]
        ctx["xi"] += 1
        S.dma(S.sp, xt[:], self.xv(x_src, x_subs, t))
        for n in range(2):
            py = self.pp.get()
            for k in range(8):
                S.mm(py[:], oT[:, k * 128:(k + 1) * 128], wout[:, k, n * 512:(n + 1) * 512], start=(k == 0), stop=(k == 7))
            tmp = octx["tmp"][n]
            S.tt(tmp[:], py[:], octx["gate"][:, n * 512:(n + 1) * 512], ALU.mult)
            S.tt(xt[:, n * 512:(n + 1) * 512], tmp[:], xt[:, n * 512:(n + 1) * 512], ALU.add, eng=S.pool)
        S.dma(S.pool, self.xv(dst, True, t), xt[:])

    def out_ctx(self, st, l, gi):
        S = self.S
        octx = {"oT": [S.sb([128, D], BF16, stack=st) for _ in range(2)],
                "tmp": [S.sb([128, 512], F32, stack=st) for _ in range(2)],
                "gate": S.sb([128, D], F32, stack=st)}
        S.dma(S.sp, octx["gate"][:], V(self.gates.h[l * 2 + gi:l * 2 + gi + 1, :].partition_broadcast(128), (self.gates.bufs[l * 2 + gi],)))
        return octx

    def rope_stage(self, rctx, src_ps, dst, g):
        S = self.S
        qf = rctx["qf"][rctx["i"] % 2]
        t1 = rctx["t1"][rctx["i"] % 2]
        rctx["i"] += 1
        S.copy(qf[:], src_ps, eng=S.act)
        prev = rctx.get("pending")
        rctx["pending"] = (qf, t1, dst, rctx["tab"])
        if prev is not None:
            self.rope_finish(prev)

    def rope_finish(self, item):
        S = self.S
        qf, t1, dst, tab = item
        pr = self.pp.get()
        S.mm(pr[:], self.cv("perm"), qf[:])
        S.tt(t1[:], qf[:], tab[:, 0, :], ALU.mult, eng=S.pool)
        S.tt(qf[:], pr[:], tab[:, 1, :], ALU.mult)
        if isinstance(dst, list):
            for (psl, d) in dst:
                S.tt(d, t1[psl, :], qf[psl, :], ALU.add)
        else:
            S.tt(dst, t1[:], qf[:], ALU.add)

    def rope_flush(self, rctx):
        prev = rctx.get("pending")
        rctx["pending"] = None
        if prev is not None:
            self.rope_finish(prev)

    def rope_ctx(self, st):
        S = self.S
        return {"qf": [S.sb([128, 512], F32, stack=st) for _ in range(2)],
                "t1": [S.sb([128, 512], F32, stack=st) for _ in range(2)],
                "tabs": [S.sb([128, 2, 512], F32, stack=st) for _ in range(2)], "i": 0, "tab": None}

    def rope_load(self, rctx, g):
        S = self.S
        tab = rctx["tabs"][g % 2]
        S.dma(S.sp, tab[:, 0, :], self.ropetab.s(0)[0, :, g * 512:(g + 1) * 512])
        S.dma(S.sp, tab[:, 1, :], self.ropetab.s(1)[1, :, g * 512:(g + 1) * 512])
        rctx["tab"] = tab

    def gl_ctx(self, st, nch, dk):
        S = self.S
        F = nch * 128
        c = {"nch": nch, "dk": dk, "F": F,
             "expE": [S.sb([128, CM_N], F32, stack=st) for _ in range(2)],
             "KO": [S.sb([128, nch, 4, 128], BF16, stack=st) for _ in range(2)],
             "qd": [S.sb([128, nch, 128 // dk, 128], BF16, stack=st) for _ in range(2)],
             "qh": [S.sb([128, nch, 128 // dk, 128], BF16, stack=st) for _ in range(2)],
             "qhf": [S.sb([128, 128], F32, stack=st) for _ in range(2)],
             "ek": [S.sb([128, F], F32, stack=st) for _ in range(2)],
             "khat": [S.sb([128, F], BF16, stack=st) for _ in range(2)],
             "A": [S.sb([128, 4, 128], BF16, stack=st) for _ in range(2)],
             "Sf": S.sb([128, nch, 128], F32, stack=st),
             "Sb": [S.sb([128, nch, 128], BF16, stack=st) for _ in range(2)],
             "etot": [S.sb([128, nch], F32, stack=st) for _ in range(2)],
             "sq": S.sb([128, 512], F32, stack=st),
             "ssum": [S.sb([128, 8], F32, stack=st) for _ in range(2)],
             "on": S.sb([128, 512], F32, stack=st),
             "i": 0}
        for t in c["KO"] + c["qd"] + c["qh"]:
            S.memset(t[:], 0.0)
        S.memset(c["Sf"][:], 0.0)
        for t in c["Sb"]:
            S.memset(t[:], 0.0)
        return c

    def gl_front(self, c, qT, kT, g_tm, k_tm, qscale):
        S = self.S
        nch, dk, F = c["nch"], c["dk"], c["F"]
        hpc = 128 // dk
        i = c["i"]
        c["i"] += 1
        stt_ = {"i": i, "KO": c["KO"][i % 2], "qd": c["qd"][i % 2], "qh": c["qh"][i % 2],
                "khat": c["khat"][i % 2], "etot": c["etot"][i % 2]}
        KO, qd, qh, etot = stt_["KO"], stt_["qd"], stt_["qh"], stt_["etot"]
        offs = [0, 32, 96, 192]
        pk = self.pp.get()
        S.mm(pk[:, 0:F], self.cv("su"), g_tm)
        pes = []
        for ch in range(nch):
            pe_ = self.pp.get()
            S.mm(pe_[:, 0:CM_N], g_tm_slice(g_tm, ch), self.cv("cm"))
            pes.append(pe_)
        ek = c["ek"][i % 2]
        S.actf(ek[:], pk[:, 0:F], AF.Exp)
        S.tt(stt_["khat"][:], ek[:], k_tm, ALU.mult, eng=S.pool)
        for ch in range(nch):
            X = c["expE"][(i * nch + ch) % 2]
            S.actf(X[:], pes[ch][:, 0:CM_N], AF.Exp)
            S.copy(etot[:, ch:ch + 1], X[:, 452:453], eng=S.pool)
            q_ = qT(ch)
            k_ = kT(ch)
            for hh in range(hpc):
                psl = slice(hh * dk, (hh + 1) * dk)
                S.stt(qd[psl, ch, hh, :], V(q_.ap[psl, :], q_.bufs), qscale, X[psl, 320:448], ALU.mult, ALU.mult)
            for I in range(4):
                n = 32 * (I + 1)
                S.tt(KO[:, ch, I, 0:n], V(k_.ap[:, 0:n], k_.bufs), X[:, offs[I]:offs[I] + n], ALU.mult)
            if qscale != 1.0:
                S.ts(X[:, 448:452], X[:, 448:452], qscale, None, ALU.mult)
            for I in range(4):
                for hh in range(hpc):
                    psl = slice(hh * dk, (hh + 1) * dk)
                    S.stt(qh[psl, ch, hh, 32 * I:32 * I + 32], V(q_.ap[psl, 32 * I:32 * I + 32], q_.bufs), X[psl, 448 + I:449 + I],
                          X[psl, 320 + 32 * I:352 + 32 * I], ALU.mult, ALU.mult)
        return stt_

    def gl_scores(self, c, stt_):
        S = self.S
        nch, dk = c["nch"], c["dk"]
        hpc = 128 // dk
        KO, qd = stt_["KO"], stt_["qd"]
        psc = self.pp.get()
        for ch in range(nch):
            for hh in range(hpc):
                h = ch * hpc + hh
                for I in range(4):
                    S.mm(psc[:, h * 128 + 32 * I:h * 128 + 32 * I + 32], KO[:, ch, I, :], qd[:, ch, hh, 32 * I:32 * I + 32])
        A = c["A"][stt_["i"] % 2]
        S.tt(A[:], V(psc.h[:].rearrange("p (h i) -> p h i", h=4), psc.bufs),
             V(self.mask_cur.h[:].unsqueeze(1).to_broadcast([128, 4, 128]), self.mask_cur.bufs), ALU.mult)
        stt_["A"] = A

    def gl_back(self, c, stt_, v_tm, gw, out_bf):
        S = self.S
        nch, dk = c["nch"], c["dk"]
        hpc = 128 // dk
        i = stt_["i"]
        A, qh, khat, etot = stt_["A"], stt_["qh"], stt_["khat"], stt_["etot"]
        Sb_prev = c["Sb"][(i + 1) % 2]
        Sb_new = c["Sb"][i % 2]
        po = self.pp.get()
        for h in range(4):
            ch, hh = divmod(h, hpc)
            S.mm(po[:, h * 128:(h + 1) * 128], A[:, h, :], v_tm(h), start=True, stop=False)
            S.mm(po[:, h * 128:(h + 1) * 128], qh[:, ch, hh, :], Sb_prev[:, ch, :], start=False, stop=True)
        pS = self.pp.get()
        for h in range(4):
            ch, hh = divmod(h, hpc)
            ps_ = slice(hh * dk, (hh + 1) * dk)
            if hpc == 1:
                S.mm(pS[:, h * 128:(h + 1) * 128], khat[:, h * 128:(h + 1) * 128], v_tm(h))
            else:
                S.mm(pS[ps_, ch * 128:(ch + 1) * 128], khat[:, h * dk:(h + 1) * dk], v_tm(h))
        for ch in range(nch):
            S.stt(c["Sf"][:, ch, :], c["Sf"][:, ch, :], etot[:, ch:ch + 1], pS[:, ch * 128:(ch + 1) * 128], ALU.mult, ALU.add)
        S.copy(Sb_new[:], c["Sf"][:], eng=S.act)
        sq = c["sq"]
        S.actf(sq[:], po[:], AF.Square)
        ssum = c["ssum"][i % 2]
        S.reduce_sum(ssum[:, 0:4], V(sq.h[:].rearrange("p (h d) -> p h d", h=4), sq.bufs))
        S.rstd(ssum[:, 0:4], ssum[:, 4:8], ssum[:, 0:4], 1.0 / 128)
        on = c["on"]
        S.tt(V(on.h[:].rearrange("p (h d) -> p h d", h=4), on.bufs), V(po.h[:].rearrange("p (h d) -> p h d", h=4), po.bufs),
             V(ssum.h[:, 0:4].unsqueeze(2).to_broadcast([128, 4, 128]), ssum.bufs), ALU.mult)
        S.tt(out_bf, on[:], gw, ALU.mult, eng=S.pool)

    def ffn_pass(self, l, x_src, x_subs):
        S = self.S
        self.S.barrier()
        self.pp.set_avail(range(8))
        with ExitStack() as st:
            w1 = S.sb([128, 8, 2 * DFF], BF16, nsub=12, stack=st)
            w2 = S.sb([128, NCHUNK_FF, D], BF16, nsub=NCHUNK_FF, stack=st)
            fw = self.din["ffn_w_in"]
            for blk in range(6):
                c0 = blk * 4
                ncol = min(4, NCHUNK_FF - c0) * 128
                for part in range(2):
                    col0 = part * DFF + c0 * 128
                    S.dma(S.pool, w1.s(blk * 2 + part)[:, :, col0:col0 + ncol],
                          V(fw.h[l, :, col0:col0 + ncol].rearrange("(k p) n -> p k n", p=128), fw.bufs))
            self.load_w(w2, self.din["ffn_w_out"], lambda k: self.din["ffn_w_out"].h[l, k * 128:(k + 1) * 128, :], NCHUNK_FF)
            cw = S.sb([128, 3, NCHUNK_FF], F32, stack=st)
            S.dma(S.sp, cw[:], self.din["convw_c"][l])
            cb = S.sb([128, NCHUNK_FF], F32, stack=st)
            S.dma(S.sp, cb[:], self.din["convb_c"][l])
            ctx = self.norm_ctx(st, nx=3)
            octx_gate = S.sb([128, D], F32, stack=st)
            S.dma(S.sp, octx_gate[:], V(self.gates.h[l * 2 + 1:l * 2 + 2, :].partition_broadcast(128), (self.gates.bufs[l * 2 + 1],)))
            hT = S.sb([128, 8, 512], BF16, stack=st)
            gT = S.sb([128, NCHUNK_FF, 512], BF16, stack=st)
            abuf = [S.sb([128, 514], F32, stack=st) for _ in range(2)]
            halo = S.sb([128, NCHUNK_FF, 2], F32, stack=st)
            S.memset(halo[:], 0.0)
            tcv = [S.sb([128, 512], F32, stack=st) for _ in range(2)]
            tsl = [S.sb([128, 512], F32, stack=st) for _ in range(2)]
            tmp = tcv
            ngroups = self.dbg.get("ngroups", NG)
            if self.dbg.get("verbose"):
                print("ffn sbuf remaining", self.nc.sbuf_bytes_remaining, flush=True)

            def ytile(g, j):
                t = g * 4 + j
                xt = ctx["xring"][ctx["xi"] % len(ctx["xring"])]
                ctx["xi"] += 1
                S.dma(S.sp, xt[:], self.xv(x_src, x_subs, t))
                for n in range(2):
                    py = self.pp.get()
                    for c in range(NCHUNK_FF):
                        S.mm(py[:], gT[:, c, j * 128:(j + 1) * 128], w2.s(c)[:, c, n * 512:(n + 1) * 512], start=(c == 0), stop=(c == NCHUNK_FF - 1))
                    S.tt(tmp[n][:], py[:], octx_gate[:, n * 512:(n + 1) * 512], ALU.mult)
                    S.tt(xt[:, n * 512:(n + 1) * 512], tmp[n][:], xt[:, n * 512:(n + 1) * 512], ALU.add, eng=S.pool)
                S.dma(S.pool, self.xv(self.xres, True, t), xt[:])

            self.norm_stage(ctx, l, 1, x_src, x_subs, 0, hT)
            for g in range(ngroups):
                for c in range(NCHUNK_FF):
                    pa = self.pp.get()
                    for k in range(8):
                        S.mm(pa[:], w1.s((c // 4) * 2)[:, k, c * 128:(c + 1) * 128], hT[:, k, :], start=(k == 0), stop=(k == 7))
                    pu = self.pp.get()
                    for k in range(8):
                        S.mm(pu[:], w1.s((c // 4) * 2 + 1)[:, k, DFF + c * 128:DFF + (c + 1) * 128], hT[:, k, :], start=(k == 0), stop=(k == 7))
                    ab = abuf[c % 2]
                    S.copy(ab[:, 0:2], halo[:, c, :], eng=S.pool)
                    S.copy(ab[:, 2:514], pa[:], eng=S.act)
                    S.copy(halo[:, c, :], ab[:, 512:514], eng=S.pool)
                    tc_ = tcv[c % 2]
                    S.ts(tc_[:], ab[:, 2:514], cw[:, 2, c:c + 1], cb[:, c:c + 1], ALU.mult, ALU.add)
                    S.stt(tc_[:], ab[:, 1:513], cw[:, 1, c:c + 1], tc_[:], ALU.mult, ALU.add)
                    S.stt(tc_[:], ab[:, 0:512], cw[:, 0, c:c + 1], tc_[:], ALU.mult, ALU.add)
                    ts_ = tsl[c % 2]
                    S.actf(ts_[:], tc_[:], AF.Silu)
                    S.tt(gT[:, c, :], ts_[:], pu[:], ALU.mult)
                nxt = g + 1 < ngroups
                if nxt:
                    self.norm_front(ctx, x_src, x_subs, g + 1)
                ytile(g, 0)
                ytile(g, 1)
                if nxt:
                    self.norm_back(ctx, l, 1, hT)
                ytile(g, 2)
                ytile(g, 3)
        self.S.barrier()

    def final_pass(self, x_src, x_subs):
        S = self.S
        self.S.barrier()
        with ExitStack() as st:
            wf = S.sb([128, D], F32, stack=st)
            S.dma(S.sp, wf[:], V(self.din["final_norm_w"].h[0:1, :].partition_broadcast(128), self.din["final_norm_w"].bufs))
            xr = [S.sb([128, D], F32, stack=st) for _ in range(3)]
            sq = S.sb([128, D], BF16, stack=st)
            ss = [S.sb([128, 4], F32, stack=st) for _ in range(2)]
            ntiles = self.dbg.get("ngroups", NG) * 4
            for t in range(ntiles):
                xt = xr[t % 3]
                S.dma(S.sp, xt[:], self.xv(x_src, x_subs, t))
                s_ = ss[t % 2]
                S.memset(s_[:, 0:1], 0.0, eng=S.pool)
                S.actf(sq[:], xt[:], AF.Square, accum_out=s_[:, 0:1])
                S.rstd(s_[:, 2:3], s_[:, 1:2], s_[:, 0:1], 1.0 / D)
                S.stt(xt[:], xt[:], s_[:, 2:3], wf[:], ALU.mult, ALU.mult)
                S.dma(S.pool, self.xv(self.out, True, t), xt[:])

    def even_pass(self, l, x_src, x_subs):
        S = self.S
        jl = l // 2
        self.S.barrier()
        self.pp.set_avail(range(8))
        with ExitStack() as st:
            win = S.sb([128, 8, EV_COLS], BF16, nsub=8, stack=st)
            wout = S.sb([128, 8, D], BF16, nsub=8, stack=st)
            self.load_w(win, self.din["ev_w_in"], lambda k: self.din["ev_w_in"].h[jl, k * 128:(k + 1) * 128, :], 8)
            self.load_w(wout, self.din["ev_w_out"], lambda k: self.din["ev_w_out"].h[jl, k * 128:(k + 1) * 128, :], 8)
            gwx = S.sb([32, 256], BF16, stack=st)
            S.dma(S.pool, gwx[:], self.din["gatew_ext"][jl])
            normw = S.sb([128, 128], F32, stack=st)
            S.dma(S.sp, normw[:], V(self.din["gla_norm_w"].h[jl:jl + 1, :].partition_broadcast(128), self.din["gla_norm_w"].bufs))
            esink = S.sb([128, 8], F32, stack=st)
            S.dma(S.sp, esink[:], V(self.din["swa_sinks"].h[jl:jl + 1, :].partition_broadcast(128), self.din["swa_sinks"].bufs))
            S.actf(esink[:], esink[:], AF.Exp)
            ctx = self.norm_ctx(st, nx=3)
            octx = self.out_ctx(st, l, 0)
            rctx = self.rope_ctx(st)
            glc = self.gl_ctx(st, 2, 64)
            hT = S.sb([128, 8, 512], BF16, stack=st)
            qT = S.sb([128, 2, 512], F32, stack=st)
            kT = S.sb([128, 2, 512], F32, stack=st)
            glrT = S.sb([32, 512], BF16, stack=st)
            S.memset(glrT[:], 1.0)
            sqT = [S.sb([128, 8, 512], BF16, stack=st) for _ in range(2)]
            for q__ in sqT:
                S.memset(q__[:], 0.0)
            skT = [S.sb([128, 2, 512], BF16, stack=st) for _ in range(2)]
            vext = [S.sb([128, 2, 65], BF16, stack=st) for _ in range(3)]
            for v_ in vext:
                S.memset(v_[:], 1.0)
            k_tm = [S.sb([128, 256], F32, stack=st) for _ in range(2)]
            v_tm = [S.sb([128, 512], BF16, stack=st) for _ in range(2)]
            gwr = [S.sb([128, 512], F32, stack=st) for _ in range(5)]
            g_tm = [S.sb([128, 256], F32, stack=st) for _ in range(2)]
            ez = [S.sb([128, 256], F32, stack=st) for _ in range(2)]
            gsl = [S.sb([128, 512], F32, stack=st) for _ in range(2)]
            ocat = [S.sb([128, D], BF16, stack=st) for _ in range(2)]
            PT = [S.sb([128, 4, 128], BF16, stack=st) for _ in range(4)]
            den = [S.sb([128, 8], F32, stack=st) for _ in range(2)]
            ngroups = self.dbg.get("ngroups", NG)
            ntiles = ngroups * 4
            tst = {}

            fronted = set()

            def group_front(g):
                fronted.add(g)
                self.rope_load(rctx, g)
                self.norm_front(ctx, x_src, x_subs, g)

            def group_stage(g):
                if g not in fronted:
                    group_front(g)
                self.norm_back(ctx, l, 0, hT)
                sq_ = sqT[g % 2]

                def fm(col0, m, dst_fn):
                    pb = self.pp.get()
                    for k in range(8):
                        S.mm(pb[0:m, :], win.s(k)[:, k, col0:col0 + m], hT[:, k, :], start=(k == 0), stop=(k == 7))
                    dst_fn(pb)
                for ch in range(2):
                    fm(ch * 128, 128, lambda pb, ch=ch: S.actf(qT[:, ch, :], pb[:], AF.Copy, scale=0.125))
                    fm(256 + ch * 128, 128, lambda pb, ch=ch: S.copy(kT[:, ch, :], pb[:], eng=S.act))
                fm(1536, 16, lambda pb: S.copy(glrT[0:16, :], pb[0:16, :], eng=S.act))
                for ch in range(4):
                    fm(1552 + ch * 128, 128, lambda pb, ch=ch: self.rope_stage(rctx, pb[:], [(slice(0, 64), sq_[0:64, 2 * ch, :]), (slice(64, 128), sq_[64:128, 2 * ch + 1, :])], g))
                skt = skT[g % 2]
                for kv in range(2):
                    pb = self.pp.get()
                    for half in range(2):
                        for k in range(8):
                            S.mm(pb[half * 64:(half + 1) * 64, :], win.s(k)[:, k, 2064 + kv * 64:2064 + (kv + 1) * 64], hT[:, k, :], start=(k == 0), stop=(k == 7))
                    self.rope_stage(rctx, pb[:], skt[:, kv, :], g)
                self.rope_flush(rctx)
                for j in range(4):
                    pb = self.pp.get()
                    for k in range(8):
                        S.mm(pb[:], hT[:, k, j * 128:(j + 1) * 128], win.s(k)[:, k, 1024:1536], start=(k == 0), stop=(k == 7))
                    gs_ = gsl[j % 2]
                    gw_ = gwr[(g * 4 + j) % 5]
                    S.actf(gs_[:], pb[:], AF.Silu)
                    S.tt(V(gw_.h[:].rearrange("p (h d) -> p h d", h=4), gw_.bufs), V(gs_.h[:].rearrange("p (h d) -> p h d", h=4), gs_.bufs),
                         V(normw.h[:].unsqueeze(1).to_broadcast([128, 4, 128]), normw.bufs), ALU.mult, eng=S.pool)

            def stage_P(t):
                g, j = divmod(t, 4)
                tsl = slice(j * 128, (j + 1) * 128)

                def tm(col0, n, dst_fn):
                    pb = self.pp.get()
                    for k in range(8):
                        S.mm(pb[:, 0:n], hT[:, k, tsl], win.s(k)[:, k, col0:col0 + n], start=(k == 0), stop=(k == 7))
                    dst_fn(pb)
                ktm = k_tm[t % 2]
                vtm = v_tm[t % 2]
                gw_ = gwr[(g * 4 + j) % 5]
                vx = vext[t % 3]
                pz = self.pp.get()
                S.mm(pz[:, 0:256], glrT[:, tsl], gwx[:])
                ez_ = ez[t % 2]
                S.actf(ez_[:], pz[:, 0:256], AF.Exp, scale=-1.0)
                S.actf(ez_[:], ez_[:], AF.Ln, bias=1.0)
                gtm = g_tm[t % 2]
                S.actf(gtm[:], ez_[:], AF.Identity, scale=-1.0 / 16.0)
                tm(256, 256, lambda pb: S.copy(ktm[:], pb[:, 0:256], eng=S.act))
                tm(512, 512, lambda pb: S.copy(vtm[:], pb[:], eng=S.act))

                tm(2192, 128, lambda pb: S.copy(vx[:, :, 0:64], V(pb.h[:, 0:128].rearrange("p (g d) -> p g d", g=2), pb.bufs), eng=S.act))
                tst[t] = {"tsl": tsl, "ktm": ktm, "vtm": vtm, "gw": gw_, "gtm": gtm, "vx": vx, "oc": ocat[t % 2]}

            def stage_G1(t):
                d = tst[t]
                tsl = d["tsl"]
                d["gl"] = self.gl_front(glc, lambda ch: qT[:, ch, tsl], lambda ch: kT[:, ch, tsl], d["gtm"][:], d["ktm"][:], 1.0)

            def stage_G2a(t):
                self.gl_scores(glc, tst[t]["gl"])

            def stage_G2b(t):
                d = tst[t]
                vtm = d["vtm"]
                self.gl_back(glc, d["gl"], lambda h: vtm[:, h * 128:(h + 1) * 128], d["gw"][:], d["oc"][:, 0:512])

            def stage_Wa(t):
                d = tst[t]
                g, j = divmod(t, 4)
                tsl = d["tsl"]
                skt = skT[g % 2]
                sq_ = sqT[g % 2]
                if j > 0:
                    prev_k = (skt, slice((j - 1) * 128, j * 128))
                elif g > 0:
                    prev_k = (skT[(g - 1) % 2], slice(384, 512))
                else:
                    prev_k = None
                vprev = vext[(t - 1) % 3]
                d["pts"] = []
                pti = 0
                for kv in range(2):
                    blocks = []
                    if prev_k is not None:
                        blocks.append((prev_k[0], prev_k[1], self.mb_prev, vprev))
                    blocks.append((skt, tsl, self.mb_cur, d["vx"]))
                    pts = []
                    for (kt_, ks_, msk, vv) in blocks:
                        pss = self.pp.get()
                        S.mm(V(pss.h[:].rearrange("p (h i) -> p h i", h=4), pss.bufs), self.ident[:],
                             V(msk.h[:].unsqueeze(1).to_broadcast([128, 4, 128]), msk.bufs), start=True, stop=False)
                        for r in range(4):
                            h = kv * 4 + r
                            S.mm(pss[:, r * 128:(r + 1) * 128], kt_[:, kv, ks_], sq_[:, h, tsl], start=False, stop=(r == 3))
                        pt = PT[pti % 4]
                        pti += 1
                        S.actf(pt[:], V(pss.h[:].rearrange("p (h i) -> p h i", h=4), pss.bufs), AF.Exp, scale=0.125)
                        pts.append((pt, vv))
                    d["pts"].append(pts)

            def stage_Wb(t):
                d = tst[t]
                oc = d["oc"]
                for kv in range(2):
                    pts = d["pts"][kv]
                    po = self.pp.get()
                    for r in range(4):
                        for bi, (pt, vv) in enumerate(pts):
                            S.mm(po[:, r * 65:(r + 1) * 65], pt[:, r, :], vv[:, kv, :], start=(bi == 0), stop=(bi == len(pts) - 1))
                    dn = den[t % 2]
                    pov = po.h[:, 0:260].rearrange("p (h d) -> p h d", h=4)
                    S.tt(dn[:, kv * 4:(kv + 1) * 4], V(pov[:, :, 64], po.bufs), esink[:, kv * 4:(kv + 1) * 4], ALU.add)
                    S.recip(dn[:, kv * 4:(kv + 1) * 4], dn[:, kv * 4:(kv + 1) * 4])
                    S.tt(V(oc.h[:, 512 + kv * 256:512 + (kv + 1) * 256].rearrange("p (h d) -> p h d", h=4), oc.bufs),
                         V(pov[:, :, 0:64], po.bufs),
                         V(dn.h[:, kv * 4:(kv + 1) * 4].unsqueeze(2).to_broadcast([128, 4, 64]), dn.bufs), ALU.mult)

            def stage_Oa(t):
                self.out_stage_a(octx, t, tst[t]["oc"])

            def stage_Ob(t):
                self.out_stage_b(ctx, octx, _WSub(wout), x_src, x_subs, t, self.xres)
                del tst[t]

            if self.dbg.get("verbose"):
                print("sbuf remaining", self.nc.sbuf_bytes_remaining, flush=True)
            group_stage(0)
            stage_P(0)
            stage_G1(0)
            for t in range(ntiles):
                if t + 2 < ntiles and (t + 2) % 4 == 0:
                    group_front((t + 2) // 4)
                if t + 1 < ntiles:
                    if (t + 1) % 4 == 0:
                        group_stage((t + 1) // 4)
                    stage_P(t + 1)
                stage_G2a(t)
                if t >= 1:
                    stage_Oa(t - 1)
                stage_Wa(t)
                if t + 1 < ntiles:
                    stage_G1(t + 1)
                stage_G2b(t)
                stage_Wb(t)
                if t >= 1:
                    stage_Ob(t - 1)
            stage_Oa(ntiles - 1)
            stage_Ob(ntiles - 1)
        self.S.barrier()

    def odd_pass_a(self, l, x_src, x_subs):
        S = self.S
        jl = l // 2
        lam_init = 0.8 - 0.6 * math.exp(-0.3 * l)
        self.S.barrier()
        self.pp.set_avail(range(6, 8))
        accsets = [[self.pp.banks[0], self.pp.banks[1]], [self.pp.banks[2], self.pp.banks[3]]]
        stb = [self.pp.banks[4], self.pp.banks[5]]
        with ExitStack() as st:
            win = S.sb([128, 8, 1536], BF16, nsub=8, stack=st)
            self.load_w(win, self.din["od_w_in"], lambda k: self.din["od_w_in"].h[jl, k * 128:(k + 1) * 128, 0:1536], 8)
            normw = S.sb([128, 128], F32, stack=st)
            S.dma(S.sp, normw[:], V(self.din["diff_norm_w"].h[jl:jl + 1, :].partition_broadcast(128), self.din["diff_norm_w"].bufs))
            S.ts(normw[:], normw[:], 1.0 - lam_init, None, ALU.mult)
            lv = S.sb([128, 256], F32, stack=st)
            S.dma(S.sp, lv[:], V(self.din["diff_lambda"].h[jl:jl + 1, :].partition_broadcast(128), self.din["diff_lambda"].bufs))
            lp = S.sb([128, 128], F32, stack=st)
            lv4 = lv.h[:].rearrange("p (a b d) -> p a b d", a=2, b=2)
            S.tt(V(lp.h[:].rearrange("p (a d) -> p a d", a=2), lp.bufs), V(lv4[:, :, 0, :], lv.bufs), V(lv4[:, :, 1, :], lv.bufs), ALU.mult)
            lsum = S.sb([128, 4], F32, stack=st)
            S.reduce_sum(lsum[:, 0:2], V(lp.h[:].rearrange("p (a d) -> p a d", a=2), lp.bufs))
            S.actf(lsum[:, 2:4], lsum[:, 0:2], AF.Exp)
            nlam = S.sb([128, 1], F32, stack=st)
            S.tt(nlam[:], lsum[:, 3:4], lsum[:, 2:3], ALU.subtract)
            S.ts(nlam[:], nlam[:], -lam_init, None, ALU.add)
            ctx = self.norm_ctx(st)
            rctx = self.rope_ctx(st)
            hT = S.sb([128, 8, 512], BF16, stack=st)
            KT = S.sb([128, 4, SEQ], BF16, nsub=NG, stack=st)
            VX = S.sb([128, NT, 4, 129], BF16, nsub=NG, stack=st)
            for g in range(NG):
                S.memset(VX.s(g)[:, g * 4:(g + 1) * 4, :, :], 1.0, eng=S.pool)
            QT = [S.sb([128, 4, 2, 512], BF16, stack=st) for _ in range(2)]
            for q_ in QT:
                S.memset(q_[:], 0.0)
            PT = [S.sb([128, 512], BF16, stack=st) for _ in range(3)]
            o1 = S.sb([128, 4, 128], F32, stack=st)
            od = [S.sb([128, 4, 128], F32, stack=st) for _ in range(4)]
            rl = [S.sb([128, 4], F32, stack=st) for _ in range(2)]
            sq = S.sb([128, 512], F32, stack=st)
            ssum = [S.sb([128, 8], F32, stack=st) for _ in range(2)]
            on = S.sb([128, 512], F32, stack=st)
            ob = [S.sb([128, 512], BF16, stack=st) for _ in range(2)]
            pti = 0
            ngroups = self.dbg.get("ngroups", NG)
            fronted = set()

            def group_front(g):
                fronted.add(g)
                self.rope_load(rctx, g)
                self.norm_front(ctx, x_src, x_subs, g)

            def group_stage(g):
                if g not in fronted:
                    group_front(g)
                self.norm_back(ctx, l, 0, hT)
                qt = QT[g % 2]
                gsl_ = slice(g * 512, (g + 1) * 512)
                for ch in range(4):
                    pb = self.pp.get()
                    for k in range(8):
                        S.mm(pb[:], win.s(k)[:, k, ch * 128:(ch + 1) * 128], hT[:, k, :], start=(k == 0), stop=(k == 7))
                    self.rope_stage(rctx, pb[:], [(slice(0, 64), qt[0:64, ch, 0, :]), (slice(64, 128), qt[64:128, ch, 1, :])], g)
                    pb = self.pp.get()
                    for k in range(8):
                        S.mm(pb[:], win.s(k)[:, k, 512 + ch * 128:512 + (ch + 1) * 128], hT[:, k, :], start=(k == 0), stop=(k == 7))
                    self.rope_stage(rctx, pb[:], KT.s(g)[:, ch, gsl_], g)
                self.rope_flush(rctx)
                for j in range(4):
                    t = g * 4 + j
                    pb = self.pp.get()
                    for k in range(8):
                        S.mm(pb[:], hT[:, k, j * 128:(j + 1) * 128], win.s(k)[:, k, 1024:1536], start=(k == 0), stop=(k == 7))
                    S.copy(VX.s(g)[:, t, :, 0:128], V(pb.h[:].rearrange("p (h d) -> p h d", h=4), pb.bufs), eng=S.act)

            group_stage(0)
            for g in range(ngroups):
                qt = QT[g % 2]
                nkb = 4 * g + 4
                its = [(h, m, kb) for h in range(4) for m in range(2) for kb in range(nkb)]

                def emit_st(it, idx):
                    h, m, kb = it
                    q0 = max(0, kb - 4 * g)
                    cols = slice(q0 * 128, 512)
                    pst = stb[idx % 2]
                    S.mm(pst[:, cols], KT.s(kb // 4)[:, h, kb * 128:(kb + 1) * 128], qt[:, h, m, cols])
                    return pst

                pst_next = emit_st(its[0], pti)
                for ii, (h, m, kb) in enumerate(its):
                    if ii == nkb and g + 1 < ngroups:
                        group_front(g + 1)
                    if ii == 2 * nkb and g + 1 < ngroups:
                        group_stage(g + 1)
                    pst = pst_next
                    pt = PT[pti % 3]
                    if ii + 1 < len(its):
                        pst_next = emit_st(its[ii + 1], pti + 1)
                    pti += 1
                    q0 = max(0, kb - 4 * g)
                    cols = slice(q0 * 128, 512)
                    kg = kb // 4
                    accs = accsets[(h * 2 + m) % 2]
                    S.actf(pt[:, cols], pst[:, cols], AF.Exp, scale=0.125)
                    if kb >= 4 * g:
                        dsl = slice(q0 * 128, (q0 + 1) * 128)
                        S.tt(pt[:, dsl], pt[:, dsl], self.mask_cur[:], ALU.mult, eng=S.pool)
                    for qb in range(q0, 4):
                        acc = accs[qb // 2]
                        o_ = (qb % 2) * 129
                        S.mm(acc[:, o_:o_ + 129], pt[:, qb * 128:(qb + 1) * 128], VX.s(kg)[:, kb, h, :],
                             start=(kb == 0 and qb % 2 == 0), stop=(kb == 4 * g + qb and qb % 2 == 1))
                    if kb == nkb - 1:
                        r_ = rl[m]
                        for qb in range(4):
                            acc = accs[qb // 2]
                            o_ = (qb % 2) * 129
                            S.recip(r_[:, qb:qb + 1], acc[:, o_ + 128:o_ + 129])
                            if m == 0:
                                S.ts(o1[:, qb, :], acc[:, o_:o_ + 128], r_[:, qb:qb + 1], None, ALU.mult)
                            else:
                                S.ts(r_[:, qb:qb + 1], r_[:, qb:qb + 1], nlam[:, 0:1], None, ALU.mult)
                                S.stt(od[qb][:, h, :], acc[:, o_:o_ + 128], r_[:, qb:qb + 1], o1[:, qb, :], ALU.mult, ALU.add)
                for qb in range(4):
                    t = g * 4 + qb
                    odv = V(od[qb].h[:].rearrange("p h d -> p (h d)"), od[qb].bufs)
                    S.actf(sq[:], odv, AF.Square)
                    ss_ = ssum[t % 2]
                    S.reduce_sum(ss_[:, 0:4], V(sq.h[:].rearrange("p (h d) -> p h d", h=4), sq.bufs))
                    S.rstd(ss_[:, 0:4], ss_[:, 4:8], ss_[:, 0:4], 1.0 / 128)
                    S.tt(V(on.h[:].rearrange("p (h d) -> p h d", h=4), on.bufs), od[qb][:],
                         V(ss_.h[:, 0:4].unsqueeze(2).to_broadcast([128, 4, 128]), ss_.bufs), ALU.mult)
                    ob_ = ob[t % 2]
                    S.tt(V(ob_.h[:].rearrange("p (h d) -> p h d", h=4), ob_.bufs), V(on.h[:].rearrange("p (h d) -> p h d", h=4), on.bufs),
                         V(normw.h[:].unsqueeze(1).to_broadcast([128, 4, 128]), normw.bufs), ALU.mult, eng=S.pool)
                    S.dma(S.pool, self.odiff.s(t)[t * 128:(t + 1) * 128, :], ob_[:])
        self.S.barrier()
        self.pp.set_avail(range(8))

    def odd_pass_b(self, l, x_src, x_subs):
        S = self.S
        jl = l // 2
        self.S.barrier()
        self.pp.set_avail(range(8))
        with ExitStack() as st:
            win = S.sb([128, 8, 2048], BF16, nsub=8, stack=st)
            wout = S.sb([128, 8, D], BF16, nsub=8, stack=st)
            self.load_w(win, self.din["od_w_in"], lambda k: self.din["od_w_in"].h[jl, k * 128:(k + 1) * 128, 1536:3584], 8)
            self.load_w(wout, self.din["od_w_out"], lambda k: self.din["od_w_out"].h[jl, k * 128:(k + 1) * 128, :], 8)
            normw = S.sb([128, 128], F32, stack=st)
            S.dma(S.sp, normw[:], V(self.din["hgrn_norm_w"].h[jl:jl + 1, :].partition_broadcast(128), self.din["hgrn_norm_w"].bufs))
            lbr = S.sb([128, 512], F32, stack=st)
            omlr = S.sb([128, 512], F32, stack=st)
            if jl == 0:
                S.memset(lbr[:], 0.0)
            else:
                l0 = S.sb([128, 512], F32, stack=st)
                S.dma(S.sp, l0[:], V(self.din["lb_r"].h[0:1, :].partition_broadcast(128), self.din["lb_r"].bufs))
                S.dma(S.sp, lbr[:], V(self.din["lb_r"].h[1:2, :].partition_broadcast(128), self.din["lb_r"].bufs))
                S.tt(lbr[:], lbr[:], l0[:], ALU.subtract)
                S.actf(lbr[:], lbr[:], AF.Sigmoid)
            S.ts(omlr[:], lbr[:], -1.0, 1.0, ALU.mult, ALU.add)
            ctx = self.norm_ctx(st)
            octx = self.out_ctx(st, l, 0)
            glc = self.gl_ctx(st, 4, 128)
            hT = S.sb([128, 8, 512], BF16, stack=st)
            qT = S.sb([128, 4, 512], F32, stack=st)
            kT = S.sb([128, 4, 512], F32, stack=st)
            sgT = [S.sb([128, 512], F32, stack=st) for _ in range(2)]
            k_tm = [S.sb([128, 512], F32, stack=st) for _ in range(2)]
            f_tm = [S.sb([128, 512], F32, stack=st) for _ in range(2)]
            v_tm = [S.sb([128, 512], BF16, stack=st) for _ in range(2)]
            gwr = [S.sb([128, 512], F32, stack=st) for _ in range(5)]
            b_tm = [S.sb([128, 512], F32, stack=st) for _ in range(2)]
            g_tm = [S.sb([128, 512], F32, stack=st) for _ in range(2)]
            gsl = [S.sb([128, 512], F32, stack=st) for _ in range(2)]
            ocat = [S.sb([128, D], BF16, stack=st) for _ in range(3)]
            ngroups = self.dbg.get("ngroups", NG)
            ntiles = ngroups * 4
            tst = {}

            fronted = set()

            def group_front(g):
                fronted.add(g)
                self.norm_front(ctx, x_src, x_subs, g)

            def group_stage(g):
                if g not in fronted:
                    group_front(g)
                self.norm_back(ctx, l, 0, hT)
                for ch in range(4):
                    pb = self.pp.get()
                    for k in range(8):
                        S.mm(pb[:], win.s(k)[:, k, ch * 128:(ch + 1) * 128], hT[:, k, :], start=(k == 0), stop=(k == 7))
                    S.actf(qT[:, ch, :], pb[:], AF.Silu)
                    pb = self.pp.get()
                    for k in range(8):
                        S.mm(pb[:], win.s(k)[:, k, 512 + ch * 128:512 + (ch + 1) * 128], hT[:, k, :], start=(k == 0), stop=(k == 7))
                    sg = sgT[ch % 2]
                    S.actf(sg[:], pb[:], AF.Sigmoid)
                    S.ts(sg[:], sg[:], -1.0, 1.0, ALU.mult, ALU.add)
                    S.actf(kT[:, ch, :], sg[:], AF.Identity, scale=self.omlc[:, jl, ch:ch + 1])
                for j in range(4):
                    pb = self.pp.get()
                    for k in range(8):
                        S.mm(pb[:], hT[:, k, j * 128:(j + 1) * 128], win.s(k)[:, k, 1536:2048], start=(k == 0), stop=(k == 7))
                    gs_ = gsl[j % 2]
                    gw_ = gwr[(g * 4 + j) % 5]
                    S.actf(gs_[:], pb[:], AF.Silu)
                    S.tt(V(gw_.h[:].rearrange("p (h d) -> p h d", h=4), gw_.bufs), V(gs_.h[:].rearrange("p (h d) -> p h d", h=4), gs_.bufs),
                         V(normw.h[:].unsqueeze(1).to_broadcast([128, 4, 128]), normw.bufs), ALU.mult, eng=S.pool)

            def stage_P(t):
                g, j = divmod(t, 4)
                tsl = slice(j * 128, (j + 1) * 128)

                def tm(col0, n, dst_fn):
                    pb = self.pp.get()
                    for k in range(8):
                        S.mm(pb[:, 0:n], hT[:, k, tsl], win.s(k)[:, k, col0:col0 + n], start=(k == 0), stop=(k == 7))
                    dst_fn(pb)
                ktm = k_tm[t % 2]
                ftm = f_tm[t % 2]
                gtm = g_tm[t % 2]
                vtm = v_tm[t % 2]
                gw_ = gwr[(g * 4 + j) % 5]
                btm = b_tm[t % 2]

                def fgate(pb):
                    S.actf(ftm[:], pb[:], AF.Exp, scale=-1.0)
                    S.actf(btm[:], ftm[:], AF.Ln, bias=1.0)
                    if jl == 0:
                        S.actf(gtm[:], btm[:], AF.Identity, scale=-1.0)
                        S.actf(ktm[:], btm[:], AF.Exp, scale=-1.0)
                        S.ts(ktm[:], ktm[:], -1.0, 1.0, ALU.mult, ALU.add, eng=S.pool)
                    else:
                        S.tt(gtm[:], ftm[:], lbr[:], ALU.mult)
                        S.actf(ktm[:], btm[:], AF.Exp, scale=-1.0)
                        S.actf(gtm[:], gtm[:], AF.Ln, bias=1.0)
                        S.tt(ktm[:], ktm[:], ftm[:], ALU.mult, eng=S.pool)
                        S.tt(gtm[:], gtm[:], btm[:], ALU.subtract)
                        S.tt(ktm[:], ktm[:], omlr[:], ALU.mult, eng=S.pool)
                tm(512, 512, fgate)
                tm(1024, 512, lambda pb: S.copy(vtm[:], pb[:], eng=S.act))

                oc = ocat[t % 3]
                S.dma(S.sp, oc[:, 0:512], self.odiff.s(t)[t * 128:(t + 1) * 128, :])
                tst[t] = {"tsl": tsl, "ktm": ktm, "vtm": vtm, "gw": gw_, "gtm": gtm, "oc": oc}

            def stage_G1(t):
                d = tst[t]
                tsl = d["tsl"]
                d["gl"] = self.gl_front(glc, lambda ch: qT[:, ch, tsl], lambda ch: kT[:, ch, tsl], d["gtm"][:], d["ktm"][:], 128.0 ** -0.5)

            def stage_G2a(t):
                self.gl_scores(glc, tst[t]["gl"])

            def stage_G2b(t):
                d = tst[t]
                vtm = d["vtm"]
                self.gl_back(glc, d["gl"], lambda h: vtm[:, h * 128:(h + 1) * 128], d["gw"][:], d["oc"][:, 512:1024])

            def stage_Oa(t):
                self.out_stage_a(octx, t, tst[t]["oc"])

            def stage_Ob(t):
                self.out_stage_b(ctx, octx, _WSub(wout), x_src, x_subs, t, self.xres)
                del tst[t]

            if self.dbg.get("verbose"):
                print("sbuf remaining", self.nc.sbuf_bytes_remaining, flush=True)
            group_stage(0)
            stage_P(0)
            stage_G1(0)
            for t in range(ntiles):
                if t + 2 < ntiles and (t + 2) % 4 == 0:
                    group_front((t + 2) // 4)
                if t + 1 < ntiles:
                    if (t + 1) % 4 == 0:
                        group_stage((t + 1) // 4)
                    stage_P(t + 1)
                stage_G2a(t)
                if t >= 1:
                    stage_Oa(t - 1)
                if t + 1 < ntiles:
                    stage_G1(t + 1)
                stage_G2b(t)
                if t >= 1:
                    stage_Ob(t - 1)
            stage_Oa(ntiles - 1)
            stage_Ob(ntiles - 1)
        self.S.barrier()


class _WSub:
    def __init__(self, t):
        self.t = t

    def __getitem__(self, idx):
        k = idx[1]
        return V(self.t.h[idx], (self.t.bufs[k],))


def g_tm_slice(g_tm, ch):
    return V(g_tm.ap[:, ch * 128:(ch + 1) * 128], g_tm.bufs)


def qd_ap(q_, I):
    return q_.ap[:, 32 * I:32 * I + 32]


_CACHE = {}


def _col(v, n):
    return np.ascontiguousarray(np.asarray(v).reshape(n, 128).T)


def make_in_maps(inputs, ncores=8):
    f = lambda a: np.ascontiguousarray(np.asarray(a, dtype=np.float32))
    x = f(inputs["x"])
    c = f(inputs["c"])
    pos = np.ascontiguousarray(np.asarray(inputs["positions"], dtype=np.int32))
    mod_b = f(inputs["mod_b"])
    gate_w = f(inputs["gla_gate_w"])
    gate_b = f(inputs["gla_gate_b"])
    gatew_ext = np.zeros((2, 32, 256), np.float32)
    gatew_ext[:, 0:16, :] = gate_w
    gatew_ext[:, 16, :] = gate_b
    lb = f(inputs["hgrn_lb_logits"])
    shared = {
        "consts": CONSTS,
        "mod_w": f(inputs["mod_w"]),
        "mod_bc": np.stack([_col(mod_b[l], 48) for l in range(DEPTH)]),
        "mod_b": mod_b,
        "nmw_c": np.stack([_col(f(inputs["norm_mix_w"])[l], 8) for l in range(DEPTH)]),
        "nfw_c": np.stack([_col(f(inputs["norm_ffn_w"])[l], 8) for l in range(DEPTH)]),
        "ev_w_in": f(inputs["ev_w_in"]),
        "gatew_ext": gatew_ext,
        "gla_norm_w": f(inputs["gla_norm_w"]),
        "swa_sinks": f(inputs["swa_sinks"]),
        "ev_w_out": f(inputs["ev_w_out"]),
        "od_w_in": f(inputs["od_w_in"]),
        "diff_lambda": f(inputs["diff_lambda"]).reshape(2, 256),
        "diff_norm_w": f(inputs["diff_norm_w"]),
        "lb_c": np.ascontiguousarray(np.stack([_col(lb[j], 4) for j in range(2)], axis=1)),
        "lb_r": lb,
        "hgrn_norm_w": f(inputs["hgrn_norm_w"]),
        "od_w_out": f(inputs["od_w_out"]),
        "ffn_w_in": f(inputs["ffn_w_in"]),
        "convw_c": np.ascontiguousarray(np.stack([np.stack([_col(f(inputs["ffn_conv_w"])[l, j], NCHUNK_FF) for j in range(3)], axis=1) for l in range(DEPTH)])),
        "convb_c": np.stack([_col(f(inputs["ffn_conv_b"])[l], NCHUNK_FF) for l in range(DEPTH)]),
        "ffn_w_out": f(inputs["ffn_w_out"]),
        "final_norm_w": f(inputs["final_norm_w"]).reshape(1, D),
    }
    maps = []
    for b in range(ncores):
        m = dict(shared)
        m["x"] = x[b]
        m["c_t"] = _col(c[b], 8)
        m["pos"] = pos[b].reshape(1, SEQ)
        maps.append(m)
    return maps


def kernel(**inputs):
    if "prog" not in _CACHE:
        _CACHE["prog"] = Prog()
    prog = _CACHE["prog"]
    maps = make_in_maps(inputs, 8)
    res = run_bass_kernel_spmd(prog.nc, maps, core_ids=list(range(8)))
    return np.stack([np.asarray(r["out"], dtype=np.float32) for r in res.results], axis=0)
```

```python
import math
from contextlib import ExitStack
import numpy as np
import concourse.bass as bass
import concourse.mybir as mybir
from concourse.bass_utils import run_bass_kernel_spmd

F32 = mybir.dt.float32
BF16 = mybir.dt.bfloat16
I32 = mybir.dt.int32
ALU = mybir.AluOpType
AF = mybir.ActivationFunctionType
AX = mybir.AxisListType

D = 1024
SEQ = 4096
DEPTH = 4
DFF = 2816
NT = SEQ // 128
NG = SEQ // 512
EPS = 1e-6
EV_COLS = 2320
OD_COLS = 3584
NCHUNK_FF = DFF // 128
CM_N = 453
TWO_PI = 2.0 * math.pi
EPOCH = 30000


class Buf:
    __slots__ = ("w", "r", "strict")

    def __init__(self):
        self.w = None
        self.r = {}
        self.strict = False


class V:
    __slots__ = ("ap", "bufs")

    def __init__(self, ap, bufs):
        self.ap = ap
        self.bufs = bufs


class _Sub:
    def __init__(self, t, bufs):
        self.t = t
        self.bufs = bufs

    def __getitem__(self, idx):
        return V(self.t.h[idx], self.bufs)


class T:
    def __init__(self, h, nsub=1):
        self.h = h
        self.bufs = tuple(Buf() for _ in range(nsub))

    def __getitem__(self, idx):
        return V(self.h[idx], self.bufs)

    def s(self, i):
        return _Sub(self, (self.bufs[i],))

    def ss(self, idxs):
        return _Sub(self, tuple(self.bufs[i] for i in idxs))

    def v(self, ap, subs=None):
        return V(ap, self.bufs if subs is None else tuple(self.bufs[i] for i in subs))


class Eng:
    def __init__(self, S, name, h):
        self.S = S
        self.name = name
        self.h = h
        self.known = {}
        self.sem_id = S.new_sem(name)
        self.count = 0


class Sched:
    def __init__(self, nc, stack):
        self.nc = nc
        self.stack = stack
        self.sems = []
        self.pe = Eng(self, "pe", nc.tensor)
        self.act = Eng(self, "act", nc.scalar)
        self.dve = Eng(self, "dve", nc.vector)
        self.pool = Eng(self, "pool", nc.gpsimd)
        self.sp = Eng(self, "sp", nc.sync)
        self.engs = [self.pe, self.act, self.dve, self.pool, self.sp]
        self.dma_K = 8
        self.rings = {}
        for e in (self.sp, self.pool):
            self.rings[e.name] = {"ring": [{"sem": self.new_sem("dma" + e.name), "n": 0} for _ in range(self.dma_K)], "i": 0}
        self.nt = 0
        self.ninst = 0

    def new_sem(self, name):
        h = self.stack.enter_context(self.nc.semaphore("s%d_%s" % (len(self.sems), name)))
        self.sems.append(h)
        return len(self.sems) - 1

    def sb(self, shape, dtype, nsub=1, stack=None):
        self.nt += 1
        h = (stack or self.stack).enter_context(self.nc.sbuf_tensor("t%d" % self.nt, list(shape), dtype))
        return T(h, nsub)

    def ps(self, shape, dtype, stack=None):
        self.nt += 1
        h = (stack or self.stack).enter_context(self.nc.psum_tensor("p%d" % self.nt, list(shape), dtype))
        return T(h)

    def _need(self, eng, ev, waits):
        sid, val = ev
        if eng.known.get(sid, 0) >= val:
            return
        if waits.get(sid, 0) < val:
            waits[sid] = val

    def _collect(self, eng, reads, writes, my_sid):
        waits = {}
        for v in reads:
            for b in v.bufs:
                if b.w is not None and not (eng is self.pe and b.w[0] == my_sid):
                    self._need(eng, b.w, waits)
        pool_strict = eng is self.pool
        for v in writes:
            for b in v.bufs:
                strict = pool_strict or b.strict
                if b.w is not None and (b.w[0] != my_sid or strict):
                    self._need(eng, b.w, waits)
                for sid, val in b.r.items():
                    if sid != my_sid or strict:
                        self._need(eng, (sid, val), waits)
        return waits

    def _emit_waits(self, eng, waits):
        if eng.sem_id in waits:
            waits[eng.sem_id] = max(waits[eng.sem_id], eng.count - 3)
        for sid, val in waits.items():
            eng.h.wait_ge(self.sems[sid], val)
            eng.known[sid] = val
            self.ninst += 1

    def _record(self, ev, reads, writes):
        sid, val = ev
        for v in reads:
            for b in v.bufs:
                if b.r.get(sid, 0) < val:
                    b.r[sid] = val
        for v in writes:
            for b in v.bufs:
                b.w = ev
                b.r = {}

    def op(self, eng, fn, reads, writes):
        if eng.count >= EPOCH:
            eng.sem_id = self.new_sem(eng.name)
            eng.count = 0
        waits = self._collect(eng, reads, writes, eng.sem_id)
        self._emit_waits(eng, waits)
        ins = fn()
        eng.count += 1
        self.ninst += 1
        ins.then_inc(self.sems[eng.sem_id], 1)
        ev = (eng.sem_id, eng.count)
        self._record(ev, reads, writes)
        return ev

    def dma(self, eng, out, in_):
        rs = self.rings[eng.name]
        slot = rs["ring"][rs["i"] % self.dma_K]
        rs["i"] += 1
        sid = slot["sem"]
        waits = self._collect(eng, [in_], [out], -1)
        if slot["n"] > 0:
            self._need(eng, (sid, 16 * slot["n"]), waits)
        self._emit_waits(eng, waits)
        ins = eng.h.dma_start(out=out.ap, in_=in_.ap)
        slot["n"] += 1
        ins.then_inc(self.sems[sid], 16)
        self.ninst += 1
        ev = (sid, 16 * slot["n"])
        self._record(ev, [in_], [out])
        return ev

    def wait_all(self, eng):
        waits = {}
        for e in self.engs:
            if e.count > 0 and e is not eng:
                self._need(eng, (e.sem_id, e.count), waits)
        for rs in self.rings.values():
            for slot in rs["ring"]:
                if slot["n"] > 0:
                    self._need(eng, (slot["sem"], 16 * slot["n"]), waits)
        self._emit_waits(eng, waits)

    def barrier(self):
        for e in self.engs:
            self.wait_all(e)

    def mm(self, out, lhsT, rhs, start=True, stop=True):
        return self.op(self.pe, lambda: self.nc.tensor.matmul(out.ap, lhsT.ap, rhs.ap, start=start, stop=stop),
                       [lhsT, rhs], [out])

    def transpose(self, out, in_, ident):
        return self.op(self.pe, lambda: self.nc.tensor.transpose(out.ap, in_.ap, ident.ap), [in_, ident], [out])

    def actf(self, out, in_, func, bias=None, scale=None, accum_out=None):
        reads = [in_]
        kw = {}
        if bias is not None:
            if isinstance(bias, V):
                reads.append(bias)
                kw["bias"] = bias.ap
            else:
                kw["bias"] = bias
        if scale is not None:
            if isinstance(scale, V):
                reads.append(scale)
                kw["scale"] = scale.ap
            else:
                kw["scale"] = scale
        writes = [out]
        if accum_out is not None:
            writes.append(accum_out)
            kw["accum_out"] = accum_out.ap
        return self.op(self.act, lambda: self.nc.scalar.activation(out.ap, in_.ap, func, **kw), reads, writes)

    def tt(self, out, in0, in1, op, eng=None):
        e = eng or self.dve
        return self.op(e, lambda: e.h.tensor_tensor(out.ap, in0.ap, in1.ap, op), [in0, in1], [out])

    def ts(self, out, in0, s1, s2, op0, op1=None, eng=None):
        e = eng or self.dve
        reads = [in0]
        a1, a2 = s1, s2
        if isinstance(s1, V):
            reads.append(s1)
            a1 = s1.ap
        if isinstance(s2, V):
            reads.append(s2)
            a2 = s2.ap
        if op1 is None:
            return self.op(e, lambda: e.h.tensor_scalar(out.ap, in0.ap, a1, a2, op0), reads, [out])
        return self.op(e, lambda: e.h.tensor_scalar(out.ap, in0.ap, a1, a2, op0, op1), reads, [out])

    def stt(self, out, in0, scalar, in1, op0, op1, eng=None):
        e = self.dve
        reads = [in0, in1]
        a = scalar
        if isinstance(scalar, V):
            reads.append(scalar)
            a = scalar.ap
        return self.op(e, lambda: e.h.scalar_tensor_tensor(out.ap, in0.ap, a, in1.ap, op0, op1), reads, [out])

    def copy(self, out, in_, eng=None):
        e = eng or self.dve
        if e is self.act:
            return self.op(e, lambda: self.nc.scalar.copy(out.ap, in_.ap), [in_], [out])
        return self.op(e, lambda: e.h.tensor_copy(out.ap, in_.ap), [in_], [out])

    def memset(self, out, val, eng=None):
        e = eng or self.dve
        return self.op(e, lambda: e.h.memset(out.ap, val), [], [out])

    def reduce_sum(self, out, in_, eng=None):
        e = eng or self.dve
        return self.op(e, lambda: e.h.tensor_reduce(out.ap, in_.ap, AX.X, ALU.add), [in_], [out])

    def rstd(self, out, tmp, ss, scale):
        self.actf(tmp, ss, AF.Ln, scale=scale, bias=EPS)
        self.actf(out, tmp, AF.Exp, scale=-0.5)

    def recip(self, out, in_):
        return self.op(self.dve, lambda: self.nc.vector.reciprocal(out.ap, in_.ap), [in_], [out])


class PsumPool:
    def __init__(self, S, stack):
        self.S = S
        self.banks = [S.ps([128, 512], F32, stack=stack) for _ in range(8)]
        self.avail = list(range(8))
        self.i = 0

    def set_avail(self, lst):
        self.avail = list(lst)
        self.i = 0

    def get(self):
        b = self.banks[self.avail[self.i % len(self.avail)]]
        self.i += 1
        return b


def bf16v(t):
    return t.h[:].bitcast(BF16)


def _consts():
    ident = np.eye(128, dtype=np.float32)
    j = np.arange(128)[:, None]
    i = np.arange(128)[None, :]
    mask_cur = (j <= i).astype(np.float32)
    mask_prev = (j > i).astype(np.float32)
    cm = np.zeros((128, CM_N), np.float32)
    offs = [0, 32, 96, 192]
    jp = np.arange(128)
    for I in range(4):
        r = 32 * I - 1
        for jj in range(32 * (I + 1)):
            col = offs[I] + jj
            cm[(jp > jj) & (jp <= r), col] = 1.0
            cm[(jp > r) & (jp <= jj), col] = -1.0
    for ii in range(128):
        cm[(jp >= 32 * (ii // 32)) & (jp <= ii), 320 + ii] = 1.0
    for I in range(4):
        cm[jp <= 32 * I - 1, 448 + I] = 1.0
    cm[:, 452] = 1.0
    su = (j > i).astype(np.float32)
    perm = np.zeros((128, 128), np.float32)
    invf = np.zeros((128, 1), np.float32)
    sgn = np.zeros((128, 1), np.float32)
    inv_freq = (np.float32(500000.0) ** (-np.arange(0, 16, 2, dtype=np.float32) / np.float32(16))).astype(np.float32)
    for f in range(128):
        m = f % 64
        if m < 8:
            perm[f + 8, f] = 1.0
            invf[f] = inv_freq[m]
            sgn[f] = -1.0
        elif m < 16:
            perm[f - 8, f] = 1.0
            invf[f] = inv_freq[m - 8]
            sgn[f] = 1.0
    hm = np.zeros((128, 2), np.float32)
    hm[0:64, 0] = 1.0
    hm[64:128, 1] = 1.0
    parts = [ident, mask_cur, mask_prev, cm, su, perm, invf, sgn, hm]
    offsets = {}
    o = 0
    for name, p in zip(["ident", "mask_cur", "mask_prev", "cm", "su", "perm", "invf", "sgn", "hm"], parts):
        offsets[name] = (o, p.shape[1])
        o += p.shape[1]
    return np.concatenate(parts, axis=1), offsets


CONSTS, COFF = _consts()
NCONST = CONSTS.shape[1]


class Prog:
    def __init__(self, dbg=None):
        self.dbg = dbg or {}
        self.nc = bass.Bass("TRN2", target_bir_lowering=False)
        nc = self.nc
        self.din = {}

        def inp(name, shape, dt=F32):
            self.din[name] = T(nc.dram_tensor(name, list(shape), dt, kind="ExternalInput").ap())
            return self.din[name]

        inp("x", [SEQ, D])
        inp("c_t", [128, 8])
        inp("pos", [1, SEQ], I32)
        inp("consts", [128, NCONST])
        inp("mod_w", [DEPTH, D, 6 * D])
        inp("mod_bc", [DEPTH, 128, 48])
        inp("mod_b", [DEPTH, 6 * D])
        inp("nmw_c", [DEPTH, 128, 8])
        inp("nfw_c", [DEPTH, 128, 8])
        inp("ev_w_in", [2, D, EV_COLS])
        inp("gatew_ext", [2, 32, 256])
        inp("gla_norm_w", [2, 128])
        inp("swa_sinks", [2, 8])
        inp("ev_w_out", [2, D, D])
        inp("od_w_in", [2, D, OD_COLS])
        inp("diff_lambda", [2, 256])
        inp("diff_norm_w", [2, 128])
        inp("lb_c", [128, 2, 4])
        inp("lb_r", [2, 512])
        inp("hgrn_norm_w", [2, 128])
        inp("od_w_out", [2, D, D])
        inp("ffn_w_in", [DEPTH, D, 2 * DFF])
        inp("convw_c", [DEPTH, 128, 3, NCHUNK_FF])
        inp("convb_c", [DEPTH, 128, NCHUNK_FF])
        inp("ffn_w_out", [DEPTH, DFF, D])
        inp("final_norm_w", [1, D])
        self.out = T(nc.dram_tensor("out", [SEQ, D], F32, kind="ExternalOutput").ap(), NT)
        self.xres = T(nc.dram_tensor("xres", [SEQ, D], F32, kind="Internal").ap(), NT)
        self.ropetab = T(nc.dram_tensor("ropetab", [2, 128, SEQ], F32, kind="Internal").ap(), 2)
        self.gates = T(nc.dram_tensor("gates", [DEPTH * 2, D], F32, kind="Internal").ap(), DEPTH * 2)
        self.odiff = T(nc.dram_tensor("odiff", [SEQ, 512], BF16, kind="Internal").ap(), NT)

        with ExitStack() as st:
            self.S = Sched(nc, st)
            self.pp = PsumPool(self.S, st)
            self.build(st)
            self.S.barrier()

    def cv(self, name, rows=128):
        o, n = COFF[name]
        return self.cst[0:rows, o:o + n]

    def build(self, st):
        S = self.S
        nc = self.nc
        self.cst = S.sb([128, NCONST], F32)
        S.dma(S.sp, self.cst[:], self.din["consts"][:])
        self.ident = S.sb([128, 128], BF16)
        S.copy(self.ident[:], self.cv("ident"))
        self.mask_cur = S.sb([128, 128], BF16)
        S.copy(self.mask_cur[:], self.cv("mask_cur"))
        self.mask_prev = S.sb([128, 128], BF16)
        S.copy(self.mask_prev[:], self.cv("mask_prev"))
        self.mb_cur = S.sb([128, 128], BF16)
        S.ts(self.mb_cur[:], self.cv("mask_cur"), 30000.0, -30000.0, ALU.mult, ALU.add)
        self.mb_prev = S.sb([128, 128], BF16)
        S.ts(self.mb_prev[:], self.cv("mask_prev"), 30000.0, -30000.0, ALU.mult, ALU.add)
        self.modc = S.sb([128, DEPTH, 4, 8], F32)
        self.lbc = S.sb([128, 2, 4], F32)
        self.omlc = S.sb([128, 2, 4], F32)
        for b in self.pp.banks:
            S.memset(b[:], 0.0)

        self.prologue()
        x_src = self.din["x"]
        x_src_subs = False
        nlayers = self.dbg.get("nlayers", DEPTH)
        for l in range(nlayers):
            if self.dbg.get("skip_mixer"):
                pass
            elif l % 2 == 0:
                self.even_pass(l, x_src, x_src_subs)
                x_src, x_src_subs = self.xres, True
            else:
                self.odd_pass_a(l, x_src, x_src_subs)
                self.odd_pass_b(l, x_src, x_src_subs)
                x_src, x_src_subs = self.xres, True
            if not self.dbg.get("skip_ffn"):
                self.ffn_pass(l, x_src, x_src_subs)
                x_src, x_src_subs = self.xres, True
        self.final_pass(x_src, x_src_subs)

    def xv(self, src, subs, t):
        ap = src.h[t * 128:(t + 1) * 128, :]
        return src.v(ap, [t] if subs else None)

    def prologue(self):
        S = self.S
        nc = self.nc
        with ExitStack() as st:
            ct = S.sb([128, 8], F32, stack=st)
            S.dma(S.sp, ct[:], self.din["c_t"][:])
            cact = S.sb([128, 8], F32, stack=st)
            S.actf(cact[:], ct[:], AF.Silu)
            cact_b = S.sb([128, 8], BF16, stack=st)
            S.copy(cact_b[:], cact[:])
            cact_bc = S.sb([128, 8, 128], BF16, stack=st)
            S.copy(cact_bc[:], V(cact.h[:].unsqueeze(2).to_broadcast([128, 8, 128]), cact.bufs))
            mw = [S.sb([128, 8, 1024], BF16, stack=st) for _ in range(2)]
            modbc = S.sb([128, DEPTH, 48], F32, stack=st)
            S.dma(S.sp, modbc[:], V(self.din["mod_bc"].h.rearrange("l p c -> p l c"), self.din["mod_bc"].bufs))
            nmw = S.sb([128, DEPTH, 8], F32, stack=st)
            S.dma(S.sp, nmw[:], V(self.din["nmw_c"].h.rearrange("l p c -> p l c"), self.din["nmw_c"].bufs))
            nfw = S.sb([128, DEPTH, 8], F32, stack=st)
            S.dma(S.sp, nfw[:], V(self.din["nfw_c"].h.rearrange("l p c -> p l c"), self.din["nfw_c"].bufs))
            gb = [S.sb([128, 1024], F32, stack=st) for _ in range(2)]
            grow = [S.sb([128, 1024], F32, stack=st) for _ in range(2)]
            it = 0
            nl = self.dbg.get("nlayers", DEPTH)
            for l in range(nl):
                for piece in range(6):
                    w = mw[it % 2]
                    it += 1
                    src = self.din["mod_w"].h[l, :, piece * 1024:(piece + 1) * 1024].rearrange("(k p) n -> p k n", p=128)
                    S.dma(S.pool, w[:], V(src, self.din["mod_w"].bufs))
                    if piece in (2, 5):
                        gi = 0 if piece == 2 else 1
                        S.dma(S.sp, gb[gi][:], V(self.din["mod_b"].h[l:l + 1, piece * 1024:(piece + 1) * 1024].partition_broadcast(128), self.din["mod_b"].bufs))
                        for n in range(2):
                            pb = self.pp.get()
                            for k in range(8):
                                S.mm(pb[:], cact_bc[:, k, :], w[:, k, n * 512:(n + 1) * 512], start=(k == 0), stop=(k == 7))
                            S.tt(grow[gi][:, n * 512:(n + 1) * 512], pb[:], gb[gi][:, n * 512:(n + 1) * 512], ALU.add)
                        S.dma(S.sp, self.gates.s(l * 2 + gi)[l * 2 + gi:l * 2 + gi + 1, :], grow[gi][0:1, :])
                    else:
                        kind = {0: 0, 1: 1, 3: 2, 4: 3}[piece]
                        pb = self.pp.get()
                        for cch in range(8):
                            for k in range(8):
                                S.mm(pb[:, cch:cch + 1], w[:, k, cch * 128:(cch + 1) * 128], cact_b[:, k:k + 1], start=(k == 0), stop=(k == 7))
                        S.tt(self.modc[:, l, kind, :], pb[:, 0:8], modbc[:, l, piece * 8:(piece + 1) * 8], ALU.add)
                        if kind in (1, 3):
                            nw = nmw if kind == 1 else nfw
                            S.stt(self.modc[:, l, kind, :], self.modc[:, l, kind, :], 1.0, nw[:, l, :], ALU.add, ALU.mult)
            lc = S.sb([128, 2, 4], F32, stack=st)
            S.dma(S.sp, lc[:], self.din["lb_c"][:])
            S.memset(self.lbc[:], 0.0)
            dlt = S.sb([128, 4], F32, stack=st)
            S.tt(dlt[:], lc[:, 1, :], lc[:, 0, :], ALU.subtract)
            S.actf(self.lbc[:, 1, :], dlt[:], AF.Sigmoid)
            S.ts(self.omlc[:], self.lbc[:], -1.0, 1.0, ALU.mult, ALU.add)
        with ExitStack() as st:
            posi = S.sb([128, 1024], I32, stack=st)
            ang = S.sb([128, 1024], F32, stack=st)
            kf = S.sb([128, 1024], F32, stack=st)
            ki = S.sb([128, 1024], I32, stack=st)
            r = S.sb([128, 1024], F32, stack=st)
            ra = S.sb([128, 1024], F32, stack=st)
            cs = S.sb([128, 1024], F32, stack=st)
            sn = S.sb([128, 1024], F32, stack=st)
            C1 = 6.28125
            C2 = TWO_PI - C1
            for q in range(4):
                sl = slice(q * 1024, (q + 1) * 1024)
                S.dma(S.sp, posi[:], V(self.din["pos"].h[0:1, sl].partition_broadcast(128), self.din["pos"].bufs))
                S.copy(ang[:], posi[:])
                S.ts(ang[:], ang[:], self.cv("invf"), None, ALU.mult)
                S.ts(kf[:], ang[:], 1.0 / TWO_PI, None, ALU.mult)
                S.copy(ki[:], kf[:])
                S.copy(kf[:], ki[:])
                S.stt(r[:], kf[:], -C1, ang[:], ALU.mult, ALU.add)
                S.stt(r[:], kf[:], -C2, r[:], ALU.mult, ALU.add)
                S.ts(r[:], r[:], -math.pi, math.pi, ALU.max, ALU.min)
                S.actf(sn[:], r[:], AF.Sin)
                S.ts(sn[:], sn[:], self.cv("sgn"), None, ALU.mult)
                S.actf(ra[:], r[:], AF.Abs)
                S.ts(ra[:], ra[:], -1.0, math.pi / 2, ALU.mult, ALU.add)
                S.actf(cs[:], ra[:], AF.Sin)
                S.dma(S.sp, self.ropetab.s(0)[0, :, sl], cs[:])
                S.dma(S.sp, self.ropetab.s(1)[1, :, sl], sn[:])

    def load_w(self, wt, src_t, src_ap_fn, nk):
        for k in range(nk):
            self.S.dma(self.S.pool, wt.s(k)[:, k, :], V(src_ap_fn(k), src_t.bufs))

    def norm_front(self, ctx, x_src, x_subs, g):
        S = self.S
        ss = ctx["ss"][g % 2]
        nb = ctx["batch"]
        S.memset(ss[:, 0, :], 0.0, eng=S.pool)
        for j0 in range(0, 4, nb):
            xts = []
            for j in range(j0, j0 + nb):
                t = g * 4 + j
                xt = ctx["xring"][ctx["xi"] % len(ctx["xring"])]
                ctx["xi"] += 1
                S.dma(S.sp, xt[:], self.xv(x_src, x_subs, t))
                S.actf(ctx["sq"][j % len(ctx["sq"])][:], xt[:], AF.Square, accum_out=ss[:, 0, j:j + 1])
                xts.append(xt)
            S.rstd(ss[:, 2, j0:j0 + nb], ss[:, 1, j0:j0 + nb], ss[:, 0, j0:j0 + nb], 1.0 / D)
            for j in range(j0, j0 + nb):
                S.ts(ctx["xn"][j][:], xts[j - j0][:], ss[:, 2, j:j + 1], None, ALU.mult)

    def norm_back(self, ctx, l, which, hT):
        S = self.S
        kind_sh, kind_w = (0, 1) if which == 0 else (2, 3)
        for j in range(4):
            xn = ctx["xn"][j]
            pb = self.pp.get()
            pv = bf16v(pb)
            for k in range(8):
                S.transpose(V(pv[:, k * 128:(k + 1) * 128], pb.bufs), xn[:, k * 128:(k + 1) * 128], self.ident[:])
            for k in range(8):
                o = hT[:, k, j * 128:(j + 1) * 128]
                i_ = V(pv[:, k * 128:(k + 1) * 128], pb.bufs)
                if k % 2 == 0:
                    S.actf(o, i_, AF.Identity, scale=self.modc[:, l, kind_w, k:k + 1], bias=self.modc[:, l, kind_sh, k:k + 1])
                else:
                    S.ts(o, i_, self.modc[:, l, kind_w, k:k + 1], self.modc[:, l, kind_sh, k:k + 1], ALU.mult, ALU.add)

    def norm_stage(self, ctx, l, which, x_src, x_subs, g, hT):
        self.norm_front(ctx, x_src, x_subs, g)
        self.norm_back(ctx, l, which, hT)

    def norm_ctx(self, st, nx=4):
        S = self.S
        ctx = {"_": None, "xring": [S.sb([128, D], F32, stack=st) for _ in range(nx)], "xi": 0, "batch": 4 if nx >= 4 else 2,
               "sq": [S.sb([128, D], BF16, stack=st) for _ in range(2 if nx >= 4 else 1)],
               "ss": [S.sb([128, 3, 4], F32, stack=st) for _ in range(2)],
               "xn": [S.sb([128, D], BF16, stack=st) for _ in range(4)]}
        for t_ in ctx["sq"]:
            t_.bufs[0].strict = True
        return ctx

    def out_stage_a(self, octx, t, ocat):
        S = self.S
        pb = self.pp.get()
        pv = bf16v(pb)
        for k in range(8):
            S.transpose(V(pv[:, k * 128:(k + 1) * 128], pb.bufs), ocat[:, k * 128:(k + 1) * 128], self.ident[:])
        oT = octx["oT"][t % 2]
        S.copy(oT[:], V(pv, pb.bufs), eng=S.act)

    def out_stage_b(self, ctx, octx, wout, x_src, x_subs, t, dst):
        S = self.S
        oT = octx["oT"][t % 2]
        xt = ctx["xring"][ctx["xi"] % len(ctx["xring"])]
        ctx["xi"] += 1
        S.dma(S.sp, xt[:], self.xv(x_src, x_subs, t))
        for n in range(2):
            py = self.pp.get()
            for k in range(8):
                S.mm(py[:], oT[:, k * 128:(k + 1) * 128], wout[:, k, n * 512:(n + 1) * 512], start=(k == 0), stop=(k == 7))
            tmp = octx["tmp"][n]
            S.tt(tmp[:], py[:], octx["gate"][:, n * 512:(n + 1) * 512], ALU.mult)
            S.tt(xt[:, n * 512:(n + 1) * 512], tmp[:], xt[:, n * 512:(n + 1) * 512], ALU.add, eng=S.pool)
        S.dma(S.pool, self.xv(dst, True, t), xt[:])

    def out_ctx(self, st, l, gi):
        S = self.S
        octx = {"oT": [S.sb([128, D], BF16, stack=st) for _ in range(2)],
                "tmp": [S.sb([128, 512], F32, stack=st) for _ in range(2)],
                "gate": S.sb([128, D], F32, stack=st)}
        S.dma(S.sp, octx["gate"][:], V(self.gates.h[l * 2 + gi:l * 2 + gi + 1, :].partition_broadcast(128), (self.gates.bufs[l * 2 + gi],)))
        return octx

    def rope_stage(self, rctx, src_ps, dst, g):
        S = self.S
        qf = rctx["qf"][rctx["i"] % 2]
        t1 = rctx["t1"][rctx["i"] % 2]
        rctx["i"] += 1
        S.copy(qf[:], src_ps, eng=S.act)
        prev = rctx.get("pending")
        rctx["pending"] = (qf, t1, dst, rctx["tab"])
        if prev is not None:
            self.rope_finish(prev)

    def rope_finish(self, item):
        S = self.S
        qf, t1, dst, tab = item
        pr = self.pp.get()
        S.mm(pr[:], self.cv("perm"), qf[:])
        S.tt(t1[:], qf[:], tab[:, 0, :], ALU.mult, eng=S.pool)
        S.tt(qf[:], pr[:], tab[:, 1, :], ALU.mult)
        if isinstance(dst, list):
            for (psl, d) in dst:
                S.tt(d, t1[psl, :], qf[psl, :], ALU.add)
        else:
            S.tt(dst, t1[:], qf[:], ALU.add)

    def rope_flush(self, rctx):
        prev = rctx.get("pending")
        rctx["pending"] = None
        if prev is not None:
            self.rope_finish(prev)

    def rope_ctx(self, st):
        S = self.S
        return {"qf": [S.sb([128, 512], F32, stack=st) for _ in range(2)],
                "t1": [S.sb([128, 512], F32, stack=st) for _ in range(2)],
                "tabs": [S.sb([128, 2, 512], F32, stack=st) for _ in range(2)], "i": 0, "tab": None}

    def rope_load(self, rctx, g):
        S = self.S
        tab = rctx["tabs"][g % 2]
        S.dma(S.sp, tab[:, 0, :], self.ropetab.s(0)[0, :, g * 512:(g + 1) * 512])
        S.dma(S.sp, tab[:, 1, :], self.ropetab.s(1)[1, :, g * 512:(g + 1) * 512])
        rctx["tab"] = tab

    def gl_ctx(self, st, nch, dk):
        S = self.S
        F = nch * 128
        c = {"nch": nch, "dk": dk, "F": F,
             "expE": [S.sb([128, CM_N], F32, stack=st) for _ in range(2)],
             "KO": [S.sb([128, nch, 4, 128], BF16, stack=st) for _ in range(2)],
             "qd": [S.sb([128, nch, 128 // dk, 128], BF16, stack=st) for _ in range(2)],
             "qh": [S.sb([128, nch, 128 // dk, 128], BF16, stack=st) for _ in range(2)],
             "qhf": [S.sb([128, 128], F32, stack=st) for _ in range(2)],
             "ek": [S.sb([128, F], F32, stack=st) for _ in range(2)],
             "khat": [S.sb([128, F], BF16, stack=st) for _ in range(2)],
             "A": [S.sb([128, 4, 128], BF16, stack=st) for _ in range(2)],
             "Sf": S.sb([128, nch, 128], F32, stack=st),
             "Sb": [S.sb([128, nch, 128], BF16, stack=st) for _ in range(2)],
             "etot": [S.sb([128, nch], F32, stack=st) for _ in range(2)],
             "sq": S.sb([128, 512], F32, stack=st),
             "ssum": [S.sb([128, 8], F32, stack=st) for _ in range(2)],
             "on": S.sb([128, 512], F32, stack=st),
             "i": 0}
        for t in c["KO"] + c["qd"] + c["qh"]:
            S.memset(t[:], 0.0)
        S.memset(c["Sf"][:], 0.0)
        for t in c["Sb"]:
            S.memset(t[:], 0.0)
        return c

    def gl_front(self, c, qT, kT, g_tm, k_tm, qscale):
        S = self.S
        nch, dk, F = c["nch"], c["dk"], c["F"]
        hpc = 128 // dk
        i = c["i"]
        c["i"] += 1
        stt_ = {"i": i, "KO": c["KO"][i % 2], "qd": c["qd"][i % 2], "qh": c["qh"][i % 2],
                "khat": c["khat"][i % 2], "etot": c["etot"][i % 2]}
        KO, qd, qh, etot = stt_["KO"], stt_["qd"], stt_["qh"], stt_["etot"]
        offs = [0, 32, 96, 192]
        pk = self.pp.get()
        S.mm(pk[:, 0:F], self.cv("su"), g_tm)
        pes = []
        for ch in range(nch):
            pe_ = self.pp.get()
            S.mm(pe_[:, 0:CM_N], g_tm_slice(g_tm, ch), self.cv("cm"))
            pes.append(pe_)
        ek = c["ek"][i % 2]
        S.actf(ek[:], pk[:, 0:F], AF.Exp)
        S.tt(stt_["khat"][:], ek[:], k_tm, ALU.mult, eng=S.pool)
        for ch in range(nch):
            X = c["expE"][(i * nch + ch) % 2]
            S.actf(X[:], pes[ch][:, 0:CM_N], AF.Exp)
            S.copy(etot[:, ch:ch + 1], X[:, 452:453], eng=S.pool)
            q_ = qT(ch)
            k_ = kT(ch)
            for hh in range(hpc):
                psl = slice(hh * dk, (hh + 1) * dk)
                S.stt(qd[psl, ch, hh, :], V(q_.ap[psl, :], q_.bufs), qscale, X[psl, 320:448], ALU.mult, ALU.mult)
            for I in range(4):
                n = 32 * (I + 1)
                S.tt(KO[:, ch, I, 0:n], V(k_.ap[:, 0:n], k_.bufs), X[:, offs[I]:offs[I] + n], ALU.mult)
            if qscale != 1.0:
                S.ts(X[:, 448:452], X[:, 448:452], qscale, None, ALU.mult)
            for I in range(4):
                for hh in range(hpc):
                    psl = slice(hh * dk, (hh + 1) * dk)
                    S.stt(qh[psl, ch, hh, 32 * I:32 * I + 32], V(q_.ap[psl, 32 * I:32 * I + 32], q_.bufs), X[psl, 448 + I:449 + I],
                          X[psl, 320 + 32 * I:352 + 32 * I], ALU.mult, ALU.mult)
        return stt_

    def gl_scores(self, c, stt_):
        S = self.S
        nch, dk = c["nch"], c["dk"]
        hpc = 128 // dk
        KO, qd = stt_["KO"], stt_["qd"]
        psc = self.pp.get()
        for ch in range(nch):
            for hh in range(hpc):
                h = ch * hpc + hh
                for I in range(4):
                    S.mm(psc[:, h * 128 + 32 * I:h * 128 + 32 * I + 32], KO[:, ch, I, :], qd[:, ch, hh, 32 * I:32 * I + 32])
        A = c["A"][stt_["i"] % 2]
        S.tt(A[:], V(psc.h[:].rearrange("p (h i) -> p h i", h=4), psc.bufs),
             V(self.mask_cur.h[:].unsqueeze(1).to_broadcast([128, 4, 128]), self.mask_cur.bufs), ALU.mult)
        stt_["A"] = A

    def gl_back(self, c, stt_, v_tm, gw, out_bf):
        S = self.S
        nch, dk = c["nch"], c["dk"]
        hpc = 128 // dk
        i = stt_["i"]
        A, qh, khat, etot = stt_["A"], stt_["qh"], stt_["khat"], stt_["etot"]
        Sb_prev = c["Sb"][(i + 1) % 2]
        Sb_new = c["Sb"][i % 2]
        po = self.pp.get()
        for h in range(4):
            ch, hh = divmod(h, hpc)
            S.mm(po[:, h * 128:(h + 1) * 128], A[:, h, :], v_tm(h), start=True, stop=False)
            S.mm(po[:, h * 128:(h + 1) * 128], qh[:, ch, hh, :], Sb_prev[:, ch, :], start=False, stop=True)
        pS = self.pp.get()
        for h in range(4):
            ch, hh = divmod(h, hpc)
            ps_ = slice(hh * dk, (hh + 1) * dk)
            if hpc == 1:
                S.mm(pS[:, h * 128:(h + 1) * 128], khat[:, h * 128:(h + 1) * 128], v_tm(h))
            else:
                S.mm(pS[ps_, ch * 128:(ch + 1) * 128], khat[:, h * dk:(h + 1) * dk], v_tm(h))
        for ch in range(nch):
            S.stt(c["Sf"][:, ch, :], c["Sf"][:, ch, :], etot[:, ch:ch + 1], pS[:, ch * 128:(ch + 1) * 128], ALU.mult, ALU.add)
        S.copy(Sb_new[:], c["Sf"][:], eng=S.act)
        sq = c["sq"]
        S.actf(sq[:], po[:], AF.Square)
        ssum = c["ssum"][i % 2]
        S.reduce_sum(ssum[:, 0:4], V(sq.h[:].rearrange("p (h d) -> p h d", h=4), sq.bufs))
        S.rstd(ssum[:, 0:4], ssum[:, 4:8], ssum[:, 0:4], 1.0 / 128)
        on = c["on"]
        S.tt(V(on.h[:].rearrange("p (h d) -> p h d", h=4), on.bufs), V(po.h[:].rearrange("p (h d) -> p h d", h=4), po.bufs),
             V(ssum.h[:, 0:4].unsqueeze(2).to_broadcast([128, 4, 128]), ssum.bufs), ALU.mult)
        S.tt(out_bf, on[:], gw, ALU.mult, eng=S.pool)

    def ffn_pass(self, l, x_src, x_subs):
        S = self.S
        self.S.barrier()
        self.pp.set_avail(range(8))
        with ExitStack() as st:
            w1 = S.sb([128, 8, 2 * DFF], BF16, nsub=12, stack=st)
            w2 = S.sb([128, NCHUNK_FF, D], BF16, nsub=NCHUNK_FF, stack=st)
            fw = self.din["ffn_w_in"]
            for blk in range(6):
                c0 = blk * 4
                ncol = min(4, NCHUNK_FF - c0) * 128
                for part in range(2):
                    col0 = part * DFF + c0 * 128
                    S.dma(S.pool, w1.s(blk * 2 + part)[:, :, col0:col0 + ncol],
                          V(fw.h[l, :, col0:col0 + ncol].rearrange("(k p) n -> p k n", p=128), fw.bufs))
            self.load_w(w2, self.din["ffn_w_out"], lambda k: self.din["ffn_w_out"].h[l, k * 128:(k + 1) * 128, :], NCHUNK_FF)
            cw = S.sb([128, 3, NCHUNK_FF], F32, stack=st)
            S.dma(S.sp, cw[:], self.din["convw_c"][l])
            cb = S.sb([128, NCHUNK_FF], F32, stack=st)
            S.dma(S.sp, cb[:], self.din["convb_c"][l])
            ctx = self.norm_ctx(st, nx=3)
            octx_gate = S.sb([128, D], F32, stack=st)
            S.dma(S.sp, octx_gate[:], V(self.gates.h[l * 2 + 1:l * 2 + 2, :].partition_broadcast(128), (self.gates.bufs[l * 2 + 1],)))
            hT = S.sb([128, 8, 512], BF16, stack=st)
            gT = S.sb([128, NCHUNK_FF, 512], BF16, stack=st)
            abuf = [S.sb([128, 514], F32, stack=st) for _ in range(2)]
            halo = S.sb([128, NCHUNK_FF, 2], F32, stack=st)
            S.memset(halo[:], 0.0)
            tcv = [S.sb([128, 512], F32, stack=st) for _ in range(2)]
            tsl = [S.sb([128, 512], F32, stack=st) for _ in range(2)]
            tmp = tcv
            ngroups = self.dbg.get("ngroups", NG)
            if self.dbg.get("verbose"):
                print("ffn sbuf remaining", self.nc.sbuf_bytes_remaining, flush=True)

            def ytile(g, j):
                t = g * 4 + j
                xt = ctx["xring"][ctx["xi"] % len(ctx["xring"])]
                ctx["xi"] += 1
                S.dma(S.sp, xt[:], self.xv(x_src, x_subs, t))
                for n in range(2):
                    py = self.pp.get()
                    for c in range(NCHUNK_FF):
                        S.mm(py[:], gT[:, c, j * 128:(j + 1) * 128], w2.s(c)[:, c, n * 512:(n + 1) * 512], start=(c == 0), stop=(c == NCHUNK_FF - 1))
                    S.tt(tmp[n][:], py[:], octx_gate[:, n * 512:(n + 1) * 512], ALU.mult)
                    S.tt(xt[:, n * 512:(n + 1) * 512], tmp[n][:], xt[:, n * 512:(n + 1) * 512], ALU.add, eng=S.pool)
                S.dma(S.pool, self.xv(self.xres, True, t), xt[:])

            self.norm_stage(ctx, l, 1, x_src, x_subs, 0, hT)
            for g in range(ngroups):
                for c in range(NCHUNK_FF):
                    pa = self.pp.get()
                    for k in range(8):
                        S.mm(pa[:], w1.s((c // 4) * 2)[:, k, c * 128:(c + 1) * 128], hT[:, k, :], start=(k == 0), stop=(k == 7))
                    pu = self.pp.get()
                    for k in range(8):
                        S.mm(pu[:], w1.s((c // 4) * 2 + 1)[:, k, DFF + c * 128:DFF + (c + 1) * 128], hT[:, k, :], start=(k == 0), stop=(k == 7))
                    ab = abuf[c % 2]
                    S.copy(ab[:, 0:2], halo[:, c, :], eng=S.pool)
                    S.copy(ab[:, 2:514], pa[:], eng=S.act)
                    S.copy(halo[:, c, :], ab[:, 512:514], eng=S.pool)
                    tc_ = tcv[c % 2]
                    S.ts(tc_[:], ab[:, 2:514], cw[:, 2, c:c + 1], cb[:, c:c + 1], ALU.mult, ALU.add)
                    S.stt(tc_[:], ab[:, 1:513], cw[:, 1, c:c + 1], tc_[:], ALU.mult, ALU.add)
                    S.stt(tc_[:], ab[:, 0:512], cw[:, 0, c:c + 1], tc_[:], ALU.mult, ALU.add)
                    ts_ = tsl[c % 2]
                    S.actf(ts_[:], tc_[:], AF.Silu)
                    S.tt(gT[:, c, :], ts_[:], pu[:], ALU.mult)
                nxt = g + 1 < ngroups
                if nxt:
                    self.norm_front(ctx, x_src, x_subs, g + 1)
                ytile(g, 0)
                ytile(g, 1)
                if nxt:
                    self.norm_back(ctx, l, 1, hT)
                ytile(g, 2)
                ytile(g, 3)
        self.S.barrier()

    def final_pass(self, x_src, x_subs):
        S = self.S
        self.S.barrier()
        with ExitStack() as st:
            wf = S.sb([128, D], F32, stack=st)
            S.dma(S.sp, wf[:], V(self.din["final_norm_w"].h[0:1, :].partition_broadcast(128), self.din["final_norm_w"].bufs))
            xr = [S.sb([128, D], F32, stack=st) for _ in range(3)]
            sq = S.sb([128, D], BF16, stack=st)
            ss = [S.sb([128, 4], F32, stack=st) for _ in range(2)]
            ntiles = self.dbg.get("ngroups", NG) * 4
            for t in range(ntiles):
                xt = xr[t % 3]
                S.dma(S.sp, xt[:], self.xv(x_src, x_subs, t))
                s_ = ss[t % 2]
                S.memset(s_[:, 0:1], 0.0, eng=S.pool)
                S.actf(sq[:], xt[:], AF.Square, accum_out=s_[:, 0:1])
                S.rstd(s_[:, 2:3], s_[:, 1:2], s_[:, 0:1], 1.0 / D)
                S.stt(xt[:], xt[:], s_[:, 2:3], wf[:], ALU.mult, ALU.mult)
                S.dma(S.pool, self.xv(self.out, True, t), xt[:])

    def even_pass(self, l, x_src, x_subs):
        S = self.S
        jl = l // 2
        self.S.barrier()
        self.pp.set_avail(range(8))
        with ExitStack() as st:
            win = S.sb([128, 8, EV_COLS], BF16, nsub=8, stack=st)
            wout = S.sb([128, 8, D], BF16, nsub=8, stack=st)
            self.load_w(win, self.din["ev_w_in"], lambda k: self.din["ev_w_in"].h[jl, k * 128:(k + 1) * 128, :], 8)
            self.load_w(wout, self.din["ev_w_out"], lambda k: self.din["ev_w_out"].h[jl, k * 128:(k + 1) * 128, :], 8)
            gwx = S.sb([32, 256], BF16, stack=st)
            S.dma(S.pool, gwx[:], self.din["gatew_ext"][jl])
            normw = S.sb([128, 128], F32, stack=st)
            S.dma(S.sp, normw[:], V(self.din["gla_norm_w"].h[jl:jl + 1, :].partition_broadcast(128), self.din["gla_norm_w"].bufs))
            esink = S.sb([128, 8], F32, stack=st)
            S.dma(S.sp, esink[:], V(self.din["swa_sinks"].h[jl:jl + 1, :].partition_broadcast(128), self.din["swa_sinks"].bufs))
            S.actf(esink[:], esink[:], AF.Exp)
            ctx = self.norm_ctx(st, nx=3)
            octx = self.out_ctx(st, l, 0)
            rctx = self.rope_ctx(st)
            glc = self.gl_ctx(st, 2, 64)
            hT = S.sb([128, 8, 512], BF16, stack=st)
            qT = S.sb([128, 2, 512], F32, stack=st)
            kT = S.sb([128, 2, 512], F32, stack=st)
            glrT = S.sb([32, 512], BF16, stack=st)
            S.memset(glrT[:], 1.0)
            sqT = [S.sb([128, 8, 512], BF16, stack=st) for _ in range(2)]
            for q__ in sqT:
                S.memset(q__[:], 0.0)
            skT = [S.sb([128, 2, 512], BF16, stack=st) for _ in range(2)]
            vext = [S.sb([128, 2, 65], BF16, stack=st) for _ in range(3)]
            for v_ in vext:
                S.memset(v_[:], 1.0)
            k_tm = [S.sb([128, 256], F32, stack=st) for _ in range(2)]
            v_tm = [S.sb([128, 512], BF16, stack=st) for _ in range(2)]
            gwr = [S.sb([128, 512], F32, stack=st) for _ in range(5)]
            g_tm = [S.sb([128, 256], F32, stack=st) for _ in range(2)]
            ez = [S.sb([128, 256], F32, stack=st) for _ in range(2)]
            gsl = [S.sb([128, 512], F32, stack=st) for _ in range(2)]
            ocat = [S.sb([128, D], BF16, stack=st) for _ in range(2)]
            PT = [S.sb([128, 4, 128], BF16, stack=st) for _ in range(4)]
            den = [S.sb([128, 8], F32, stack=st) for _ in range(2)]
            ngroups = self.dbg.get("ngroups", NG)
            ntiles = ngroups * 4
            tst = {}

            fronted = set()

            def group_front(g):
                fronted.add(g)
                self.rope_load(rctx, g)
                self.norm_front(ctx, x_src, x_subs, g)

            def group_stage(g):
                if g not in fronted:
                    group_front(g)
                self.norm_back(ctx, l, 0, hT)
                sq_ = sqT[g % 2]

                def fm(col0, m, dst_fn):
                    pb = self.pp.get()
                    for k in range(8):
                        S.mm(pb[0:m, :], win.s(k)[:, k, col0:col0 + m], hT[:, k, :], start=(k == 0), stop=(k == 7))
                    dst_fn(pb)
                for ch in range(2):
                    fm(ch * 128, 128, lambda pb, ch=ch: S.actf(qT[:, ch, :], pb[:], AF.Copy, scale=0.125))
                    fm(256 + ch * 128, 128, lambda pb, ch=ch: S.copy(kT[:, ch, :], pb[:], eng=S.act))
                fm(1536, 16, lambda pb: S.copy(glrT[0:16, :], pb[0:16, :], eng=S.act))
                for ch in range(4):
                    fm(1552 + ch * 128, 128, lambda pb, ch=ch: self.rope_stage(rctx, pb[:], [(slice(0, 64), sq_[0:64, 2 * ch, :]), (slice(64, 128), sq_[64:128, 2 * ch + 1, :])], g))
                skt = skT[g % 2]
                for kv in range(2):
                    pb = self.pp.get()
                    for half in range(2):
                        for k in range(8):
                            S.mm(pb[half * 64:(half + 1) * 64, :], win.s(k)[:, k, 2064 + kv * 64:2064 + (kv + 1) * 64], hT[:, k, :], start=(k == 0), stop=(k == 7))
                    self.rope_stage(rctx, pb[:], skt[:, kv, :], g)
                self.rope_flush(rctx)
                for j in range(4):
                    pb = self.pp.get()
                    for k in range(8):
                        S.mm(pb[:], hT[:, k, j * 128:(j + 1) * 128], win.s(k)[:, k, 1024:1536], start=(k == 0), stop=(k == 7))
                    gs_ = gsl[j % 2]
                    gw_ = gwr[(g * 4 + j) % 5]
                    S.actf(gs_[:], pb[:], AF.Silu)
                    S.tt(V(gw_.h[:].rearrange("p (h d) -> p h d", h=4), gw_.bufs), V(gs_.h[:].rearrange("p (h d) -> p h d", h=4), gs_.bufs),
                         V(normw.h[:].unsqueeze(1).to_broadcast([128, 4, 128]), normw.bufs), ALU.mult, eng=S.pool)

            def stage_P(t):
                g, j = divmod(t, 4)
                tsl = slice(j * 128, (j + 1) * 128)

                def tm(col0, n, dst_fn):
                    pb = self.pp.get()
                    for k in range(8):
                        S.mm(pb[:, 0:n], hT[:, k, tsl], win.s(k)[:, k, col0:col0 + n], start=(k == 0), stop=(k == 7))
                    dst_fn(pb)
                ktm = k_tm[t % 2]
                vtm = v_tm[t % 2]
                gw_ = gwr[(g * 4 + j) % 5]
                vx = vext[t % 3]
                pz = self.pp.get()
                S.mm(pz[:, 0:256], glrT[:, tsl], gwx[:])
                ez_ = ez[t % 2]
                S.actf(ez_[:], pz[:, 0:256], AF.Exp, scale=-1.0)
                S.actf(ez_[:], ez_[:], AF.Ln, bias=1.0)
                gtm = g_tm[t % 2]
                S.actf(gtm[:], ez_[:], AF.Identity, scale=-1.0 / 16.0)
                tm(256, 256, lambda pb: S.copy(ktm[:], pb[:, 0:256], eng=S.act))
                tm(512, 512, lambda pb: S.copy(vtm[:], pb[:], eng=S.act))

                tm(2192, 128, lambda pb: S.copy(vx[:, :, 0:64], V(pb.h[:, 0:128].rearrange("p (g d) -> p g d", g=2), pb.bufs), eng=S.act))
                tst[t] = {"tsl": tsl, "ktm": ktm, "vtm": vtm, "gw": gw_, "gtm": gtm, "vx": vx, "oc": ocat[t % 2]}

            def stage_G1(t):
                d = tst[t]
                tsl = d["tsl"]
                d["gl"] = self.gl_front(glc, lambda ch: qT[:, ch, tsl], lambda ch: kT[:, ch, tsl], d["gtm"][:], d["ktm"][:], 1.0)

            def stage_G2a(t):
                self.gl_scores(glc, tst[t]["gl"])

            def stage_G2b(t):
                d = tst[t]
                vtm = d["vtm"]
                self.gl_back(glc, d["gl"], lambda h: vtm[:, h * 128:(h + 1) * 128], d["gw"][:], d["oc"][:, 0:512])

            def stage_Wa(t):
                d = tst[t]
                g, j = divmod(t, 4)
                tsl = d["tsl"]
                skt = skT[g % 2]
                sq_ = sqT[g % 2]
                if j > 0:
                    prev_k = (skt, slice((j - 1) * 128, j * 128))
                elif g > 0:
                    prev_k = (skT[(g - 1) % 2], slice(384, 512))
                else:
                    prev_k = None
                vprev = vext[(t - 1) % 3]
                d["pts"] = []
                pti = 0
                for kv in range(2):
                    blocks = []
                    if prev_k is not None:
                        blocks.append((prev_k[0], prev_k[1], self.mb_prev, vprev))
                    blocks.append((skt, tsl, self.mb_cur, d["vx"]))
                    pts = []
                    for (kt_, ks_, msk, vv) in blocks:
                        pss = self.pp.get()
                        S.mm(V(pss.h[:].rearrange("p (h i) -> p h i", h=4), pss.bufs), self.ident[:],
                             V(msk.h[:].unsqueeze(1).to_broadcast([128, 4, 128]), msk.bufs), start=True, stop=False)
                        for r in range(4):
                            h = kv * 4 + r
                            S.mm(pss[:, r * 128:(r + 1) * 128], kt_[:, kv, ks_], sq_[:, h, tsl], start=False, stop=(r == 3))
                        pt = PT[pti % 4]
                        pti += 1
                        S.actf(pt[:], V(pss.h[:].rearrange("p (h i) -> p h i", h=4), pss.bufs), AF.Exp, scale=0.125)
                        pts.append((pt, vv))
                    d["pts"].append(pts)

            def stage_Wb(t):
                d = tst[t]
                oc = d["oc"]
                for kv in range(2):
                    pts = d["pts"][kv]
                    po = self.pp.get()
                    for r in range(4):
                        for bi, (pt, vv) in enumerate(pts):
                            S.mm(po[:, r * 65:(r + 1) * 65], pt[:, r, :], vv[:, kv, :], start=(bi == 0), stop=(bi == len(pts) - 1))
                    dn = den[t % 2]
                    pov = po.h[:, 0:260].rearrange("p (h d) -> p h d", h=4)
                    S.tt(dn[:, kv * 4:(kv + 1) * 4], V(pov[:, :, 64], po.bufs), esink[:, kv * 4:(kv + 1) * 4], ALU.add)
                    S.recip(dn[:, kv * 4:(kv + 1) * 4], dn[:, kv * 4:(kv + 1) * 4])
                    S.tt(V(oc.h[:, 512 + kv * 256:512 + (kv + 1) * 256].rearrange("p (h d) -> p h d", h=4), oc.bufs),
                         V(pov[:, :, 0:64], po.bufs),
                         V(dn.h[:, kv * 4:(kv + 1) * 4].unsqueeze(2).to_broadcast([128, 4, 64]), dn.bufs), ALU.mult)

            def stage_Oa(t):
                self.out_stage_a(octx, t, tst[t]["oc"])

            def stage_Ob(t):
                self.out_stage_b(ctx, octx, _WSub(wout), x_src, x_subs, t, self.xres)
                del tst[t]

            if self.dbg.get("verbose"):
                print("sbuf remaining", self.nc.sbuf_bytes_remaining, flush=True)
            group_stage(0)
            stage_P(0)
            stage_G1(0)
            for t in range(ntiles):
                if t + 2 < ntiles and (t + 2) % 4 == 0:
                    group_front((t + 2) // 4)
                if t + 1 < ntiles:
                    if (t + 1) % 4 == 0:
                        group_stage((t + 1) // 4)
                    stage_P(t + 1)
                stage_Wa(t)
                stage_G2a(t)
                if t >= 1:
                    stage_Oa(t - 1)
                if t + 1 < ntiles:
                    stage_G1(t + 1)
                stage_G2b(t)
                stage_Wb(t)
                if t >= 1:
                    stage_Ob(t - 1)
            stage_Oa(ntiles - 1)
            stage_Ob(ntiles - 1)
        self.S.barrier()

    def odd_pass_a(self, l, x_src, x_subs):
        S = self.S
        jl = l // 2
        lam_init = 0.8 - 0.6 * math.exp(-0.3 * l)
        self.S.barrier()
        self.pp.set_avail(range(6, 8))
        accsets = [[self.pp.banks[0], self.pp.banks[1]], [self.pp.banks[2], self.pp.banks[3]]]
        stb = [self.pp.banks[4], self.pp.banks[5]]
        with ExitStack() as st:
            win = S.sb([128, 8, 1536], BF16, nsub=8, stack=st)
            self.load_w(win, self.din["od_w_in"], lambda k: self.din["od_w_in"].h[jl, k * 128:(k + 1) * 128, 0:1536], 8)
            normw = S.sb([128, 128], F32, stack=st)
            S.dma(S.sp, normw[:], V(self.din["diff_norm_w"].h[jl:jl + 1, :].partition_broadcast(128), self.din["diff_norm_w"].bufs))
            S.ts(normw[:], normw[:], 1.0 - lam_init, None, ALU.mult)
            lv = S.sb([128, 256], F32, stack=st)
            S.dma(S.sp, lv[:], V(self.din["diff_lambda"].h[jl:jl + 1, :].partition_broadcast(128), self.din["diff_lambda"].bufs))
            lp = S.sb([128, 128], F32, stack=st)
            lv4 = lv.h[:].rearrange("p (a b d) -> p a b d", a=2, b=2)
            S.tt(V(lp.h[:].rearrange("p (a d) -> p a d", a=2), lp.bufs), V(lv4[:, :, 0, :], lv.bufs), V(lv4[:, :, 1, :], lv.bufs), ALU.mult)
            lsum = S.sb([128, 4], F32, stack=st)
            S.reduce_sum(lsum[:, 0:2], V(lp.h[:].rearrange("p (a d) -> p a d", a=2), lp.bufs))
            S.actf(lsum[:, 2:4], lsum[:, 0:2], AF.Exp)
            nlam = S.sb([128, 1], F32, stack=st)
            S.tt(nlam[:], lsum[:, 3:4], lsum[:, 2:3], ALU.subtract)
            S.ts(nlam[:], nlam[:], -lam_init, None, ALU.add)
            ctx = self.norm_ctx(st)
            rctx = self.rope_ctx(st)
            hT = S.sb([128, 8, 512], BF16, stack=st)
            KT = S.sb([128, 4, SEQ], BF16, nsub=NG, stack=st)
            VX = S.sb([128, NT, 4, 129], BF16, nsub=NG, stack=st)
            for g in range(NG):
                S.memset(VX.s(g)[:, g * 4:(g + 1) * 4, :, :], 1.0, eng=S.pool)
            QT = [S.sb([128, 4, 2, 512], BF16, stack=st) for _ in range(2)]
            for q_ in QT:
                S.memset(q_[:], 0.0)
            PT = [S.sb([128, 512], BF16, stack=st) for _ in range(3)]
            o1 = S.sb([128, 4, 128], F32, stack=st)
            od = [S.sb([128, 4, 128], F32, stack=st) for _ in range(4)]
            rl = [S.sb([128, 4], F32, stack=st) for _ in range(2)]
            sq = S.sb([128, 512], F32, stack=st)
            ssum = [S.sb([128, 8], F32, stack=st) for _ in range(2)]
            on = S.sb([128, 512], F32, stack=st)
            ob = [S.sb([128, 512], BF16, stack=st) for _ in range(2)]
            pti = 0
            ngroups = self.dbg.get("ngroups", NG)
            fronted = set()

            def group_front(g):
                fronted.add(g)
                self.rope_load(rctx, g)
                self.norm_front(ctx, x_src, x_subs, g)

            def group_stage(g):
                if g not in fronted:
                    group_front(g)
                self.norm_back(ctx, l, 0, hT)
                qt = QT[g % 2]
                gsl_ = slice(g * 512, (g + 1) * 512)
                for ch in range(4):
                    pb = self.pp.get()
                    for k in range(8):
                        S.mm(pb[:], win.s(k)[:, k, ch * 128:(ch + 1) * 128], hT[:, k, :], start=(k == 0), stop=(k == 7))
                    self.rope_stage(rctx, pb[:], [(slice(0, 64), qt[0:64, ch, 0, :]), (slice(64, 128), qt[64:128, ch, 1, :])], g)
                    pb = self.pp.get()
                    for k in range(8):
                        S.mm(pb[:], win.s(k)[:, k, 512 + ch * 128:512 + (ch + 1) * 128], hT[:, k, :], start=(k == 0), stop=(k == 7))
                    self.rope_stage(rctx, pb[:], KT.s(g)[:, ch, gsl_], g)
                self.rope_flush(rctx)
                for j in range(4):
                    t = g * 4 + j
                    pb = self.pp.get()
                    for k in range(8):
                        S.mm(pb[:], hT[:, k, j * 128:(j + 1) * 128], win.s(k)[:, k, 1024:1536], start=(k == 0), stop=(k == 7))
                    S.copy(VX.s(g)[:, t, :, 0:128], V(pb.h[:].rearrange("p (h d) -> p h d", h=4), pb.bufs), eng=S.act)

            group_stage(0)
            for g in range(ngroups):
                qt = QT[g % 2]
                nkb = 4 * g + 4
                its = [(h, m, kb) for h in range(4) for m in range(2) for kb in range(nkb)]

                def emit_st(it, idx):
                    h, m, kb = it
                    q0 = max(0, kb - 4 * g)
                    cols = slice(q0 * 128, 512)
                    pst = stb[idx % 2]
                    S.mm(pst[:, cols], KT.s(kb // 4)[:, h, kb * 128:(kb + 1) * 128], qt[:, h, m, cols])
                    return pst

                pst_next = emit_st(its[0], pti)
                for ii, (h, m, kb) in enumerate(its):
                    if ii == nkb and g + 1 < ngroups:
                        group_front(g + 1)
                    if ii == 2 * nkb and g + 1 < ngroups:
                        group_stage(g + 1)
                    pst = pst_next
                    pt = PT[pti % 3]
                    if ii + 1 < len(its):
                        pst_next = emit_st(its[ii + 1], pti + 1)
                    pti += 1
                    q0 = max(0, kb - 4 * g)
                    cols = slice(q0 * 128, 512)
                    kg = kb // 4
                    accs = accsets[(h * 2 + m) % 2]
                    S.actf(pt[:, cols], pst[:, cols], AF.Exp, scale=0.125)
                    if kb >= 4 * g:
                        dsl = slice(q0 * 128, (q0 + 1) * 128)
                        S.tt(pt[:, dsl], pt[:, dsl], self.mask_cur[:], ALU.mult, eng=S.pool)
                    for qb in range(q0, 4):
                        acc = accs[qb // 2]
                        o_ = (qb % 2) * 129
                        S.mm(acc[:, o_:o_ + 129], pt[:, qb * 128:(qb + 1) * 128], VX.s(kg)[:, kb, h, :],
                             start=(kb == 0 and qb % 2 == 0), stop=(kb == 4 * g + qb and qb % 2 == 1))
                    if kb == nkb - 1:
                        r_ = rl[m]
                        for qb in range(4):
                            acc = accs[qb // 2]
                            o_ = (qb % 2) * 129
                            S.recip(r_[:, qb:qb + 1], acc[:, o_ + 128:o_ + 129])
                            if m == 0:
                                S.ts(o1[:, qb, :], acc[:, o_:o_ + 128], r_[:, qb:qb + 1], None, ALU.mult)
                            else:
                                S.ts(r_[:, qb:qb + 1], r_[:, qb:qb + 1], nlam[:, 0:1], None, ALU.mult)
                                S.stt(od[qb][:, h, :], acc[:, o_:o_ + 128], r_[:, qb:qb + 1], o1[:, qb, :], ALU.mult, ALU.add)
                for qb in range(4):
                    t = g * 4 + qb
                    odv = V(od[qb].h[:].rearrange("p h d -> p (h d)"), od[qb].bufs)
                    S.actf(sq[:], odv, AF.Square)
                    ss_ = ssum[t % 2]
                    S.reduce_sum(ss_[:, 0:4], V(sq.h[:].rearrange("p (h d) -> p h d", h=4), sq.bufs))
                    S.rstd(ss_[:, 0:4], ss_[:, 4:8], ss_[:, 0:4], 1.0 / 128)
                    S.tt(V(on.h[:].rearrange("p (h d) -> p h d", h=4), on.bufs), od[qb][:],
                         V(ss_.h[:, 0:4].unsqueeze(2).to_broadcast([128, 4, 128]), ss_.bufs), ALU.mult)
                    ob_ = ob[t % 2]
                    S.tt(V(ob_.h[:].rearrange("p (h d) -> p h d", h=4), ob_.bufs), V(on.h[:].rearrange("p (h d) -> p h d", h=4), on.bufs),
                         V(normw.h[:].unsqueeze(1).to_broadcast([128, 4, 128]), normw.bufs), ALU.mult, eng=S.pool)
                    S.dma(S.pool, self.odiff.s(t)[t * 128:(t + 1) * 128, :], ob_[:])
        self.S.barrier()
        self.pp.set_avail(range(8))

    def odd_pass_b(self, l, x_src, x_subs):
        S = self.S
        jl = l // 2
        self.S.barrier()
        self.pp.set_avail(range(8))
        with ExitStack() as st:
            win = S.sb([128, 8, 2048], BF16, nsub=8, stack=st)
            wout = S.sb([128, 8, D], BF16, nsub=8, stack=st)
            self.load_w(win, self.din["od_w_in"], lambda k: self.din["od_w_in"].h[jl, k * 128:(k + 1) * 128, 1536:3584], 8)
            self.load_w(wout, self.din["od_w_out"], lambda k: self.din["od_w_out"].h[jl, k * 128:(k + 1) * 128, :], 8)
            normw = S.sb([128, 128], F32, stack=st)
            S.dma(S.sp, normw[:], V(self.din["hgrn_norm_w"].h[jl:jl + 1, :].partition_broadcast(128), self.din["hgrn_norm_w"].bufs))
            lbr = S.sb([128, 512], F32, stack=st)
            omlr = S.sb([128, 512], F32, stack=st)
            if jl == 0:
                S.memset(lbr[:], 0.0)
            else:
                l0 = S.sb([128, 512], F32, stack=st)
                S.dma(S.sp, l0[:], V(self.din["lb_r"].h[0:1, :].partition_broadcast(128), self.din["lb_r"].bufs))
                S.dma(S.sp, lbr[:], V(self.din["lb_r"].h[1:2, :].partition_broadcast(128), self.din["lb_r"].bufs))
                S.tt(lbr[:], lbr[:], l0[:], ALU.subtract)
                S.actf(lbr[:], lbr[:], AF.Sigmoid)
            S.ts(omlr[:], lbr[:], -1.0, 1.0, ALU.mult, ALU.add)
            ctx = self.norm_ctx(st)
            octx = self.out_ctx(st, l, 0)
            glc = self.gl_ctx(st, 4, 128)
            hT = S.sb([128, 8, 512], BF16, stack=st)
            qT = S.sb([128, 4, 512], F32, stack=st)
            kT = S.sb([128, 4, 512], F32, stack=st)
            sgT = [S.sb([128, 512], F32, stack=st) for _ in range(2)]
            k_tm = [S.sb([128, 512], F32, stack=st) for _ in range(2)]
            f_tm = [S.sb([128, 512], F32, stack=st) for _ in range(2)]
            v_tm = [S.sb([128, 512], BF16, stack=st) for _ in range(2)]
            gwr = [S.sb([128, 512], F32, stack=st) for _ in range(5)]
            b_tm = [S.sb([128, 512], F32, stack=st) for _ in range(2)]
            g_tm = [S.sb([128, 512], F32, stack=st) for _ in range(2)]
            gsl = [S.sb([128, 512], F32, stack=st) for _ in range(2)]
            ocat = [S.sb([128, D], BF16, stack=st) for _ in range(3)]
            ngroups = self.dbg.get("ngroups", NG)
            ntiles = ngroups * 4
            tst = {}

            fronted = set()

            def group_front(g):
                fronted.add(g)
                self.norm_front(ctx, x_src, x_subs, g)

            def group_stage(g):
                if g not in fronted:
                    group_front(g)
                self.norm_back(ctx, l, 0, hT)
                for ch in range(4):
                    pb = self.pp.get()
                    for k in range(8):
                        S.mm(pb[:], win.s(k)[:, k, ch * 128:(ch + 1) * 128], hT[:, k, :], start=(k == 0), stop=(k == 7))
                    S.actf(qT[:, ch, :], pb[:], AF.Silu)
                    pb = self.pp.get()
                    for k in range(8):
                        S.mm(pb[:], win.s(k)[:, k, 512 + ch * 128:512 + (ch + 1) * 128], hT[:, k, :], start=(k == 0), stop=(k == 7))
                    sg = sgT[ch % 2]
                    S.actf(sg[:], pb[:], AF.Sigmoid)
                    S.ts(sg[:], sg[:], -1.0, 1.0, ALU.mult, ALU.add)
                    S.actf(kT[:, ch, :], sg[:], AF.Identity, scale=self.omlc[:, jl, ch:ch + 1])
                for j in range(4):
                    pb = self.pp.get()
                    for k in range(8):
                        S.mm(pb[:], hT[:, k, j * 128:(j + 1) * 128], win.s(k)[:, k, 1536:2048], start=(k == 0), stop=(k == 7))
                    gs_ = gsl[j % 2]
                    gw_ = gwr[(g * 4 + j) % 5]
                    S.actf(gs_[:], pb[:], AF.Silu)
                    S.tt(V(gw_.h[:].rearrange("p (h d) -> p h d", h=4), gw_.bufs), V(gs_.h[:].rearrange("p (h d) -> p h d", h=4), gs_.bufs),
                         V(normw.h[:].unsqueeze(1).to_broadcast([128, 4, 128]), normw.bufs), ALU.mult, eng=S.pool)

            def stage_P(t):
                g, j = divmod(t, 4)
                tsl = slice(j * 128, (j + 1) * 128)

                def tm(col0, n, dst_fn):
                    pb = self.pp.get()
                    for k in range(8):
                        S.mm(pb[:, 0:n], hT[:, k, tsl], win.s(k)[:, k, col0:col0 + n], start=(k == 0), stop=(k == 7))
                    dst_fn(pb)
                ktm = k_tm[t % 2]
                ftm = f_tm[t % 2]
                gtm = g_tm[t % 2]
                vtm = v_tm[t % 2]
                gw_ = gwr[(g * 4 + j) % 5]
                btm = b_tm[t % 2]

                def fgate(pb):
                    S.actf(ftm[:], pb[:], AF.Exp, scale=-1.0)
                    S.actf(btm[:], ftm[:], AF.Ln, bias=1.0)
                    if jl == 0:
                        S.actf(gtm[:], btm[:], AF.Identity, scale=-1.0)
                        S.actf(ktm[:], btm[:], AF.Exp, scale=-1.0)
                        S.ts(ktm[:], ktm[:], -1.0, 1.0, ALU.mult, ALU.add, eng=S.pool)
                    else:
                        S.tt(gtm[:], ftm[:], lbr[:], ALU.mult)
                        S.actf(ktm[:], btm[:], AF.Exp, scale=-1.0)
                        S.actf(gtm[:], gtm[:], AF.Ln, bias=1.0)
                        S.tt(ktm[:], ktm[:], ftm[:], ALU.mult, eng=S.pool)
                        S.tt(gtm[:], gtm[:], btm[:], ALU.subtract)
                        S.tt(ktm[:], ktm[:], omlr[:], ALU.mult, eng=S.pool)
                tm(512, 512, fgate)
                tm(1024, 512, lambda pb: S.copy(vtm[:], pb[:], eng=S.act))

                oc = ocat[t % 3]
                S.dma(S.sp, oc[:, 0:512], self.odiff.s(t)[t * 128:(t + 1) * 128, :])
                tst[t] = {"tsl": tsl, "ktm": ktm, "vtm": vtm, "gw": gw_, "gtm": gtm, "oc": oc}

            def stage_G1(t):
                d = tst[t]
                tsl = d["tsl"]
                d["gl"] = self.gl_front(glc, lambda ch: qT[:, ch, tsl], lambda ch: kT[:, ch, tsl], d["gtm"][:], d["ktm"][:], 128.0 ** -0.5)

            def stage_G2a(t):
                self.gl_scores(glc, tst[t]["gl"])

            def stage_G2b(t):
                d = tst[t]
                vtm = d["vtm"]
                self.gl_back(glc, d["gl"], lambda h: vtm[:, h * 128:(h + 1) * 128], d["gw"][:], d["oc"][:, 512:1024])

            def stage_Oa(t):
                self.out_stage_a(octx, t, tst[t]["oc"])

            def stage_Ob(t):
                self.out_stage_b(ctx, octx, _WSub(wout), x_src, x_subs, t, self.xres)
                del tst[t]

            if self.dbg.get("verbose"):
                print("sbuf remaining", self.nc.sbuf_bytes_remaining, flush=True)
            group_stage(0)
            stage_P(0)
            stage_G1(0)
            for t in range(ntiles):
                if t + 2 < ntiles and (t + 2) % 4 == 0:
                    group_front((t + 2) // 4)
                if t + 1 < ntiles:
                    if (t + 1) % 4 == 0:
                        group_stage((t + 1) // 4)
                    stage_P(t + 1)
                stage_G2a(t)
                if t >= 1:
                    stage_Oa(t - 1)
                if t + 1 < ntiles:
                    stage_G1(t + 1)
                stage_G2b(t)
                if t >= 1:
                    stage_Ob(t - 1)
            stage_Oa(ntiles - 1)
            stage_Ob(ntiles - 1)
        self.S.barrier()


class _WSub:
    def __init__(self, t):
        self.t = t

    def __getitem__(self, idx):
        k = idx[1]
        return V(self.t.h[idx], (self.t.bufs[k],))


def g_tm_slice(g_tm, ch):
    return V(g_tm.ap[:, ch * 128:(ch + 1) * 128], g_tm.bufs)


def qd_ap(q_, I):
    return q_.ap[:, 32 * I:32 * I + 32]


_CACHE = {}


def _col(v, n):
    return np.ascontiguousarray(np.asarray(v).reshape(n, 128).T)


def make_in_maps(inputs, ncores=8):
    f = lambda a: np.ascontiguousarray(np.asarray(a, dtype=np.float32))
    x = f(inputs["x"])
    c = f(inputs["c"])
    pos = np.ascontiguousarray(np.asarray(inputs["positions"], dtype=np.int32))
    mod_b = f(inputs["mod_b"])
    gate_w = f(inputs["gla_gate_w"])
    gate_b = f(inputs["gla_gate_b"])
    gatew_ext = np.zeros((2, 32, 256), np.float32)
    gatew_ext[:, 0:16, :] = gate_w
    gatew_ext[:, 16, :] = gate_b
    lb = f(inputs["hgrn_lb_logits"])
    shared = {
        "consts": CONSTS,
        "mod_w": f(inputs["mod_w"]),
        "mod_bc": np.stack([_col(mod_b[l], 48) for l in range(DEPTH)]),
        "mod_b": mod_b,
        "nmw_c": np.stack([_col(f(inputs["norm_mix_w"])[l], 8) for l in range(DEPTH)]),
        "nfw_c": np.stack([_col(f(inputs["norm_ffn_w"])[l], 8) for l in range(DEPTH)]),
        "ev_w_in": f(inputs["ev_w_in"]),
        "gatew_ext": gatew_ext,
        "gla_norm_w": f(inputs["gla_norm_w"]),
        "swa_sinks": f(inputs["swa_sinks"]),
        "ev_w_out": f(inputs["ev_w_out"]),
        "od_w_in": f(inputs["od_w_in"]),
        "diff_lambda": f(inputs["diff_lambda"]).reshape(2, 256),
        "diff_norm_w": f(inputs["diff_norm_w"]),
        "lb_c": np.ascontiguousarray(np.stack([_col(lb[j], 4) for j in range(2)], axis=1)),
        "lb_r": lb,
        "hgrn_norm_w": f(inputs["hgrn_norm_w"]),
        "od_w_out": f(inputs["od_w_out"]),
        "ffn_w_in": f(inputs["ffn_w_in"]),
        "convw_c": np.ascontiguousarray(np.stack([np.stack([_col(f(inputs["ffn_conv_w"])[l, j], NCHUNK_FF) for j in range(3)], axis=1) for l in range(DEPTH)])),
        "convb_c": np.stack([_col(f(inputs["ffn_conv_b"])[l], NCHUNK_FF) for l in range(DEPTH)]),
        "ffn_w_out": f(inputs["ffn_w_out"]),
        "final_norm_w": f(inputs["final_norm_w"]).reshape(1, D),
    }
    maps = []
    for b in range(ncores):
        m = dict(shared)
        m["x"] = x[b]
        m["c_t"] = _col(c[b], 8)
        m["pos"] = pos[b].reshape(1, SEQ)
        maps.append(m)
    return maps


def kernel(**inputs):
    if "prog" not in _CACHE:
        _CACHE["prog"] = Prog()
    prog = _CACHE["prog"]
    maps = make_in_maps(inputs, 8)
    res = run_bass_kernel_spmd(prog.nc, maps, core_ids=list(range(8)))
    return np.stack([np.asarray(r["out"], dtype=np.float32) for r in res.results], axis=0)
```

```python
import math
from contextlib import ExitStack
import numpy as np
import concourse.bass as bass
import concourse.mybir as mybir
from concourse.bass_utils import run_bass_kernel_spmd

F32 = mybir.dt.float32
BF16 = mybir.dt.bfloat16
I32 = mybir.dt.int32
ALU = mybir.AluOpType
AF = mybir.ActivationFunctionType
AX = mybir.AxisListType

D = 1024
SEQ = 4096
DEPTH = 4
DFF = 2816
NT = SEQ // 128
NG = SEQ // 512
EPS = 1e-6
EV_COLS = 2320
OD_COLS = 3584
NCHUNK_FF = DFF // 128
CM_N = 453
TWO_PI = 2.0 * math.pi
EPOCH = 30000


class Buf:
    __slots__ = ("w", "r", "strict")

    def __init__(self):
        self.w = None
        self.r = {}
        self.strict = False


class V:
    __slots__ = ("ap", "bufs")

    def __init__(self, ap, bufs):
        self.ap = ap
        self.bufs = bufs


class _Sub:
    def __init__(self, t, bufs):
        self.t = t
        self.bufs = bufs

    def __getitem__(self, idx):
        return V(self.t.h[idx], self.bufs)


class T:
    def __init__(self, h, nsub=1):
        self.h = h
        self.bufs = tuple(Buf() for _ in range(nsub))

    def __getitem__(self, idx):
        return V(self.h[idx], self.bufs)

    def s(self, i):
        return _Sub(self, (self.bufs[i],))

    def ss(self, idxs):
        return _Sub(self, tuple(self.bufs[i] for i in idxs))

    def v(self, ap, subs=None):
        return V(ap, self.bufs if subs is None else tuple(self.bufs[i] for i in subs))


class Eng:
    def __init__(self, S, name, h):
        self.S = S
        self.name = name
        self.h = h
        self.known = {}
        self.sem_id = S.new_sem(name)
        self.count = 0


class Sched:
    def __init__(self, nc, stack):
        self.nc = nc
        self.stack = stack
        self.sems = []
        self.pe = Eng(self, "pe", nc.tensor)
        self.act = Eng(self, "act", nc.scalar)
        self.dve = Eng(self, "dve", nc.vector)
        self.pool = Eng(self, "pool", nc.gpsimd)
        self.sp = Eng(self, "sp", nc.sync)
        self.engs = [self.pe, self.act, self.dve, self.pool, self.sp]
        self.dma_K = 8
        self.rings = {}
        for e in (self.sp, self.pool):
            self.rings[e.name] = {"ring": [{"sem": self.new_sem("dma" + e.name), "n": 0} for _ in range(self.dma_K)], "i": 0}
        self.nt = 0
        self.ninst = 0

    def new_sem(self, name):
        h = self.stack.enter_context(self.nc.semaphore("s%d_%s" % (len(self.sems), name)))
        self.sems.append(h)
        return len(self.sems) - 1

    def sb(self, shape, dtype, nsub=1, stack=None):
        self.nt += 1
        h = (stack or self.stack).enter_context(self.nc.sbuf_tensor("t%d" % self.nt, list(shape), dtype))
        return T(h, nsub)

    def ps(self, shape, dtype, stack=None):
        self.nt += 1
        h = (stack or self.stack).enter_context(self.nc.psum_tensor("p%d" % self.nt, list(shape), dtype))
        return T(h)

    def _need(self, eng, ev, waits):
        sid, val = ev
        if eng.known.get(sid, 0) >= val:
            return
        if waits.get(sid, 0) < val:
            waits[sid] = val

    def _collect(self, eng, reads, writes, my_sid):
        waits = {}
        for v in reads:
            for b in v.bufs:
                if b.w is not None and not (eng is self.pe and b.w[0] == my_sid):
                    self._need(eng, b.w, waits)
        pool_strict = eng is self.pool
        for v in writes:
            for b in v.bufs:
                strict = pool_strict or b.strict
                if b.w is not None and (b.w[0] != my_sid or strict):
                    self._need(eng, b.w, waits)
                for sid, val in b.r.items():
                    if sid != my_sid or strict:
                        self._need(eng, (sid, val), waits)
        return waits

    def _emit_waits(self, eng, waits):
        if eng.sem_id in waits:
            waits[eng.sem_id] = max(waits[eng.sem_id], eng.count - 3)
        for sid, val in waits.items():
            eng.h.wait_ge(self.sems[sid], val)
            eng.known[sid] = val
            self.ninst += 1

    def _record(self, ev, reads, writes):
        sid, val = ev
        for v in reads:
            for b in v.bufs:
                if b.r.get(sid, 0) < val:
                    b.r[sid] = val
        for v in writes:
            for b in v.bufs:
                b.w = ev
                b.r = {}

    def op(self, eng, fn, reads, writes):
        if eng.count >= EPOCH:
            eng.sem_id = self.new_sem(eng.name)
            eng.count = 0
        waits = self._collect(eng, reads, writes, eng.sem_id)
        self._emit_waits(eng, waits)
        ins = fn()
        eng.count += 1
        self.ninst += 1
        ins.then_inc(self.sems[eng.sem_id], 1)
        ev = (eng.sem_id, eng.count)
        self._record(ev, reads, writes)
        return ev

    def dma(self, eng, out, in_):
        rs = self.rings[eng.name]
        slot = rs["ring"][rs["i"] % self.dma_K]
        rs["i"] += 1
        sid = slot["sem"]
        waits = self._collect(eng, [in_], [out], -1)
        if slot["n"] > 0:
            self._need(eng, (sid, 16 * slot["n"]), waits)
        self._emit_waits(eng, waits)
        ins = eng.h.dma_start(out=out.ap, in_=in_.ap)
        slot["n"] += 1
        ins.then_inc(self.sems[sid], 16)
        self.ninst += 1
        ev = (sid, 16 * slot["n"])
        self._record(ev, [in_], [out])
        return ev

    def wait_all(self, eng):
        waits = {}
        for e in self.engs:
            if e.count > 0 and e is not eng:
                self._need(eng, (e.sem_id, e.count), waits)
        for rs in self.rings.values():
            for slot in rs["ring"]:
                if slot["n"] > 0:
                    self._need(eng, (slot["sem"], 16 * slot["n"]), waits)
        self._emit_waits(eng, waits)

    def barrier(self):
        for e in self.engs:
            self.wait_all(e)

    def mm(self, out, lhsT, rhs, start=True, stop=True):
        return self.op(self.pe, lambda: self.nc.tensor.matmul(out.ap, lhsT.ap, rhs.ap, start=start, stop=stop),
                       [lhsT, rhs], [out])

    def transpose(self, out, in_, ident):
        return self.op(self.pe, lambda: self.nc.tensor.transpose(out.ap, in_.ap, ident.ap), [in_, ident], [out])

    def actf(self, out, in_, func, bias=None, scale=None, accum_out=None):
        reads = [in_]
        kw = {}
        if bias is not None:
            if isinstance(bias, V):
                reads.append(bias)
                kw["bias"] = bias.ap
            else:
                kw["bias"] = bias
        if scale is not None:
            if isinstance(scale, V):
                reads.append(scale)
                kw["scale"] = scale.ap
            else:
                kw["scale"] = scale
        writes = [out]
        if accum_out is not None:
            writes.append(accum_out)
            kw["accum_out"] = accum_out.ap
        return self.op(self.act, lambda: self.nc.scalar.activation(out.ap, in_.ap, func, **kw), reads, writes)

    def tt(self, out, in0, in1, op, eng=None):
        e = eng or self.dve
        return self.op(e, lambda: e.h.tensor_tensor(out.ap, in0.ap, in1.ap, op), [in0, in1], [out])

    def ts(self, out, in0, s1, s2, op0, op1=None, eng=None):
        e = eng or self.dve
        reads = [in0]
        a1, a2 = s1, s2
        if isinstance(s1, V):
            reads.append(s1)
            a1 = s1.ap
        if isinstance(s2, V):
            reads.append(s2)
            a2 = s2.ap
        if op1 is None:
            return self.op(e, lambda: e.h.tensor_scalar(out.ap, in0.ap, a1, a2, op0), reads, [out])
        return self.op(e, lambda: e.h.tensor_scalar(out.ap, in0.ap, a1, a2, op0, op1), reads, [out])

    def stt(self, out, in0, scalar, in1, op0, op1, eng=None):
        e = self.dve
        reads = [in0, in1]
        a = scalar
        if isinstance(scalar, V):
            reads.append(scalar)
            a = scalar.ap
        return self.op(e, lambda: e.h.scalar_tensor_tensor(out.ap, in0.ap, a, in1.ap, op0, op1), reads, [out])

    def copy(self, out, in_, eng=None):
        e = eng or self.dve
        if e is self.act:
            return self.op(e, lambda: self.nc.scalar.copy(out.ap, in_.ap), [in_], [out])
        return self.op(e, lambda: e.h.tensor_copy(out.ap, in_.ap), [in_], [out])

    def memset(self, out, val, eng=None):
        e = eng or self.dve
        return self.op(e, lambda: e.h.memset(out.ap, val), [], [out])

    def reduce_sum(self, out, in_, eng=None):
        e = eng or self.dve
        return self.op(e, lambda: e.h.tensor_reduce(out.ap, in_.ap, AX.X, ALU.add), [in_], [out])

    def rstd(self, out, tmp, ss, scale):
        self.actf(tmp, ss, AF.Ln, scale=scale, bias=EPS)
        self.actf(out, tmp, AF.Exp, scale=-0.5)

    def recip(self, out, in_):
        return self.op(self.dve, lambda: self.nc.vector.reciprocal(out.ap, in_.ap), [in_], [out])


class PsumPool:
    def __init__(self, S, stack):
        self.S = S
        self.banks = [S.ps([128, 512], F32, stack=stack) for _ in range(8)]
        self.avail = list(range(8))
        self.i = 0

    def set_avail(self, lst):
        self.avail = list(lst)
        self.i = 0

    def get(self):
        b = self.banks[self.avail[self.i % len(self.avail)]]
        self.i += 1
        return b


def bf16v(t):
    return t.h[:].bitcast(BF16)


def _consts():
    ident = np.eye(128, dtype=np.float32)
    j = np.arange(128)[:, None]
    i = np.arange(128)[None, :]
    mask_cur = (j <= i).astype(np.float32)
    mask_prev = (j > i).astype(np.float32)
    cm = np.zeros((128, CM_N), np.float32)
    offs = [0, 32, 96, 192]
    jp = np.arange(128)
    for I in range(4):
        r = 32 * I - 1
        for jj in range(32 * (I + 1)):
            col = offs[I] + jj
            cm[(jp > jj) & (jp <= r), col] = 1.0
            cm[(jp > r) & (jp <= jj), col] = -1.0
    for ii in range(128):
        cm[(jp >= 32 * (ii // 32)) & (jp <= ii), 320 + ii] = 1.0
    for I in range(4):
        cm[jp <= 32 * I - 1, 448 + I] = 1.0
    cm[:, 452] = 1.0
    su = (j > i).astype(np.float32)
    perm = np.zeros((128, 128), np.float32)
    invf = np.zeros((128, 1), np.float32)
    sgn = np.zeros((128, 1), np.float32)
    inv_freq = (np.float32(500000.0) ** (-np.arange(0, 16, 2, dtype=np.float32) / np.float32(16))).astype(np.float32)
    for f in range(128):
        m = f % 64
        if m < 8:
            perm[f + 8, f] = 1.0
            invf[f] = inv_freq[m]
            sgn[f] = -1.0
        elif m < 16:
            perm[f - 8, f] = 1.0
            invf[f] = inv_freq[m - 8]
            sgn[f] = 1.0
    hm = np.zeros((128, 2), np.float32)
    hm[0:64, 0] = 1.0
    hm[64:128, 1] = 1.0
    parts = [ident, mask_cur, mask_prev, cm, su, perm, invf, sgn, hm]
    offsets = {}
    o = 0
    for name, p in zip(["ident", "mask_cur", "mask_prev", "cm", "su", "perm", "invf", "sgn", "hm"], parts):
        offsets[name] = (o, p.shape[1])
        o += p.shape[1]
    return np.concatenate(parts, axis=1), offsets


CONSTS, COFF = _consts()
NCONST = CONSTS.shape[1]


class Prog:
    def __init__(self, dbg=None):
        self.dbg = dbg or {}
        self.nc = bass.Bass("TRN2", target_bir_lowering=False)
        nc = self.nc
        self.din = {}

        def inp(name, shape, dt=F32):
            self.din[name] = T(nc.dram_tensor(name, list(shape), dt, kind="ExternalInput").ap())
            return self.din[name]

        inp("x", [SEQ, D])
        inp("c_t", [128, 8])
        inp("pos", [1, SEQ], I32)
        inp("consts", [128, NCONST])
        inp("mod_w", [DEPTH, D, 6 * D])
        inp("mod_bc", [DEPTH, 128, 48])
        inp("mod_b", [DEPTH, 6 * D])
        inp("nmw_c", [DEPTH, 128, 8])
        inp("nfw_c", [DEPTH, 128, 8])
        inp("ev_w_in", [2, D, EV_COLS])
        inp("gatew_ext", [2, 32, 256])
        inp("gla_norm_w", [2, 128])
        inp("swa_sinks", [2, 8])
        inp("ev_w_out", [2, D, D])
        inp("od_w_in", [2, D, OD_COLS])
        inp("diff_lambda", [2, 256])
        inp("diff_norm_w", [2, 128])
        inp("lb_c", [128, 2, 4])
        inp("lb_r", [2, 512])
        inp("hgrn_norm_w", [2, 128])
        inp("od_w_out", [2, D, D])
        inp("ffn_w_in", [DEPTH, D, 2 * DFF])
        inp("convw_c", [DEPTH, 128, 3, NCHUNK_FF])
        inp("convb_c", [DEPTH, 128, NCHUNK_FF])
        inp("ffn_w_out", [DEPTH, DFF, D])
        inp("final_norm_w", [1, D])
        self.out = T(nc.dram_tensor("out", [SEQ, D], F32, kind="ExternalOutput").ap(), NT)
        self.xres = T(nc.dram_tensor("xres", [SEQ, D], F32, kind="Internal").ap(), NT)
        self.ropetab = T(nc.dram_tensor("ropetab", [2, 128, SEQ], F32, kind="Internal").ap(), 2)
        self.gates = T(nc.dram_tensor("gates", [DEPTH * 2, D], F32, kind="Internal").ap(), DEPTH * 2)
        self.odiff = T(nc.dram_tensor("odiff", [SEQ, 512], BF16, kind="Internal").ap(), NT)

        with ExitStack() as st:
            self.S = Sched(nc, st)
            self.pp = PsumPool(self.S, st)
            self.build(st)
            self.S.barrier()

    def cv(self, name, rows=128):
        o, n = COFF[name]
        return self.cst[0:rows, o:o + n]

    def build(self, st):
        S = self.S
        nc = self.nc
        self.cst = S.sb([128, NCONST], F32)
        S.dma(S.sp, self.cst[:], self.din["consts"][:])
        self.ident = S.sb([128, 128], BF16)
        S.copy(self.ident[:], self.cv("ident"))
        self.mask_cur = S.sb([128, 128], BF16)
        S.copy(self.mask_cur[:], self.cv("mask_cur"))
        self.mask_prev = S.sb([128, 128], BF16)
        S.copy(self.mask_prev[:], self.cv("mask_prev"))
        self.mb_cur = S.sb([128, 128], BF16)
        S.ts(self.mb_cur[:], self.cv("mask_cur"), 30000.0, -30000.0, ALU.mult, ALU.add)
        self.mb_prev = S.sb([128, 128], BF16)
        S.ts(self.mb_prev[:], self.cv("mask_prev"), 30000.0, -30000.0, ALU.mult, ALU.add)
        self.modc = S.sb([128, DEPTH, 4, 8], F32)
        self.lbc = S.sb([128, 2, 4], F32)
        self.omlc = S.sb([128, 2, 4], F32)
        for b in self.pp.banks:
            S.memset(b[:], 0.0)

        self.prologue()
        x_src = self.din["x"]
        x_src_subs = False
        nlayers = self.dbg.get("nlayers", DEPTH)
        for l in range(nlayers):
            if self.dbg.get("skip_mixer"):
                pass
            elif l % 2 == 0:
                self.even_pass(l, x_src, x_src_subs)
                x_src, x_src_subs = self.xres, True
            else:
                self.odd_pass_a(l, x_src, x_src_subs)
                self.odd_pass_b(l, x_src, x_src_subs)
                x_src, x_src_subs = self.xres, True
            if not self.dbg.get("skip_ffn"):
                self.ffn_pass(l, x_src, x_src_subs)
                x_src, x_src_subs = self.xres, True
        self.final_pass(x_src, x_src_subs)

    def xv(self, src, subs, t):
        ap = src.h[t * 128:(t + 1) * 128, :]
        return src.v(ap, [t] if subs else None)

    def prologue(self):
        S = self.S
        nc = self.nc
        with ExitStack() as st:
            ct = S.sb([128, 8], F32, stack=st)
            S.dma(S.sp, ct[:], self.din["c_t"][:])
            cact = S.sb([128, 8], F32, stack=st)
            S.actf(cact[:], ct[:], AF.Silu)
            cact_b = S.sb([128, 8], BF16, stack=st)
            S.copy(cact_b[:], cact[:])
            cact_bc = S.sb([128, 8, 128], BF16, stack=st)
            S.copy(cact_bc[:], V(cact.h[:].unsqueeze(2).to_broadcast([128, 8, 128]), cact.bufs))
            mw = [S.sb([128, 8, 1024], BF16, stack=st) for _ in range(2)]
            modbc = S.sb([128, DEPTH, 48], F32, stack=st)
            S.dma(S.sp, modbc[:], V(self.din["mod_bc"].h.rearrange("l p c -> p l c"), self.din["mod_bc"].bufs))
            nmw = S.sb([128, DEPTH, 8], F32, stack=st)
            S.dma(S.sp, nmw[:], V(self.din["nmw_c"].h.rearrange("l p c -> p l c"), self.din["nmw_c"].bufs))
            nfw = S.sb([128, DEPTH, 8], F32, stack=st)
            S.dma(S.sp, nfw[:], V(self.din["nfw_c"].h.rearrange("l p c -> p l c"), self.din["nfw_c"].bufs))
            gb = [S.sb([128, 1024], F32, stack=st) for _ in range(2)]
            grow = [S.sb([128, 1024], F32, stack=st) for _ in range(2)]
            it = 0
            nl = self.dbg.get("nlayers", DEPTH)
            for l in range(nl):
                for piece in range(6):
                    w = mw[it % 2]
                    it += 1
                    src = self.din["mod_w"].h[l, :, piece * 1024:(piece + 1) * 1024].rearrange("(k p) n -> p k n", p=128)
                    S.dma(S.pool, w[:], V(src, self.din["mod_w"].bufs))
                    if piece in (2, 5):
                        gi = 0 if piece == 2 else 1
                        S.dma(S.sp, gb[gi][:], V(self.din["mod_b"].h[l:l + 1, piece * 1024:(piece + 1) * 1024].partition_broadcast(128), self.din["mod_b"].bufs))
                        for n in range(2):
                            pb = self.pp.get()
                            for k in range(8):
                                S.mm(pb[:], cact_bc[:, k, :], w[:, k, n * 512:(n + 1) * 512], start=(k == 0), stop=(k == 7))
                            S.tt(grow[gi][:, n * 512:(n + 1) * 512], pb[:], gb[gi][:, n * 512:(n + 1) * 512], ALU.add)
                        S.dma(S.sp, self.gates.s(l * 2 + gi)[l * 2 + gi:l * 2 + gi + 1, :], grow[gi][0:1, :])
                    else:
                        kind = {0: 0, 1: 1, 3: 2, 4: 3}[piece]
                        pb = self.pp.get()
                        for cch in range(8):
                            for k in range(8):
                                S.mm(pb[:, cch:cch + 1], w[:, k, cch * 128:(cch + 1) * 128], cact_b[:, k:k + 1], start=(k == 0), stop=(k == 7))
                        S.tt(self.modc[:, l, kind, :], pb[:, 0:8], modbc[:, l, piece * 8:(piece + 1) * 8], ALU.add)
                        if kind in (1, 3):
                            nw = nmw if kind == 1 else nfw
                            S.stt(self.modc[:, l, kind, :], self.modc[:, l, kind, :], 1.0, nw[:, l, :], ALU.add, ALU.mult)
            lc = S.sb([128, 2, 4], F32, stack=st)
            S.dma(S.sp, lc[:], self.din["lb_c"][:])
            S.memset(self.lbc[:], 0.0)
            dlt = S.sb([128, 4], F32, stack=st)
            S.tt(dlt[:], lc[:, 1, :], lc[:, 0, :], ALU.subtract)
            S.actf(self.lbc[:, 1, :], dlt[:], AF.Sigmoid)
            S.ts(self.omlc[:], self.lbc[:], -1.0, 1.0, ALU.mult, ALU.add)
        with ExitStack() as st:
            posi = S.sb([128, 1024], I32, stack=st)
            ang = S.sb([128, 1024], F32, stack=st)
            kf = S.sb([128, 1024], F32, stack=st)
            ki = S.sb([128, 1024], I32, stack=st)
            r = S.sb([128, 1024], F32, stack=st)
            ra = S.sb([128, 1024], F32, stack=st)
            cs = S.sb([128, 1024], F32, stack=st)
            sn = S.sb([128, 1024], F32, stack=st)
            C1 = 6.28125
            C2 = TWO_PI - C1
            for q in range(4):
                sl = slice(q * 1024, (q + 1) * 1024)
                S.dma(S.sp, posi[:], V(self.din["pos"].h[0:1, sl].partition_broadcast(128), self.din["pos"].bufs))
                S.copy(ang[:], posi[:])
                S.ts(ang[:], ang[:], self.cv("invf"), None, ALU.mult)
                S.ts(kf[:], ang[:], 1.0 / TWO_PI, None, ALU.mult)
                S.copy(ki[:], kf[:])
                S.copy(kf[:], ki[:])
                S.stt(r[:], kf[:], -C1, ang[:], ALU.mult, ALU.add)
                S.stt(r[:], kf[:], -C2, r[:], ALU.mult, ALU.add)
                S.ts(r[:], r[:], -math.pi, math.pi, ALU.max, ALU.min)
                S.actf(sn[:], r[:], AF.Sin)
                S.ts(sn[:], sn[:], self.cv("sgn"), None, ALU.mult)
                S.actf(ra[:], r[:], AF.Abs)
                S.ts(ra[:], ra[:], -1.0, math.pi / 2, ALU.mult, ALU.add)
                S.actf(cs[:], ra[:], AF.Sin)
                S.dma(S.sp, self.ropetab.s(0)[0, :, sl], cs[:])
                S.dma(S.sp, self.ropetab.s(1)[1, :, sl], sn[:])

    def load_w(self, wt, src_t, src_ap_fn, nk):
        for k in range(nk):
            self.S.dma(self.S.pool, wt.s(k)[:, k, :], V(src_ap_fn(k), src_t.bufs))

    def norm_front(self, ctx, x_src, x_subs, g):
        S = self.S
        ss = ctx["ss"][g % 2]
        nb = ctx["batch"]
        S.memset(ss[:, 0, :], 0.0, eng=S.pool)
        for j0 in range(0, 4, nb):
            xts = []
            for j in range(j0, j0 + nb):
                t = g * 4 + j
                xt = ctx["xring"][ctx["xi"] % len(ctx["xring"])]
                ctx["xi"] += 1
                S.dma(S.sp, xt[:], self.xv(x_src, x_subs, t))
                S.actf(ctx["sq"][j % len(ctx["sq"])][:], xt[:], AF.Square, accum_out=ss[:, 0, j:j + 1])
                xts.append(xt)
            S.rstd(ss[:, 2, j0:j0 + nb], ss[:, 1, j0:j0 + nb], ss[:, 0, j0:j0 + nb], 1.0 / D)
            for j in range(j0, j0 + nb):
                S.ts(ctx["xn"][j][:], xts[j - j0][:], ss[:, 2, j:j + 1], None, ALU.mult)

    def norm_back(self, ctx, l, which, hT):
        S = self.S
        kind_sh, kind_w = (0, 1) if which == 0 else (2, 3)
        for j in range(4):
            xn = ctx["xn"][j]
            pb = self.pp.get()
            pv = bf16v(pb)
            for k in range(8):
                S.transpose(V(pv[:, k * 128:(k + 1) * 128], pb.bufs), xn[:, k * 128:(k + 1) * 128], self.ident[:])
            for k in range(8):
                o = hT[:, k, j * 128:(j + 1) * 128]
                i_ = V(pv[:, k * 128:(k + 1) * 128], pb.bufs)
                if k % 2 == 0:
                    S.actf(o, i_, AF.Identity, scale=self.modc[:, l, kind_w, k:k + 1], bias=self.modc[:, l, kind_sh, k:k + 1])
                else:
                    S.ts(o, i_, self.modc[:, l, kind_w, k:k + 1], self.modc[:, l, kind_sh, k:k + 1], ALU.mult, ALU.add)

    def norm_stage(self, ctx, l, which, x_src, x_subs, g, hT):
        self.norm_front(ctx, x_src, x_subs, g)
        self.norm_back(ctx, l, which, hT)

    def norm_ctx(self, st, nx=4):
        S = self.S
        ctx = {"_": None, "xring": [S.sb([128, D], F32, stack=st) for _ in range(nx)], "xi": 0, "batch": 4 if nx >= 4 else 2,
               "sq": [S.sb([128, D], BF16, stack=st) for _ in range(2 if nx >= 4 else 1)],
               "ss": [S.sb([128, 3, 4], F32, stack=st) for _ in range(2)],
               "xn": [S.sb([128, D], BF16, stack=st) for _ in range(4)]}
        for t_ in ctx["sq"]:
            t_.bufs[0].strict = True
        return ctx

    def out_stage_a(self, octx, t, ocat):
        S = self.S
        pb = self.pp.get()
        pv = bf16v(pb)
        for k in range(8):
            S.transpose(V(pv[:, k * 128:(k + 1) * 128], pb.bufs), ocat[:, k * 128:(k + 1) * 128], self.ident[:])
        oT = octx["oT"][t % 2]
        S.copy(oT[:], V(pv, pb.bufs), eng=S.act)

    def out_stage_b(self, ctx, octx, wout, x_src, x_subs, t, dst):
        S = self.S
        oT = octx["oT"][t % 2]
        xt = ctx["xring"][ctx["xi"] % len(ctx["xring"])]
        ctx["xi"] += 1
        S.dma(S.sp, xt[:], self.xv(x_src, x_subs, t))
        for n in range(2):
            py = self.pp.get()
            for k in range(8):
                S.mm(py[:], oT[:, k * 128:(k + 1) * 128], wout[:, k, n * 512:(n + 1) * 512], start=(k == 0), stop=(k == 7))
            tmp = octx["tmp"][n]
            S.tt(tmp[:], py[:], octx["gate"][:, n * 512:(n + 1) * 512], ALU.mult)
            S.tt(xt[:, n * 512:(n + 1) * 512], tmp[:], xt[:, n * 512:(n + 1) * 512], ALU.add, eng=S.pool)
        S.dma(S.pool, self.xv(dst, True, t), xt[:])

    def out_ctx(self, st, l, gi):
        S = self.S
        octx = {"oT": [S.sb([128, D], BF16, stack=st) for _ in range(2)],
                "tmp": [S.sb([128, 512], F32, stack=st) for _ in range(2)],
                "gate": S.sb([128, D], F32, stack=st)}
        S.dma(S.sp, octx["gate"][:], V(self.gates.h[l * 2 + gi:l * 2 + gi + 1, :].partition_broadcast(128), (self.gates.bufs[l * 2 + gi],)))
        return octx

    def rope_stage(self, rctx, src_ps, dst, g):
        S = self.S
        qf = rctx["qf"][rctx["i"] % 2]
        t1 = rctx["t1"][rctx["i"] % 2]
        rctx["i"] += 1
        S.copy(qf[:], src_ps, eng=S.act)
        prev = rctx.get("pending")
        rctx["pending"] = (qf, t1, dst, rctx["tab"])
        if prev is not None:
            self.rope_finish(prev)

    def rope_finish(self, item):
        S = self.S
        qf, t1, dst, tab = item
        pr = self.pp.get()
        S.mm(pr[:], self.cv("perm"), qf[:])
        S.tt(t1[:], qf[:], tab[:, 0, :], ALU.mult, eng=S.pool)
        S.tt(qf[:], pr[:], tab[:, 1, :], ALU.mult)
        if isinstance(dst, list):
            for (psl, d) in dst:
                S.tt(d, t1[psl, :], qf[psl, :], ALU.add)
        else:
            S.tt(dst, t1[:], qf[:], ALU.add)

    def rope_flush(self, rctx):
        prev = rctx.get("pending")
        rctx["pending"] = None
        if prev is not None:
            self.rope_finish(prev)

    def rope_ctx(self, st):
        S = self.S
        return {"qf": [S.sb([128, 512], F32, stack=st) for _ in range(2)],
                "t1": [S.sb([128, 512], F32, stack=st) for _ in range(2)],
                "tabs": [S.sb([128, 2, 512], F32, stack=st) for _ in range(2)], "i": 0, "tab": None}

    def rope_load(self, rctx, g):
        S = self.S
        tab = rctx["tabs"][g % 2]
        S.dma(S.sp, tab[:, 0, :], self.ropetab.s(0)[0, :, g * 512:(g + 1) * 512])
        S.dma(S.sp, tab[:, 1, :], self.ropetab.s(1)[1, :, g * 512:(g + 1) * 512])
        rctx["tab"] = tab

    def gl_ctx(self, st, nch, dk):
        S = self.S
        F = nch * 128
        c = {"nch": nch, "dk": dk, "F": F,
             "expE": [S.sb([128, CM_N], F32, stack=st) for _ in range(2)],
             "KO": [S.sb([128, nch, 4, 128], BF16, stack=st) for _ in range(2)],
             "qd": [S.sb([128, nch, 128 // dk, 128], BF16, stack=st) for _ in range(2)],
             "qh": [S.sb([128, nch, 128 // dk, 128], BF16, stack=st) for _ in range(2)],
             "qhf": [S.sb([128, 128], F32, stack=st) for _ in range(2)],
             "ek": [S.sb([128, F], F32, stack=st) for _ in range(2)],
             "khat": [S.sb([128, F], BF16, stack=st) for _ in range(2)],
             "A": [S.sb([128, 4, 128], BF16, stack=st) for _ in range(2)],
             "Sf": S.sb([128, nch, 128], F32, stack=st),
             "Sb": [S.sb([128, nch, 128], BF16, stack=st) for _ in range(2)],
             "etot": [S.sb([128, nch], F32, stack=st) for _ in range(2)],
             "sq": S.sb([128, 512], F32, stack=st),
             "ssum": [S.sb([128, 8], F32, stack=st) for _ in range(2)],
             "on": S.sb([128, 512], F32, stack=st),
             "i": 0}
        for t in c["KO"] + c["qd"] + c["qh"]:
            S.memset(t[:], 0.0)
        S.memset(c["Sf"][:], 0.0)
        for t in c["Sb"]:
            S.memset(t[:], 0.0)
        return c

    def gl_front(self, c, qT, kT, g_tm, k_tm, qscale):
        S = self.S
        nch, dk, F = c["nch"], c["dk"], c["F"]
        hpc = 128 // dk
        i = c["i"]
        c["i"] += 1
        stt_ = {"i": i, "KO": c["KO"][i % 2], "qd": c["qd"][i % 2], "qh": c["qh"][i % 2],
                "khat": c["khat"][i % 2], "etot": c["etot"][i % 2]}
        KO, qd, qh, etot = stt_["KO"], stt_["qd"], stt_["qh"], stt_["etot"]
        offs = [0, 32, 96, 192]
        pk = self.pp.get()
        S.mm(pk[:, 0:F], self.cv("su"), g_tm)
        pes = []
        for ch in range(nch):
            pe_ = self.pp.get()
            S.mm(pe_[:, 0:CM_N], g_tm_slice(g_tm, ch), self.cv("cm"))
            pes.append(pe_)
        ek = c["ek"][i % 2]
        S.actf(ek[:], pk[:, 0:F], AF.Exp)
        S.tt(stt_["khat"][:], ek[:], k_tm, ALU.mult, eng=S.pool)
        for ch in range(nch):
            X = c["expE"][(i * nch + ch) % 2]
            S.actf(X[:], pes[ch][:, 0:CM_N], AF.Exp)
            S.copy(etot[:, ch:ch + 1], X[:, 452:453], eng=S.pool)
            q_ = qT(ch)
            k_ = kT(ch)
            for hh in range(hpc):
                psl = slice(hh * dk, (hh + 1) * dk)
                S.stt(qd[psl, ch, hh, :], V(q_.ap[psl, :], q_.bufs), qscale, X[psl, 320:448], ALU.mult, ALU.mult)
            for I in range(4):
                n = 32 * (I + 1)
                S.tt(KO[:, ch, I, 0:n], V(k_.ap[:, 0:n], k_.bufs), X[:, offs[I]:offs[I] + n], ALU.mult)
            if qscale != 1.0:
                S.ts(X[:, 448:452], X[:, 448:452], qscale, None, ALU.mult)
            for I in range(4):
                for hh in range(hpc):
                    psl = slice(hh * dk, (hh + 1) * dk)
                    S.stt(qh[psl, ch, hh, 32 * I:32 * I + 32], V(q_.ap[psl, 32 * I:32 * I + 32], q_.bufs), X[psl, 448 + I:449 + I],
                          X[psl, 320 + 32 * I:352 + 32 * I], ALU.mult, ALU.mult)
        return stt_

    def gl_scores(self, c, stt_):
        S = self.S
        nch, dk = c["nch"], c["dk"]
        hpc = 128 // dk
        KO, qd = stt_["KO"], stt_["qd"]
        psc = self.pp.get()
        for ch in range(nch):
            for hh in range(hpc):
                h = ch * hpc + hh
                for I in range(4):
                    S.mm(psc[:, h * 128 + 32 * I:h * 128 + 32 * I + 32], KO[:, ch, I, :], qd[:, ch, hh, 32 * I:32 * I + 32])
        A = c["A"][stt_["i"] % 2]
        S.tt(A[:], V(psc.h[:].rearrange("p (h i) -> p h i", h=4), psc.bufs),
             V(self.mask_cur.h[:].unsqueeze(1).to_broadcast([128, 4, 128]), self.mask_cur.bufs), ALU.mult)
        stt_["A"] = A

    def gl_back(self, c, stt_, v_tm, gw, out_bf):
        S = self.S
        nch, dk = c["nch"], c["dk"]
        hpc = 128 // dk
        i = stt_["i"]
        A, qh, khat, etot = stt_["A"], stt_["qh"], stt_["khat"], stt_["etot"]
        Sb_prev = c["Sb"][(i + 1) % 2]
        Sb_new = c["Sb"][i % 2]
        po = self.pp.get()
        for h in range(4):
            ch, hh = divmod(h, hpc)
            S.mm(po[:, h * 128:(h + 1) * 128], A[:, h, :], v_tm(h), start=True, stop=False)
            S.mm(po[:, h * 128:(h + 1) * 128], qh[:, ch, hh, :], Sb_prev[:, ch, :], start=False, stop=True)
        pS = self.pp.get()
        for h in range(4):
            ch, hh = divmod(h, hpc)
            ps_ = slice(hh * dk, (hh + 1) * dk)
            if hpc == 1:
                S.mm(pS[:, h * 128:(h + 1) * 128], khat[:, h * 128:(h + 1) * 128], v_tm(h))
            else:
                S.mm(pS[ps_, ch * 128:(ch + 1) * 128], khat[:, h * dk:(h + 1) * dk], v_tm(h))
        for ch in range(nch):
            S.stt(c["Sf"][:, ch, :], c["Sf"][:, ch, :], etot[:, ch:ch + 1], pS[:, ch * 128:(ch + 1) * 128], ALU.mult, ALU.add)
        S.copy(Sb_new[:], c["Sf"][:], eng=S.act)
        sq = c["sq"]
        S.actf(sq[:], po[:], AF.Square)
        ssum = c["ssum"][i % 2]
        S.reduce_sum(ssum[:, 0:4], V(sq.h[:].rearrange("p (h d) -> p h d", h=4), sq.bufs))
        S.rstd(ssum[:, 0:4], ssum[:, 4:8], ssum[:, 0:4], 1.0 / 128)
        on = c["on"]
        S.tt(V(on.h[:].rearrange("p (h d) -> p h d", h=4), on.bufs), V(po.h[:].rearrange("p (h d) -> p h d", h=4), po.bufs),
             V(ssum.h[:, 0:4].unsqueeze(2).to_broadcast([128, 4, 128]), ssum.bufs), ALU.mult)
        S.tt(out_bf, on[:], gw, ALU.mult, eng=S.pool)

    def ffn_pass(self, l, x_src, x_subs):
        S = self.S
        self.S.barrier()
        self.pp.set_avail(range(8))
        with ExitStack() as st:
            w1 = S.sb([128, 8, 2 * DFF], BF16, nsub=12, stack=st)
            w2 = S.sb([128, NCHUNK_FF, D], BF16, nsub=NCHUNK_FF, stack=st)
            fw = self.din["ffn_w_in"]
            for blk in range(6):
                c0 = blk * 4
                ncol = min(4, NCHUNK_FF - c0) * 128
                for part in range(2):
                    col0 = part * DFF + c0 * 128
                    S.dma(S.pool, w1.s(blk * 2 + part)[:, :, col0:col0 + ncol],
                          V(fw.h[l, :, col0:col0 + ncol].rearrange("(k p) n -> p k n", p=128), fw.bufs))
            self.load_w(w2, self.din["ffn_w_out"], lambda k: self.din["ffn_w_out"].h[l, k * 128:(k + 1) * 128, :], NCHUNK_FF)
            cw = S.sb([128, 3, NCHUNK_FF], F32, stack=st)
            S.dma(S.sp, cw[:], self.din["convw_c"][l])
            cb = S.sb([128, NCHUNK_FF], F32, stack=st)
            S.dma(S.sp, cb[:], self.din["convb_c"][l])
            ctx = self.norm_ctx(st, nx=3)
            octx_gate = S.sb([128, D], F32, stack=st)
            S.dma(S.sp, octx_gate[:], V(self.gates.h[l * 2 + 1:l * 2 + 2, :].partition_broadcast(128), (self.gates.bufs[l * 2 + 1],)))
            hT = S.sb([128, 8, 512], BF16, stack=st)
            gT = S.sb([128, NCHUNK_FF, 512], BF16, stack=st)
            abuf = [S.sb([128, 514], F32, stack=st) for _ in range(2)]
            halo = S.sb([128, NCHUNK_FF, 2], F32, stack=st)
            S.memset(halo[:], 0.0)
            tcv = [S.sb([128, 512], F32, stack=st) for _ in range(2)]
            tsl = [S.sb([128, 512], F32, stack=st) for _ in range(2)]
            tmp = tcv
            ngroups = self.dbg.get("ngroups", NG)
            if self.dbg.get("verbose"):
                print("ffn sbuf remaining", self.nc.sbuf_bytes_remaining, flush=True)

            def ytile(g, j):
                t = g * 4 + j
                xt = ctx["xring"][ctx["xi"] % len(ctx["xring"])]
                ctx["xi"] += 1
                S.dma(S.sp, xt[:], self.xv(x_src, x_subs, t))
                for n in range(2):
                    py = self.pp.get()
                    for c in range(NCHUNK_FF):
                        S.mm(py[:], gT[:, c, j * 128:(j + 1) * 128], w2.s(c)[:, c, n * 512:(n + 1) * 512], start=(c == 0), stop=(c == NCHUNK_FF - 1))
                    S.tt(tmp[n][:], py[:], octx_gate[:, n * 512:(n + 1) * 512], ALU.mult)
                    S.tt(xt[:, n * 512:(n + 1) * 512], tmp[n][:], xt[:, n * 512:(n + 1) * 512], ALU.add, eng=S.pool)
                S.dma(S.pool, self.xv(self.xres, True, t), xt[:])

            self.norm_stage(ctx, l, 1, x_src, x_subs, 0, hT)
            for g in range(ngroups):
                for c in range(NCHUNK_FF):
                    pa = self.pp.get()
                    for k in range(8):
                        S.mm(pa[:], w1.s((c // 4) * 2)[:, k, c * 128:(c + 1) * 128], hT[:, k, :], start=(k == 0), stop=(k == 7))
                    pu = self.pp.get()
                    for k in range(8):
                        S.mm(pu[:], w1.s((c // 4) * 2 + 1)[:, k, DFF + c * 128:DFF + (c + 1) * 128], hT[:, k, :], start=(k == 0), stop=(k == 7))
                    ab = abuf[c % 2]
                    S.copy(ab[:, 0:2], halo[:, c, :], eng=S.pool)
                    S.copy(ab[:, 2:514], pa[:], eng=S.act)
                    S.copy(halo[:, c, :], ab[:, 512:514], eng=S.pool)
                    tc_ = tcv[c % 2]
                    S.ts(tc_[:], ab[:, 2:514], cw[:, 2, c:c + 1], cb[:, c:c + 1], ALU.mult, ALU.add)
                    S.stt(tc_[:], ab[:, 1:513], cw[:, 1, c:c + 1], tc_[:], ALU.mult, ALU.add)
                    S.stt(tc_[:], ab[:, 0:512], cw[:, 0, c:c + 1], tc_[:], ALU.mult, ALU.add)
                    ts_ = tsl[c % 2]
                    S.actf(ts_[:], tc_[:], AF.Silu)
                    S.tt(gT[:, c, :], ts_[:], pu[:], ALU.mult)
                nxt = g + 1 < ngroups
                if nxt:
                    self.norm_front(ctx, x_src, x_subs, g + 1)
                ytile(g, 0)
                ytile(g, 1)
                if nxt:
                    self.norm_back(ctx, l, 1, hT)
                ytile(g, 2)
                ytile(g, 3)
        self.S.barrier()

    def final_pass(self, x_src, x_subs):
        S = self.S
        self.S.barrier()
        with ExitStack() as st:
            wf = S.sb([128, D], F32, stack=st)
            S.dma(S.sp, wf[:], V(self.din["final_norm_w"].h[0:1, :].partition_broadcast(128), self.din["final_norm_w"].bufs))
            xr = [S.sb([128, D], F32, stack=st) for _ in range(3)]
            sq = S.sb([128, D], BF16, stack=st)
            ss = [S.sb([128, 4], F32, stack=st) for _ in range(2)]
            ntiles = self.dbg.get("ngroups", NG) * 4
            for t in range(ntiles):
                xt = xr[t % 3]
                S.dma(S.sp, xt[:], self.xv(x_src, x_subs, t))
                s_ = ss[t % 2]
                S.memset(s_[:, 0:1], 0.0, eng=S.pool)
                S.actf(sq[:], xt[:], AF.Square, accum_out=s_[:, 0:1])
                S.rstd(s_[:, 2:3], s_[:, 1:2], s_[:, 0:1], 1.0 / D)
                S.stt(xt[:], xt[:], s_[:, 2:3], wf[:], ALU.mult, ALU.mult)
                S.dma(S.pool, self.xv(self.out, True, t), xt[:])

    def even_pass(self, l, x_src, x_subs):
        S = self.S
        jl = l // 2
        self.S.barrier()
        self.pp.set_avail(range(8))
        with ExitStack() as st:
            win = S.sb([128, 8, EV_COLS], BF16, nsub=8, stack=st)
            wout = S.sb([128, 8, D], BF16, nsub=8, stack=st)
            self.load_w(win, self.din["ev_w_in"], lambda k: self.din["ev_w_in"].h[jl, k * 128:(k + 1) * 128, :], 8)
            self.load_w(wout, self.din["ev_w_out"], lambda k: self.din["ev_w_out"].h[jl, k * 128:(k + 1) * 128, :], 8)
            gwx = S.sb([32, 256], BF16, stack=st)
            S.dma(S.pool, gwx[:], self.din["gatew_ext"][jl])
            normw = S.sb([128, 128], F32, stack=st)
            S.dma(S.sp, normw[:], V(self.din["gla_norm_w"].h[jl:jl + 1, :].partition_broadcast(128), self.din["gla_norm_w"].bufs))
            esink = S.sb([128, 8], F32, stack=st)
            S.dma(S.sp, esink[:], V(self.din["swa_sinks"].h[jl:jl + 1, :].partition_broadcast(128), self.din["swa_sinks"].bufs))
            S.actf(esink[:], esink[:], AF.Exp)
            ctx = self.norm_ctx(st, nx=3)
            octx = self.out_ctx(st, l, 0)
            rctx = self.rope_ctx(st)
            glc = self.gl_ctx(st, 2, 64)
            hT = S.sb([128, 8, 512], BF16, stack=st)
            qT = S.sb([128, 2, 512], F32, stack=st)
            kT = S.sb([128, 2, 512], F32, stack=st)
            glrT = S.sb([32, 512], BF16, stack=st)
            S.memset(glrT[:], 1.0)
            sqT = [S.sb([128, 8, 512], BF16, stack=st) for _ in range(2)]
            for q__ in sqT:
                S.memset(q__[:], 0.0)
            skT = [S.sb([128, 2, 512], BF16, stack=st) for _ in range(2)]
            vext = [S.sb([128, 2, 65], BF16, stack=st) for _ in range(3)]
            for v_ in vext:
                S.memset(v_[:], 1.0)
            k_tm = [S.sb([128, 256], F32, stack=st) for _ in range(2)]
            v_tm = [S.sb([128, 512], BF16, stack=st) for _ in range(2)]
            gwr = [S.sb([128, 512], F32, stack=st) for _ in range(5)]
            g_tm = [S.sb([128, 256], F32, stack=st) for _ in range(2)]
            ez = [S.sb([128, 256], F32, stack=st) for _ in range(2)]
            gsl = [S.sb([128, 512], F32, stack=st) for _ in range(2)]
            ocat = [S.sb([128, D], BF16, stack=st) for _ in range(2)]
            PT = [S.sb([128, 4, 128], BF16, stack=st) for _ in range(4)]
            den = [S.sb([128, 8], F32, stack=st) for _ in range(2)]
            ngroups = self.dbg.get("ngroups", NG)
            ntiles = ngroups * 4
            tst = {}

            fronted = set()

            def group_front(g):
                fronted.add(g)
                self.rope_load(rctx, g)
                self.norm_front(ctx, x_src, x_subs, g)

            def group_stage(g):
                if g not in fronted:
                    group_front(g)
                self.norm_back(ctx, l, 0, hT)
                sq_ = sqT[g % 2]

                def fm(col0, m, dst_fn):
                    pb = self.pp.get()
                    for k in range(8):
                        S.mm(pb[0:m, :], win.s(k)[:, k, col0:col0 + m], hT[:, k, :], start=(k == 0), stop=(k == 7))
                    dst_fn(pb)
                for ch in range(2):
                    fm(ch * 128, 128, lambda pb, ch=ch: S.actf(qT[:, ch, :], pb[:], AF.Copy, scale=0.125))
                    fm(256 + ch * 128, 128, lambda pb, ch=ch: S.copy(kT[:, ch, :], pb[:], eng=S.act))
                fm(1536, 16, lambda pb: S.copy(glrT[0:16, :], pb[0:16, :], eng=S.act))
                for ch in range(4):
                    fm(1552 + ch * 128, 128, lambda pb, ch=ch: self.rope_stage(rctx, pb[:], [(slice(0, 64), sq_[0:64, 2 * ch, :]), (slice(64, 128), sq_[64:128, 2 * ch + 1, :])], g))
                skt = skT[g % 2]
                for kv in range(2):
                    pb = self.pp.get()
                    for half in range(2):
                        for k in range(8):
                            S.mm(pb[half * 64:(half + 1) * 64, :], win.s(k)[:, k, 2064 + kv * 64:2064 + (kv + 1) * 64], hT[:, k, :], start=(k == 0), stop=(k == 7))
                    self.rope_stage(rctx, pb[:], skt[:, kv, :], g)
                self.rope_flush(rctx)
                for j in range(4):
                    pb = self.pp.get()
                    for k in range(8):
                        S.mm(pb[:], hT[:, k, j * 128:(j + 1) * 128], win.s(k)[:, k, 1024:1536], start=(k == 0), stop=(k == 7))
                    gs_ = gsl[j % 2]
                    gw_ = gwr[(g * 4 + j) % 5]
                    S.actf(gs_[:], pb[:], AF.Silu)
                    S.tt(V(gw_.h[:].rearrange("p (h d) -> p h d", h=4), gw_.bufs), V(gs_.h[:].rearrange("p (h d) -> p h d", h=4), gs_.bufs),
                         V(normw.h[:].unsqueeze(1).to_broadcast([128, 4, 128]), normw.bufs), ALU.mult, eng=S.pool)

            def stage_P(t):
                g, j = divmod(t, 4)
                tsl = slice(j * 128, (j + 1) * 128)

                def tm(col0, n, dst_fn):
                    pb = self.pp.get()
                    for k in range(8):
                        S.mm(pb[:, 0:n], hT[:, k, tsl], win.s(k)[:, k, col0:col0 + n], start=(k == 0), stop=(k == 7))
                    dst_fn(pb)
                ktm = k_tm[t % 2]
                vtm = v_tm[t % 2]
                gw_ = gwr[(g * 4 + j) % 5]
                vx = vext[t % 3]
                pz = self.pp.get()
                S.mm(pz[:, 0:256], glrT[:, tsl], gwx[:])
                ez_ = ez[t % 2]
                S.actf(ez_[:], pz[:, 0:256], AF.Exp, scale=-1.0)
                S.actf(ez_[:], ez_[:], AF.Ln, bias=1.0)
                gtm = g_tm[t % 2]
                S.actf(gtm[:], ez_[:], AF.Identity, scale=-1.0 / 16.0)
                tm(256, 256, lambda pb: S.copy(ktm[:], pb[:, 0:256], eng=S.act))
                tm(512, 512, lambda pb: S.copy(vtm[:], pb[:], eng=S.act))

                tm(2192, 128, lambda pb: S.copy(vx[:, :, 0:64], V(pb.h[:, 0:128].rearrange("p (g d) -> p g d", g=2), pb.bufs), eng=S.act))
                tst[t] = {"tsl": tsl, "ktm": ktm, "vtm": vtm, "gw": gw_, "gtm": gtm, "vx": vx, "oc": ocat[t % 2]}

            def stage_G1(t):
                d = tst[t]
                tsl = d["tsl"]
                d["gl"] = self.gl_front(glc, lambda ch: qT[:, ch, tsl], lambda ch: kT[:, ch, tsl], d["gtm"][:], d["ktm"][:], 1.0)

            def stage_G2a(t):
                self.gl_scores(glc, tst[t]["gl"])

            def stage_G2b(t):
                d = tst[t]
                vtm = d["vtm"]
                self.gl_back(glc, d["gl"], lambda h: vtm[:, h * 128:(h + 1) * 128], d["gw"][:], d["oc"][:, 0:512])

            def stage_Wa(t):
                d = tst[t]
                g, j = divmod(t, 4)
                tsl = d["tsl"]
                skt = skT[g % 2]
                sq_ = sqT[g % 2]
                if j > 0:
                    prev_k = (skt, slice((j - 1) * 128, j * 128))
                elif g > 0:
                    prev_k = (skT[(g - 1) % 2], slice(384, 512))
                else:
                    prev_k = None
                vprev = vext[(t - 1) % 3]
                d["pts"] = []
                pti = 0
                for kv in range(2):
                    blocks = []
                    if prev_k is not None:
                        blocks.append((prev_k[0], prev_k[1], self.mb_prev, vprev))
                    blocks.append((skt, tsl, self.mb_cur, d["vx"]))
                    pts = []
                    for (kt_, ks_, msk, vv) in blocks:
                        pss = self.pp.get()
                        S.mm(V(pss.h[:].rearrange("p (h i) -> p h i", h=4), pss.bufs), self.ident[:],
                             V(msk.h[:].unsqueeze(1).to_broadcast([128, 4, 128]), msk.bufs), start=True, stop=False)
                        for r in range(4):
                            h = kv * 4 + r
                            S.mm(pss[:, r * 128:(r + 1) * 128], kt_[:, kv, ks_], sq_[:, h, tsl], start=False, stop=(r == 3))
                        pt = PT[pti % 4]
                        pti += 1
                        S.actf(pt[:], V(pss.h[:].rearrange("p (h i) -> p h i", h=4), pss.bufs), AF.Exp, scale=0.125)
                        pts.append((pt, vv))
                    d["pts"].append(pts)

            def stage_Wb(t):
                d = tst[t]
                oc = d["oc"]
                for kv in range(2):
                    pts = d["pts"][kv]
                    po = self.pp.get()
                    for r in range(4):
                        for bi, (pt, vv) in enumerate(pts):
                            S.mm(po[:, r * 65:(r + 1) * 65], pt[:, r, :], vv[:, kv, :], start=(bi == 0), stop=(bi == len(pts) - 1))
                    dn = den[t % 2]
                    pov = po.h[:, 0:260].rearrange("p (h d) -> p h d", h=4)
                    S.tt(dn[:, kv * 4:(kv + 1) * 4], V(pov[:, :, 64], po.bufs), esink[:, kv * 4:(kv + 1) * 4], ALU.add)
                    S.recip(dn[:, kv * 4:(kv + 1) * 4], dn[:, kv * 4:(kv + 1) * 4])
                    S.tt(V(oc.h[:, 512 + kv * 256:512 + (kv + 1) * 256].rearrange("p (h d) -> p h d", h=4), oc.bufs),
                         V(pov[:, :, 0:64], po.bufs),
                         V(dn.h[:, kv * 4:(kv + 1) * 4].unsqueeze(2).to_broadcast([128, 4, 64]), dn.bufs), ALU.mult)

            def stage_Oa(t):
                self.out_stage_a(octx, t, tst[t]["oc"])

            def stage_Ob(t):
                self.out_stage_b(ctx, octx, _WSub(wout), x_src, x_subs, t, self.xres)
                del tst[t]

            if self.dbg.get("verbose"):
                print("sbuf remaining", self.nc.sbuf_bytes_remaining, flush=True)
            group_stage(0)
            stage_P(0)
            stage_G1(0)
            for t in range(ntiles):
                if t + 2 < ntiles and (t + 2) % 4 == 0:
                    group_front((t + 2) // 4)
                if t + 1 < ntiles:
                    if (t + 1) % 4 == 0:
                        group_stage((t + 1) // 4)
                    stage_P(t + 1)
                stage_G2a(t)
                stage_Wa(t)
                if t + 1 < ntiles:
                    stage_G1(t + 1)
                if t >= 1:
                    stage_Oa(t - 1)
                stage_G2b(t)
                stage_Wb(t)
                if t >= 1:
                    stage_Ob(t - 1)
            stage_Oa(ntiles - 1)
            stage_Ob(ntiles - 1)
        self.S.barrier()

    def odd_pass_a(self, l, x_src, x_subs):
        S = self.S
        jl = l // 2
        lam_init = 0.8 - 0.6 * math.exp(-0.3 * l)
        self.S.barrier()
        self.pp.set_avail(range(6, 8))
        accsets = [[self.pp.banks[0], self.pp.banks[1]], [self.pp.banks[2], self.pp.banks[3]]]
        stb = [self.pp.banks[4], self.pp.banks[5]]
        with ExitStack() as st:
            win = S.sb([128, 8, 1536], BF16, nsub=8, stack=st)
            self.load_w(win, self.din["od_w_in"], lambda k: self.din["od_w_in"].h[jl, k * 128:(k + 1) * 128, 0:1536], 8)
            normw = S.sb([128, 128], F32, stack=st)
            S.dma(S.sp, normw[:], V(self.din["diff_norm_w"].h[jl:jl + 1, :].partition_broadcast(128), self.din["diff_norm_w"].bufs))
            S.ts(normw[:], normw[:], 1.0 - lam_init, None, ALU.mult)
            lv = S.sb([128, 256], F32, stack=st)
            S.dma(S.sp, lv[:], V(self.din["diff_lambda"].h[jl:jl + 1, :].partition_broadcast(128), self.din["diff_lambda"].bufs))
            lp = S.sb([128, 128], F32, stack=st)
            lv4 = lv.h[:].rearrange("p (a b d) -> p a b d", a=2, b=2)
            S.tt(V(lp.h[:].rearrange("p (a d) -> p a d", a=2), lp.bufs), V(lv4[:, :, 0, :], lv.bufs), V(lv4[:, :, 1, :], lv.bufs), ALU.mult)
            lsum = S.sb([128, 4], F32, stack=st)
            S.reduce_sum(lsum[:, 0:2], V(lp.h[:].rearrange("p (a d) -> p a d", a=2), lp.bufs))
            S.actf(lsum[:, 2:4], lsum[:, 0:2], AF.Exp)
            nlam = S.sb([128, 1], F32, stack=st)
            S.tt(nlam[:], lsum[:, 3:4], lsum[:, 2:3], ALU.subtract)
            S.ts(nlam[:], nlam[:], -lam_init, None, ALU.add)
            ctx = self.norm_ctx(st)
            rctx = self.rope_ctx(st)
            hT = S.sb([128, 8, 512], BF16, stack=st)
            KT = S.sb([128, 4, SEQ], BF16, nsub=NG, stack=st)
            VX = S.sb([128, NT, 4, 129], BF16, nsub=NG, stack=st)
            for g in range(NG):
                S.memset(VX.s(g)[:, g * 4:(g + 1) * 4, :, :], 1.0, eng=S.pool)
            QT = [S.sb([128, 4, 2, 512], BF16, stack=st) for _ in range(2)]
            for q_ in QT:
                S.memset(q_[:], 0.0)
            PT = [S.sb([128, 512], BF16, stack=st) for _ in range(3)]
            o1 = S.sb([128, 4, 128], F32, stack=st)
            od = [S.sb([128, 4, 128], F32, stack=st) for _ in range(4)]
            rl = [S.sb([128, 4], F32, stack=st) for _ in range(2)]
            sq = S.sb([128, 512], F32, stack=st)
            ssum = [S.sb([128, 8], F32, stack=st) for _ in range(2)]
            on = S.sb([128, 512], F32, stack=st)
            ob = [S.sb([128, 512], BF16, stack=st) for _ in range(2)]
            pti = 0
            ngroups = self.dbg.get("ngroups", NG)
            fronted = set()

            def group_front(g):
                fronted.add(g)
                self.rope_load(rctx, g)
                self.norm_front(ctx, x_src, x_subs, g)

            def group_stage(g):
                if g not in fronted:
                    group_front(g)
                self.norm_back(ctx, l, 0, hT)
                qt = QT[g % 2]
                gsl_ = slice(g * 512, (g + 1) * 512)
                for ch in range(4):
                    pb = self.pp.get()
                    for k in range(8):
                        S.mm(pb[:], win.s(k)[:, k, ch * 128:(ch + 1) * 128], hT[:, k, :], start=(k == 0), stop=(k == 7))
                    self.rope_stage(rctx, pb[:], [(slice(0, 64), qt[0:64, ch, 0, :]), (slice(64, 128), qt[64:128, ch, 1, :])], g)
                    pb = self.pp.get()
                    for k in range(8):
                        S.mm(pb[:], win.s(k)[:, k, 512 + ch * 128:512 + (ch + 1) * 128], hT[:, k, :], start=(k == 0), stop=(k == 7))
                    self.rope_stage(rctx, pb[:], KT.s(g)[:, ch, gsl_], g)
                self.rope_flush(rctx)
                for j in range(4):
                    t = g * 4 + j
                    pb = self.pp.get()
                    for k in range(8):
                        S.mm(pb[:], hT[:, k, j * 128:(j + 1) * 128], win.s(k)[:, k, 1024:1536], start=(k == 0), stop=(k == 7))
                    S.copy(VX.s(g)[:, t, :, 0:128], V(pb.h[:].rearrange("p (h d) -> p h d", h=4), pb.bufs), eng=S.act)

            group_stage(0)
            for g in range(ngroups):
                qt = QT[g % 2]
                nkb = 4 * g + 4
                its = [(h, m, kb) for h in range(4) for m in range(2) for kb in range(nkb)]

                def emit_st(it, idx):
                    h, m, kb = it
                    q0 = max(0, kb - 4 * g)
                    cols = slice(q0 * 128, 512)
                    pst = stb[idx % 2]
                    S.mm(pst[:, cols], KT.s(kb // 4)[:, h, kb * 128:(kb + 1) * 128], qt[:, h, m, cols])
                    return pst

                pst_next = emit_st(its[0], pti)
                for ii, (h, m, kb) in enumerate(its):
                    if ii == nkb and g + 1 < ngroups:
                        group_front(g + 1)
                    if ii == 2 * nkb and g + 1 < ngroups:
                        group_stage(g + 1)
                    pst = pst_next
                    pt = PT[pti % 3]
                    if ii + 1 < len(its):
                        pst_next = emit_st(its[ii + 1], pti + 1)
                    pti += 1
                    q0 = max(0, kb - 4 * g)
                    cols = slice(q0 * 128, 512)
                    kg = kb // 4
                    accs = accsets[(h * 2 + m) % 2]
                    S.actf(pt[:, cols], pst[:, cols], AF.Exp, scale=0.125)
                    if kb >= 4 * g:
                        dsl = slice(q0 * 128, (q0 + 1) * 128)
                        S.tt(pt[:, dsl], pt[:, dsl], self.mask_cur[:], ALU.mult, eng=S.pool)
                    for qb in range(q0, 4):
                        acc = accs[qb // 2]
                        o_ = (qb % 2) * 129
                        S.mm(acc[:, o_:o_ + 129], pt[:, qb * 128:(qb + 1) * 128], VX.s(kg)[:, kb, h, :],
                             start=(kb == 0 and qb % 2 == 0), stop=(kb == 4 * g + qb and qb % 2 == 1))
                    if kb == nkb - 1:
                        r_ = rl[m]
                        for qb in range(4):
                            acc = accs[qb // 2]
                            o_ = (qb % 2) * 129
                            S.recip(r_[:, qb:qb + 1], acc[:, o_ + 128:o_ + 129])
                            if m == 0:
                                S.ts(o1[:, qb, :], acc[:, o_:o_ + 128], r_[:, qb:qb + 1], None, ALU.mult)
                            else:
                                S.ts(r_[:, qb:qb + 1], r_[:, qb:qb + 1], nlam[:, 0:1], None, ALU.mult)
                                S.stt(od[qb][:, h, :], acc[:, o_:o_ + 128], r_[:, qb:qb + 1], o1[:, qb, :], ALU.mult, ALU.add)
                for qb in range(4):
                    t = g * 4 + qb
                    odv = V(od[qb].h[:].rearrange("p h d -> p (h d)"), od[qb].bufs)
                    S.actf(sq[:], odv, AF.Square)
                    ss_ = ssum[t % 2]
                    S.reduce_sum(ss_[:, 0:4], V(sq.h[:].rearrange("p (h d) -> p h d", h=4), sq.bufs))
                    S.rstd(ss_[:, 0:4], ss_[:, 4:8], ss_[:, 0:4], 1.0 / 128)
                    S.tt(V(on.h[:].rearrange("p (h d) -> p h d", h=4), on.bufs), od[qb][:],
                         V(ss_.h[:, 0:4].unsqueeze(2).to_broadcast([128, 4, 128]), ss_.bufs), ALU.mult)
                    ob_ = ob[t % 2]
                    S.tt(V(ob_.h[:].rearrange("p (h d) -> p h d", h=4), ob_.bufs), V(on.h[:].rearrange("p (h d) -> p h d", h=4), on.bufs),
                         V(normw.h[:].unsqueeze(1).to_broadcast([128, 4, 128]), normw.bufs), ALU.mult, eng=S.pool)
                    S.dma(S.pool, self.odiff.s(t)[t * 128:(t + 1) * 128, :], ob_[:])
        self.S.barrier()
        self.pp.set_avail(range(8))

    def odd_pass_b(self, l, x_src, x_subs):
        S = self.S
        jl = l // 2
        self.S.barrier()
        self.pp.set_avail(range(8))
        with ExitStack() as st:
            win = S.sb([128, 8, 2048], BF16, nsub=8, stack=st)
            wout = S.sb([128, 8, D], BF16, nsub=8, stack=st)
            self.load_w(win, self.din["od_w_in"], lambda k: self.din["od_w_in"].h[jl, k * 128:(k + 1) * 128, 1536:3584], 8)
            self.load_w(wout, self.din["od_w_out"], lambda k: self.din["od_w_out"].h[jl, k * 128:(k + 1) * 128, :], 8)
            normw = S.sb([128, 128], F32, stack=st)
            S.dma(S.sp, normw[:], V(self.din["hgrn_norm_w"].h[jl:jl + 1, :].partition_broadcast(128), self.din["hgrn_norm_w"].bufs))
            lbr = S.sb([128, 512], F32, stack=st)
            omlr = S.sb([128, 512], F32, stack=st)
            if jl == 0:
                S.memset(lbr[:], 0.0)
            else:
                l0 = S.sb([128, 512], F32, stack=st)
                S.dma(S.sp, l0[:], V(self.din["lb_r"].h[0:1, :].partition_broadcast(128), self.din["lb_r"].bufs))
                S.dma(S.sp, lbr[:], V(self.din["lb_r"].h[1:2, :].partition_broadcast(128), self.din["lb_r"].bufs))
                S.tt(lbr[:], lbr[:], l0[:], ALU.subtract)
                S.actf(lbr[:], lbr[:], AF.Sigmoid)
            S.ts(omlr[:], lbr[:], -1.0, 1.0, ALU.mult, ALU.add)
            ctx = self.norm_ctx(st)
            octx = self.out_ctx(st, l, 0)
            glc = self.gl_ctx(st, 4, 128)
            hT = S.sb([128, 8, 512], BF16, stack=st)
            qT = S.sb([128, 4, 512], F32, stack=st)
            kT = S.sb([128, 4, 512], F32, stack=st)
            sgT = [S.sb([128, 512], F32, stack=st) for _ in range(2)]
            k_tm = [S.sb([128, 512], F32, stack=st) for _ in range(2)]
            f_tm = [S.sb([128, 512], F32, stack=st) for _ in range(2)]
            v_tm = [S.sb([128, 512], BF16, stack=st) for _ in range(2)]
            gwr = [S.sb([128, 512], F32, stack=st) for _ in range(5)]
            b_tm = [S.sb([128, 512], F32, stack=st) for _ in range(2)]
            g_tm = [S.sb([128, 512], F32, stack=st) for _ in range(2)]
            gsl = [S.sb([128, 512], F32, stack=st) for _ in range(2)]
            ocat = [S.sb([128, D], BF16, stack=st) for _ in range(3)]
            ngroups = self.dbg.get("ngroups", NG)
            ntiles = ngroups * 4
            tst = {}

            fronted = set()

            def group_front(g):
                fronted.add(g)
                self.norm_front(ctx, x_src, x_subs, g)

            def group_stage(g):
                if g not in fronted:
                    group_front(g)
                self.norm_back(ctx, l, 0, hT)
                for ch in range(4):
                    pb = self.pp.get()
                    for k in range(8):
                        S.mm(pb[:], win.s(k)[:, k, ch * 128:(ch + 1) * 128], hT[:, k, :], start=(k == 0), stop=(k == 7))
                    S.actf(qT[:, ch, :], pb[:], AF.Silu)
                    pb = self.pp.get()
                    for k in range(8):
                        S.mm(pb[:], win.s(k)[:, k, 512 + ch * 128:512 + (ch + 1) * 128], hT[:, k, :], start=(k == 0), stop=(k == 7))
                    sg = sgT[ch % 2]
                    S.actf(sg[:], pb[:], AF.Sigmoid)
                    S.ts(sg[:], sg[:], -1.0, 1.0, ALU.mult, ALU.add)
                    S.actf(kT[:, ch, :], sg[:], AF.Identity, scale=self.omlc[:, jl, ch:ch + 1])
                for j in range(4):
                    pb = self.pp.get()
                    for k in range(8):
                        S.mm(pb[:], hT[:, k, j * 128:(j + 1) * 128], win.s(k)[:, k, 1536:2048], start=(k == 0), stop=(k == 7))
                    gs_ = gsl[j % 2]
                    gw_ = gwr[(g * 4 + j) % 5]
                    S.actf(gs_[:], pb[:], AF.Silu)
                    S.tt(V(gw_.h[:].rearrange("p (h d) -> p h d", h=4), gw_.bufs), V(gs_.h[:].rearrange("p (h d) -> p h d", h=4), gs_.bufs),
                         V(normw.h[:].unsqueeze(1).to_broadcast([128, 4, 128]), normw.bufs), ALU.mult, eng=S.pool)

            def stage_P(t):
                g, j = divmod(t, 4)
                tsl = slice(j * 128, (j + 1) * 128)

                def tm(col0, n, dst_fn):
                    pb = self.pp.get()
                    for k in range(8):
                        S.mm(pb[:, 0:n], hT[:, k, tsl], win.s(k)[:, k, col0:col0 + n], start=(k == 0), stop=(k == 7))
                    dst_fn(pb)
                ktm = k_tm[t % 2]
                ftm = f_tm[t % 2]
                gtm = g_tm[t % 2]
                vtm = v_tm[t % 2]
                gw_ = gwr[(g * 4 + j) % 5]
                btm = b_tm[t % 2]

                def fgate(pb):
                    S.actf(ftm[:], pb[:], AF.Exp, scale=-1.0)
                    S.actf(btm[:], ftm[:], AF.Ln, bias=1.0)
                    if jl == 0:
                        S.actf(gtm[:], btm[:], AF.Identity, scale=-1.0)
                        S.actf(ktm[:], btm[:], AF.Exp, scale=-1.0)
                        S.ts(ktm[:], ktm[:], -1.0, 1.0, ALU.mult, ALU.add, eng=S.pool)
                    else:
                        S.tt(gtm[:], ftm[:], lbr[:], ALU.mult)
                        S.actf(ktm[:], btm[:], AF.Exp, scale=-1.0)
                        S.actf(gtm[:], gtm[:], AF.Ln, bias=1.0)
                        S.tt(ktm[:], ktm[:], ftm[:], ALU.mult, eng=S.pool)
                        S.tt(gtm[:], gtm[:], btm[:], ALU.subtract)
                        S.tt(ktm[:], ktm[:], omlr[:], ALU.mult, eng=S.pool)
                tm(512, 512, fgate)
                tm(1024, 512, lambda pb: S.copy(vtm[:], pb[:], eng=S.act))

                oc = ocat[t % 3]
                S.dma(S.sp, oc[:, 0:512], self.odiff.s(t)[t * 128:(t + 1) * 128, :])
                tst[t] = {"tsl": tsl, "ktm": ktm, "vtm": vtm, "gw": gw_, "gtm": gtm, "oc": oc}

            def stage_G1(t):
                d = tst[t]
                tsl = d["tsl"]
                d["gl"] = self.gl_front(glc, lambda ch: qT[:, ch, tsl], lambda ch: kT[:, ch, tsl], d["gtm"][:], d["ktm"][:], 128.0 ** -0.5)

            def stage_G2a(t):
                self.gl_scores(glc, tst[t]["gl"])

            def stage_G2b(t):
                d = tst[t]
                vtm = d["vtm"]
                self.gl_back(glc, d["gl"], lambda h: vtm[:, h * 128:(h + 1) * 128], d["gw"][:], d["oc"][:, 512:1024])

            def stage_Oa(t):
                self.out_stage_a(octx, t, tst[t]["oc"])

            def stage_Ob(t):
                self.out_stage_b(ctx, octx, _WSub(wout), x_src, x_subs, t, self.xres)
                del tst[t]

            if self.dbg.get("verbose"):
                print("sbuf remaining", self.nc.sbuf_bytes_remaining, flush=True)
            group_stage(0)
            stage_P(0)
            stage_G1(0)
            for t in range(ntiles):
                if t + 2 < ntiles and (t + 2) % 4 == 0:
                    group_front((t + 2) // 4)
                if t + 1 < ntiles:
                    if (t + 1) % 4 == 0:
                        group_stage((t + 1) // 4)
                    stage_P(t + 1)
                stage_G2a(t)
                if t + 1 < ntiles:
                    stage_G1(t + 1)
                if t >= 1:
                    stage_Oa(t - 1)
                stage_G2b(t)
                if t >= 1:
                    stage_Ob(t - 1)
            stage_Oa(ntiles - 1)
            stage_Ob(ntiles - 1)
        self.S.barrier()


class _WSub:
    def __init__(self, t):
        self.t = t

    def __getitem__(self, idx):
        k = idx[1]
        return V(self.t.h[idx], (self.t.bufs[k],))


def g_tm_slice(g_tm, ch):
    return V(g_tm.ap[:, ch * 128:(ch + 1) * 128], g_tm.bufs)


def qd_ap(q_, I):
    return q_.ap[:, 32 * I:32 * I + 32]


_CACHE = {}


def _col(v, n):
    return np.ascontiguousarray(np.asarray(v).reshape(n, 128).T)


def make_in_maps(inputs, ncores=8):
    f = lambda a: np.ascontiguousarray(np.asarray(a, dtype=np.float32))
    x = f(inputs["x"])
    c = f(inputs["c"])
    pos = np.ascontiguousarray(np.asarray(inputs["positions"], dtype=np.int32))
    mod_b = f(inputs["mod_b"])
    gate_w = f(inputs["gla_gate_w"])
    gate_b = f(inputs["gla_gate_b"])
    gatew_ext = np.zeros((2, 32, 256), np.float32)
    gatew_ext[:, 0:16, :] = gate_w
    gatew_ext[:, 16, :] = gate_b
    lb = f(inputs["hgrn_lb_logits"])
    shared = {
        "consts": CONSTS,
        "mod_w": f(inputs["mod_w"]),
        "mod_bc": np.stack([_col(mod_b[l], 48) for l in range(DEPTH)]),
        "mod_b": mod_b,
        "nmw_c": np.stack([_col(f(inputs["norm_mix_w"])[l], 8) for l in range(DEPTH)]),
        "nfw_c": np.stack([_col(f(inputs["norm_ffn_w"])[l], 8) for l in range(DEPTH)]),
        "ev_w_in": f(inputs["ev_w_in"]),
        "gatew_ext": gatew_ext,
        "gla_norm_w": f(inputs["gla_norm_w"]),
        "swa_sinks": f(inputs["swa_sinks"]),
        "ev_w_out": f(inputs["ev_w_out"]),
        "od_w_in": f(inputs["od_w_in"]),
        "diff_lambda": f(inputs["diff_lambda"]).reshape(2, 256),
        "diff_norm_w": f(inputs["diff_norm_w"]),
        "lb_c": np.ascontiguousarray(np.stack([_col(lb[j], 4) for j in range(2)], axis=1)),
        "lb_r": lb,
        "hgrn_norm_w": f(inputs["hgrn_norm_w"]),
        "od_w_out": f(inputs["od_w_out"]),
        "ffn_w_in": f(inputs["ffn_w_in"]),
        "convw_c": np.ascontiguousarray(np.stack([np.stack([_col(f(inputs["ffn_conv_w"])[l, j], NCHUNK_FF) for j in range(3)], axis=1) for l in range(DEPTH)])),
        "convb_c": np.stack([_col(f(inputs["ffn_conv_b"])[l], NCHUNK_FF) for l in range(DEPTH)]),
        "ffn_w_out": f(inputs["ffn_w_out"]),
        "final_norm_w": f(inputs["final_norm_w"]).reshape(1, D),
    }
    maps = []
    for b in range(ncores):
        m = dict(shared)
        m["x"] = x[b]
        m["c_t"] = _col(c[b], 8)
        m["pos"] = pos[b].reshape(1, SEQ)
        maps.append(m)
    return maps


def kernel(**inputs):
    if "prog" not in _CACHE:
        _CACHE["prog"] = Prog()
    prog = _CACHE["prog"]
    maps = make_in_maps(inputs, 8)
    res = run_bass_kernel_spmd(prog.nc, maps, core_ids=list(range(8)))
    return np.stack([np.asarray(r["out"], dtype=np.float32) for r in res.results], axis=0)
```
